# Optimizing a Trainium2 kernel written in Bass

```python
import math
import jax, jax.numpy as jnp
from jax import lax
import numpy as np

D_MODEL = 1024
BATCH = 2
SEQ = 8192
DEPTH = 2

HEAD_DIM = 64
MLA_HEADS = 6
MLA_NOPE = 64
MLA_ROPE = 32
MLA_V = 64
MLA_QK = MLA_NOPE + MLA_ROPE
Q_LORA = 256
KV_LORA = 128
SWA_HEADS = 6
SWA_KV_HEADS = 2
SWA_GROUP = SWA_HEADS // SWA_KV_HEADS
WINDOW = 128
SB_HEADS = 4
BLOCK = 128
NUM_BUCKETS = 32
MAX_DISTANCE = 128
D_FF = 2816
ROPE_THETA = 10000.0
EPS = 1e-6

MLA_WIDTH = MLA_HEADS * MLA_V
SWA_WIDTH = SWA_HEADS * HEAD_DIM
SB_WIDTH = SB_HEADS * HEAD_DIM
D_MIX = MLA_WIDTH + SWA_WIDTH + SB_WIDTH
IN_SIZES = (Q_LORA, KV_LORA, MLA_ROPE,
            SWA_HEADS * HEAD_DIM, SWA_KV_HEADS * HEAD_DIM, SWA_KV_HEADS * HEAD_DIM,
            3 * SB_HEADS * HEAD_DIM)
D_IN = Q_LORA + KV_LORA + MLA_ROPE + (SWA_HEADS + 2 * SWA_KV_HEADS) * HEAD_DIM + 3 * SB_HEADS * HEAD_DIM

kernel_name = "hybrid_mla_swa_stickbreak_macaron"


def _split_cols(x, sizes):
    offs = np.cumsum(np.array(sizes))[:-1].tolist()
    return jnp.split(x, offs, axis=-1)


def rms_norm(x, g):
    xf = x.astype(jnp.float32)
    y = xf * lax.rsqrt(jnp.mean(xf * xf, axis=-1, keepdims=True) + EPS)
    return (y * g.astype(jnp.float32)).astype(x.dtype)


def modulate(x, g, shift, scale):
    return rms_norm(x, g) * (1.0 + scale[:, None, :]) + shift[:, None, :]


def rope(x, pos):
    half = x.shape[-1] // 2
    freqs = ROPE_THETA ** (-jnp.arange(half, dtype=jnp.float32) / half)
    ang = pos.astype(jnp.float32)[:, :, None, None] * freqs
    cos, sin = jnp.cos(ang), jnp.sin(ang)
    xf = x.astype(jnp.float32)
    x1, x2 = xf[..., :half], xf[..., half:]
    return jnp.concatenate([x1 * cos - x2 * sin, x2 * cos + x1 * sin], axis=-1).astype(x.dtype)


def t5_bucket(rel):
    n = jnp.maximum(rel, 0)
    max_exact = NUM_BUCKETS // 2
    nf = jnp.maximum(n, 1).astype(jnp.float32)
    large = max_exact + (jnp.log(nf / max_exact) / math.log(MAX_DISTANCE / max_exact)
                         * (NUM_BUCKETS - max_exact)).astype(jnp.int32)
    large = jnp.minimum(large, NUM_BUCKETS - 1)
    return jnp.where(n < max_exact, n, large)


def swiglu(h, w_gu, w_down):
    g, u = jnp.split(h @ w_gu, 2, axis=-1)
    return (jax.nn.silu(g) * u) @ w_down


def mla_attention(q, k, v):
    B, S, H, Dq = q.shape
    nb = S // BLOCK
    qb = q.reshape(B, nb, BLOCK, H, Dq).transpose(1, 0, 2, 3, 4)
    key_idx = jnp.arange(S)
    scale = Dq ** -0.5

    def one_block(args):
        i, qi = args
        s = jnp.einsum('bqhd,bkhd->bhqk', qi, k).astype(jnp.float32) * scale
        q_idx = i * BLOCK + jnp.arange(BLOCK)
        mask = key_idx[None, :] <= q_idx[:, None]
        p = jax.nn.softmax(jnp.where(mask, s, -jnp.inf), axis=-1)
        return jnp.einsum('bhqk,bkhd->bqhd', p.astype(v.dtype), v)

    out = lax.map(one_block, (jnp.arange(nb), qb))
    return out.transpose(1, 0, 2, 3, 4).reshape(B, S, H * v.shape[-1])


def swa_attention(q, k, v, pos, sinks, rel_table):
    B, S, Hq, D = q.shape
    nb = S // WINDOW

    def band(x):
        xb = x.reshape((B, nb, WINDOW) + x.shape[2:])
        prev = jnp.concatenate([jnp.zeros_like(xb[:, :1]), xb[:, :-1]], axis=1)
        return jnp.concatenate([prev, xb], axis=2)

    kb, vb = band(k), band(v)
    qb = q.reshape(B, nb, WINDOW, SWA_KV_HEADS, SWA_GROUP, D)
    s = jnp.einsum('bnqkgd,bnskd->bnkgqs', qb, kb).astype(jnp.float32) * (D ** -0.5)
    rel = pos.reshape(B, nb, WINDOW)[..., :, None] - band(pos)[..., None, :]
    bias = rel_table[t5_bucket(rel)].astype(jnp.float32)
    bias = bias.reshape(B, nb, WINDOW, 2 * WINDOW, SWA_KV_HEADS, SWA_GROUP).transpose(0, 1, 4, 5, 2, 3)
    q_in = jnp.arange(WINDOW)[:, None] + WINDOW
    k_in = jnp.arange(2 * WINDOW)[None, :]
    d = q_in - k_in
    in_win = (d >= 0) & (d < WINDOW)
    not_pad = (jnp.arange(nb)[:, None, None] > 0) | (k_in[None] >= WINDOW)
    mask = (in_win[None] & not_pad)[None, :, None, None]
    s = jnp.where(mask, s + bias, -jnp.inf)
    sink = sinks.astype(jnp.float32).reshape(SWA_KV_HEADS, SWA_GROUP)[None, None, :, :, None, None]
    m = jnp.maximum(jnp.max(s, axis=-1, keepdims=True), sink)
    p = jnp.exp(s - m)
    p = p / (jnp.sum(p, axis=-1, keepdims=True) + jnp.exp(sink - m))
    o = jnp.einsum('bnkgqs,bnskd->bnqkgd', p.astype(v.dtype), vb)
    return o.reshape(B, S, Hq * D)


def stick_breaking_attention(q, k, v):
    B, S, H, D = q.shape
    nb = S // BLOCK
    qb = q.reshape(B, nb, BLOCK, H, D).transpose(1, 0, 2, 3, 4)
    key_idx = jnp.arange(S)
    scale = D ** -0.5

    def one_block(args):
        i, qi = args
        z = jnp.einsum('bqhd,bkhd->bhqk', qi, k).astype(jnp.float32) * scale
        q_idx = i * BLOCK + jnp.arange(BLOCK)
        mask = key_idx[None, :] < q_idx[:, None]
        log_not = jnp.where(mask, jax.nn.log_sigmoid(-z), 0.0)
        after = lax.cumsum(log_not, axis=3, reverse=True) - log_not
        a = jnp.where(mask, jnp.exp(jax.nn.log_sigmoid(z) + after), 0.0)
        return jnp.einsum('bhqk,bkhd->bqhd', a.astype(v.dtype), v)

    out = lax.map(one_block, (jnp.arange(nb), qb))
    return out.transpose(1, 0, 2, 3, 4).reshape(B, S, H * D)


def setup_inputs(seed: int = 0) -> dict:
    key = jax.random.key(seed)
    ks = jax.random.split(key, 24)
    f32 = jnp.float32

    def nrm(k, shape, scale):
        return jax.random.normal(k, shape, f32) * scale

    def gain(k, shape):
        return 1.0 + 0.1 * jax.random.normal(k, shape, f32)

    x = jax.random.normal(ks[0], (BATCH, SEQ, D_MODEL), f32)
    c = jax.random.normal(ks[1], (BATCH, D_MODEL), f32)
    offset = jax.random.randint(ks[2], (BATCH, 1), 0, 4096, dtype=jnp.int32)
    positions = offset + jnp.arange(SEQ, dtype=jnp.int32)[None, :]
    return {
        "x": x,
        "c": c,
        "positions": positions,
        "rel_bias": nrm(ks[3], (NUM_BUCKETS, SWA_HEADS), 0.5),
        "norm_g": gain(ks[4], (DEPTH, 3, D_MODEL)),
        "w_mod": nrm(ks[5], (DEPTH, D_MODEL, 9 * D_MODEL), 0.5 * D_MODEL ** -0.5),
        "b_mod": nrm(ks[6], (DEPTH, 9 * D_MODEL), 0.02),
        "w_ffn1_gu": nrm(ks[7], (DEPTH, D_MODEL, 2 * D_FF), D_MODEL ** -0.5),
        "w_ffn1_down": nrm(ks[8], (DEPTH, D_FF, D_MODEL), D_FF ** -0.5),
        "w_in": nrm(ks[9], (DEPTH, D_MODEL, D_IN), D_MODEL ** -0.5),
        "q_a_norm": gain(ks[10], (DEPTH, Q_LORA)),
        "kv_a_norm": gain(ks[11], (DEPTH, KV_LORA)),
        "w_uq": nrm(ks[12], (DEPTH, Q_LORA, MLA_HEADS * MLA_QK), Q_LORA ** -0.5),
        "w_ukv": nrm(ks[13], (DEPTH, KV_LORA, MLA_HEADS * (MLA_NOPE + MLA_V)), KV_LORA ** -0.5),
        "mla_q_norm": gain(ks[14], (DEPTH, MLA_QK)),
        "mla_k_norm": gain(ks[15], (DEPTH, MLA_QK)),
        "swa_q_norm": gain(ks[16], (DEPTH, HEAD_DIM)),
        "swa_k_norm": gain(ks[17], (DEPTH, HEAD_DIM)),
        "sinks": nrm(ks[18], (DEPTH, SWA_HEADS), 0.5),
        "out_norm": gain(ks[19], (DEPTH, D_MIX)),
        "w_out": nrm(ks[20], (DEPTH, D_MIX, D_MODEL), D_MIX ** -0.5),
        "w_ffn2_gu": nrm(ks[21], (DEPTH, D_MODEL, 2 * D_FF), D_MODEL ** -0.5),
        "w_ffn2_down": nrm(ks[22], (DEPTH, D_FF, D_MODEL), D_FF ** -0.5),
    }


def reference(x, c, positions, rel_bias, norm_g, w_mod, b_mod, w_ffn1_gu, w_ffn1_down,
              w_in, q_a_norm, kv_a_norm, w_uq, w_ukv, mla_q_norm, mla_k_norm,
              swa_q_norm, swa_k_norm, sinks, out_norm, w_out, w_ffn2_gu, w_ffn2_down):
    B, S, _ = x.shape
    cond = jax.nn.silu(c)
    for l in range(DEPTH):
        mod = cond @ w_mod[l] + b_mod[l]
        sh1, sc1, g1, sh2, sc2, g2, sh3, sc3, g3 = jnp.split(mod, 9, axis=-1)

        h = modulate(x, norm_g[l, 0], sh1, sc1)
        x = x + 0.5 * g1[:, None, :] * swiglu(h, w_ffn1_gu[l], w_ffn1_down[l])

        h = modulate(x, norm_g[l, 1], sh2, sc2)
        proj = h @ w_in[l]
        c_q, c_kv, k_pe, q_s, k_s, v_s, qkv_sb = _split_cols(proj, IN_SIZES)

        qa = (rms_norm(c_q, q_a_norm[l]) @ w_uq[l]).reshape(B, S, MLA_HEADS, MLA_QK)
        qa = rms_norm(qa, mla_q_norm[l])
        qa = jnp.concatenate([qa[..., :MLA_NOPE], rope(qa[..., MLA_NOPE:], positions)], axis=-1)
        kv = (rms_norm(c_kv, kv_a_norm[l]) @ w_ukv[l]).reshape(B, S, MLA_HEADS, MLA_NOPE + MLA_V)
        k_nope, va = kv[..., :MLA_NOPE], kv[..., MLA_NOPE:]
        k_pe_h = jnp.broadcast_to(k_pe[:, :, None, :], (B, S, MLA_HEADS, MLA_ROPE))
        ka = rms_norm(jnp.concatenate([k_nope, k_pe_h], axis=-1), mla_k_norm[l])
        ka = jnp.concatenate([ka[..., :MLA_NOPE], rope(ka[..., MLA_NOPE:], positions)], axis=-1)
        o_a = mla_attention(qa, ka, va)

        qb = rms_norm(q_s.reshape(B, S, SWA_HEADS, HEAD_DIM), swa_q_norm[l])
        kb = rms_norm(k_s.reshape(B, S, SWA_KV_HEADS, HEAD_DIM), swa_k_norm[l])
        vb = v_s.reshape(B, S, SWA_KV_HEADS, HEAD_DIM)
        o_b = swa_attention(qb, kb, vb, positions, sinks[l], rel_bias)

        qc, kc, vc = jnp.split(qkv_sb.reshape(B, S, 3, SB_HEADS, HEAD_DIM), 3, axis=2)
        o_c = stick_breaking_attention(qc[:, :, 0], kc[:, :, 0], vc[:, :, 0])

        on = out_norm[l]
        mix = jnp.concatenate([
            rms_norm(o_a, on[:MLA_WIDTH]),
            rms_norm(o_b, on[MLA_WIDTH:MLA_WIDTH + SWA_WIDTH]),
            rms_norm(o_c, on[MLA_WIDTH + SWA_WIDTH:]),
        ], axis=-1)
        x = x + g2[:, None, :] * (mix @ w_out[l])

        h = modulate(x, norm_g[l, 2], sh3, sc3)
        x = x + 0.5 * g3[:, None, :] * swiglu(h, w_ffn2_gu[l], w_ffn2_down[l])
    return x
```

```python
import math
import numpy as np
import ml_dtypes
import concourse.bass as bass
import concourse.mybir as mybir
from concourse.bass_utils import run_bass_kernel_spmd

F32 = mybir.dt.float32
BF16 = mybir.dt.bfloat16
I32 = mybir.dt.int32
AF = mybir.ActivationFunctionType
ALU = mybir.AluOpType

T = 2048
NBLK = 16
D = 1024
DFF = 2816
NF = 22
EPS = 1e-6
NCORES = 8
TWO_PI = 2.0 * math.pi

ENGS = ("pe", "act", "dve", "pool", "sp")
EPOCH = 30000


class Buf:
    __slots__ = ("name", "writers", "readers", "sem", "cnt", "multi")

    def __init__(self, name="", multi=False):
        self.name = name
        self.writers = []
        self.readers = []
        self.sem = None
        self.cnt = 0
        self.multi = multi


class Op:
    __slots__ = ("eng", "fn", "deps", "signal", "k", "dma_tok", "is_dma")

    def __init__(self, eng, fn):
        self.eng = eng
        self.fn = fn
        self.deps = []
        self.signal = False
        self.k = -1
        self.dma_tok = None
        self.is_dma = False


class Sched:
    def __init__(self, nc, same_engine_raw=True):
        self.nc = nc
        self.q = {e: [] for e in ENGS}
        self.same_engine_raw = same_engine_raw
        self.nsem = 0
        self.dmas = []
        self.csem = None
        self.ccnt = 0
        self.sem_pool = []
        self.sem_bufs = []

    def new_sem(self, name):
        self.nsem += 1
        return self.nc.alloc_semaphore(f"{name}_{self.nsem}")

    def _dep(self, o, p, raw):
        if p is None or p is o:
            return
        if (not p.is_dma) and (not o.is_dma) and p.eng == o.eng:
            if not (raw and self.same_engine_raw and o.eng != "pe"):
                return
        if p not in o.deps:
            o.deps.append(p)
            if not p.is_dma:
                p.signal = True

    def _track(self, o, reads, writes):
        for b in reads:
            for w in b.writers:
                self._dep(o, w, True)
        for b in writes:
            if not (b.multi and o.is_dma):
                for w in b.writers:
                    self._dep(o, w, False)
            for r in b.readers:
                self._dep(o, r, False)
        for b in reads:
            b.readers.append(o)
        for b in writes:
            if b.multi and o.is_dma and not b.readers:
                b.writers.append(o)
            else:
                b.writers = [o]
            b.readers = []

    def op(self, eng, fn, reads=(), writes=()):
        o = Op(eng, fn)
        self._track(o, reads, writes)
        self.q[eng].append(o)
        return o

    def dma(self, eng, fn, reads, writes, semb=None):
        o = Op(eng, fn)
        o.is_dma = True
        self._track(o, reads, writes)
        semb = semb if semb is not None else writes[0]
        if semb.sem is None:
            if self.sem_pool:
                semb.sem, semb.cnt = self.sem_pool.pop()
            else:
                semb.sem = self.new_sem("d")
            self.sem_bufs.append(semb)
        semb.cnt += 16
        o.dma_tok = (semb.sem, semb.cnt)
        self.q[eng].append(o)
        self.dmas.append(o)
        return o

    def coll(self, eng, fn, reads, writes, dummy_fn):
        o = Op(eng, fn)
        o.is_dma = True
        self._track(o, reads, [])
        if self.csem is None:
            self.csem = self.new_sem("c")
        self.ccnt += 1
        o.dma_tok = (self.csem, self.ccnt)
        o.k = -2
        self.q[eng].append(o)
        return self.op(eng, dummy_fn, [], writes)

    def wait_all(self, eng, ops):
        o = Op(eng, None)
        for p in ops:
            self._dep(o, p, True)
        self.q[eng].append(o)

    def barrier(self):
        lasts = []
        for e in ENGS:
            for o in reversed(self.q[e]):
                if not o.is_dma and o.fn is not None:
                    lasts.append(o)
                    break
        pend = list(self.dmas)
        self.dmas = []
        for b in self.sem_bufs:
            self.sem_pool.append((b.sem, b.cnt))
            b.sem = None
        self.sem_bufs = []
        for e in ENGS:
            o = Op(e, None)
            for p in lasts:
                if p.eng != e:
                    o.deps.append(p)
                    p.signal = True
            for p in pend:
                o.deps.append(p)
            self.q[e].append(o)

    def emit(self):
        nc = self.nc
        esems = {}
        for e in ENGS:
            k = 0
            for o in self.q[e]:
                if o.signal and not o.is_dma:
                    o.k = k
                    k += 1
            esems[e] = [self.new_sem(e) for _ in range(k // EPOCH + 1)]

        def tok(p):
            if p.is_dma:
                return p.dma_tok
            return (esems[p.eng][p.k // EPOCH], p.k % EPOCH + 1)

        def run(e, h):
            waited = {}
            for o in self.q[e]:
                for p in o.deps:
                    s, v = tok(p)
                    key = id(s)
                    if waited.get(key, 0) < v:
                        h.wait_ge(s, v)
                        waited[key] = v
                if o.fn is None:
                    continue
                ins = o.fn(h)
                if o.is_dma and o.k == -2:
                    ins.then_inc(o.dma_tok[0])
                    h.wait_ge(o.dma_tok[0], o.dma_tok[1])
                elif o.is_dma:
                    ins.then_inc(o.dma_tok[0], 16)
                elif o.signal:
                    s, v = tok(o)
                    ins.then_inc(s, 1)

        with nc.Block() as block:
            @block.tensor
            def _(h):
                run("pe", h)

            @block.scalar
            def _(h):
                run("act", h)

            @block.vector
            def _(h):
                run("dve", h)

            @block.gpsimd
            def _(h):
                run("pool", h)

            @block.sync
            def _(h):
                run("sp", h)


def _bucket_thresholds():
    n = np.arange(0, 128)
    nf = np.maximum(n, 1).astype(np.float32)
    large = 16 + (np.log(nf / np.float32(16)) / np.float32(math.log(128 / 16)) * np.float32(16)).astype(np.int32)
    large = np.minimum(large, 31)
    b = np.where(n < 16, n, large)
    lo = []
    for bb in range(1, 32):
        idx = np.nonzero(b >= bb)[0]
        lo.append(int(idx[0]) if len(idx) else 1 << 20)
    return lo


class Ctx:
    pass


class Builder:
    def __init__(self):
        self.nc = bass.Bass("TRN2", target_bir_lowering=False)
        self.S = Sched(self.nc)
        self.ins = {}
        self.outs = {}
        self.finals = []
        self.uid = 0
        self.off = self.SB_BASE
        self.peak = self.off
        self.ps = [self.nc.alloc_psum_tensor(f"psb{i}", [128, 512], F32) for i in range(8)]
        self.psb = [Buf(f"ps{i}") for i in range(8)]
        self.ones_bf = self.sb("ones", [128, 128], BF16)
        self.b_ones = Buf("ones")
        self.memset(self.ones_bf[:, :], 1.0, [self.b_ones])

    def inp(self, name, shape, dt=F32):
        t = self.nc.dram_tensor(name, list(shape), dt, kind="ExternalInput")
        self.ins[name] = t
        return t

    def outp(self, name, shape, dt=F32):
        t = self.nc.dram_tensor(name, list(shape), dt, kind="ExternalOutput")
        self.outs[name] = t
        return t

    SB_BASE = 16512
    SB_TOP = 229344

    def sb(self, name, shape, dt=F32):
        self.uid += 1
        sz = int(np.prod(shape[1:])) * (4 if dt in (F32, I32) else 2)
        sz = (sz + 31) // 32 * 32
        assert self.off + sz <= self.SB_TOP, f"SBUF overflow allocating {name} {shape}: off={self.off} sz={sz}"
        t = self.nc.alloc_sbuf_tensor_at(f"{name}_{self.uid}", list(shape), dt, offset=self.off)
        self.off += sz
        self.peak = max(self.peak, self.off)
        return t

    def mark(self):
        return self.off

    def release(self, mk):
        self.off = mk

    def mm(self, out, lhsT, rhs, start, stop, reads, writes):
        return self.S.op("pe", lambda h: h.matmul(out, lhsT=lhsT, rhs=rhs, start=start, stop=stop), reads, writes)

    def act(self, out, in_, func, reads, writes, scale=1.0, bias=0.0):
        return self.S.op("act", lambda h: h.activation(out=out, in_=in_, func=func, scale=scale, bias=bias), reads, writes)

    def tt(self, out, in0, in1, op, reads, writes, eng="dve"):
        return self.S.op(eng, lambda h: h.tensor_tensor(out=out, in0=in0, in1=in1, op=op), reads, writes)

    def ts(self, out, in0, s1, s2, op0, op1, reads, writes, eng="dve"):
        if s2 is None:
            return self.S.op(eng, lambda h: h.tensor_scalar(out=out, in0=in0, scalar1=s1, scalar2=None, op0=op0), reads, writes)
        return self.S.op(eng, lambda h: h.tensor_scalar(out=out, in0=in0, scalar1=s1, scalar2=s2, op0=op0, op1=op1), reads, writes)

    def stt(self, out, in0, scalar, in1, op0, op1, reads, writes, eng="dve"):
        return self.S.op(eng, lambda h: h.scalar_tensor_tensor(out=out, in0=in0, scalar=scalar, in1=in1, op0=op0, op1=op1), reads, writes)

    def copy(self, out, in_, reads, writes, eng="dve"):
        return self.S.op(eng, lambda h: h.tensor_copy(out=out, in_=in_), reads, writes)

    def recip(self, out, in_, reads, writes):
        return self.S.op("dve", lambda h: h.reciprocal(out=out, in_=in_), reads, writes)

    def memset(self, ap, val, writes, eng="pool"):
        return self.S.op(eng, lambda h: h.memset(ap, val), [], writes)

    def load(self, dst, src, buf, eng="sp", reads=()):
        return self.S.dma(eng, lambda h: h.dma_start(out=dst, in_=src), list(reads), [buf])

    def store(self, dst, src, srcbuf, eng="pool", final=False, dstbuf=None):
        o = self.S.dma(eng, lambda h: h.dma_start(out=dst, in_=src), [srcbuf],
                       [dstbuf] if dstbuf is not None else [Buf()], semb=srcbuf)
        if final:
            self.finals.append(o)
        return o

    def rstd(self, ss, n, out, tmp, reads, tmpbuf, outbuf):
        self.act(tmp, ss, AF.Ln, reads, [tmpbuf], scale=1.0 / n, bias=EPS)
        self.act(out, tmp, AF.Exp, [tmpbuf], [outbuf], scale=-0.5)

    def norm_work(self):
        w = Ctx()
        w.sq = self.sb("nsq", [128, 8, 512], BF16); w.b_sq = Buf()
        w.tmp = self.sb("ntmp", [128, 512], F32); w.b_tmp = Buf()
        w.rstd = self.sb("nrstd", [128, 512], F32); w.b_rstd = Buf()
        w.xn = self.sb("nxn", [128, 8, 512], F32); w.b_xn = Buf()
        return w

    def modnorm_tile(self, x, bx, t0, A, Bv, bmod, hdst, bh, w, pb=7):
        self.act(w.sq[:, :, :], x[:, :, t0:t0 + 512], AF.Square, [bx], [w.b_sq])
        for kc in range(8):
            self.mm(self.ps[pb][:, :], self.ones_bf[:, :], w.sq[:, kc, :], kc == 0, kc == 7, [w.b_sq, self.b_ones], [self.psb[pb]])
        self.rstd(self.ps[pb][:, :], float(D), w.rstd[:, :], w.tmp[:, :], [self.psb[pb]], w.b_tmp, w.b_rstd)
        self.tt(w.xn[:, :, :], x[:, :, t0:t0 + 512], w.rstd[:, None, :].to_broadcast([128, 8, 512]), ALU.mult, [bx, w.b_rstd], [w.b_xn])
        for kc in range(8):
            self.act(hdst[:, kc, :], w.xn[:, kc, :], AF.Identity, [w.b_xn, bmod], [bh], scale=A[:, kc:kc + 1], bias=Bv[:, kc:kc + 1])

    def mod_alloc(self):
        a = Ctx()
        a.cond = self.sb("cond", [128, 8], F32)
        a.bm = self.sb("bmod", [128, 72], F32)
        a.ng = self.sb("ng", [128, 3, 8], F32)
        a.modsb = self.sb("mod", [128, 72], F32)
        a.A = self.sb("modA", [128, 3, 8], F32)
        a.gate = self.sb("modG", [128, 3, 8], F32)
        return a

    def derive_mod(self, mod, b_mod, ng, b_ng, a):
        m = Ctx()
        m.A = a.A
        m.gate = a.gate
        m.mod = mod
        m.b = Buf("modder")
        for i in range(3):
            self.stt(m.A[:, i, :], mod[:, (3 * i + 1) * 8:(3 * i + 2) * 8], 1.0, ng[:, i, :], ALU.add, ALU.mult, [b_mod, b_ng], [m.b])
            self.ts(m.gate[:, i, :], mod[:, (3 * i + 2) * 8:(3 * i + 3) * 8], 1.0 if i == 1 else 0.5, None, ALU.mult, None, [b_mod], [m.b])
        m.B = lambda i: mod[:, (3 * i) * 8:(3 * i + 1) * 8]
        return m

    def ffn_work(self, nw):
        fw = Ctx()
        fw.h = self.sb("ffh", [128, 2, 8, 512], BF16)
        fw.b_h = [Buf(), Buf()]
        fw.act = self.sb("ffact", [128, NF, 1024], BF16)
        fw.b_act = [[Buf() for t in range(2)] for f in range(NF)]
        fw.wg = [self.sb(f"wg{i}", [128, 8, 2, 128], BF16) for i in range(3)]
        fw.b_wg = [Buf() for i in range(3)]
        fw.wd = [self.sb(f"wd{i}", [128, NF, 128], BF16) for i in range(2)]
        fw.b_wd = [Buf() for i in range(2)]
        fw.sg = [self.sb(f"sg{i}", [128, 512], F32) for i in range(2)]
        fw.b_sg = [Buf() for i in range(2)]
        fw.nw = nw
        return fw

    def ffn(self, x, bx, wgu_d, wdn_d, m, i, fw):
        A = m.A[:, i, :]
        Bv = m.B(i)
        gate = m.gate[:, i, :]
        for half in range(2):
            for tt in range(2):
                self.modnorm_tile(x, bx, half * 1024 + tt * 512, A, Bv, m.b, fw.h[:, tt, :, :], fw.b_h[tt], fw.nw)
            for f in range(NF):
                s = f % 3
                self.load(fw.wg[s][:, :, :, :], wgu_d[f, :, :, :, :], fw.b_wg[s], eng="pool")
                for tt in range(2):
                    par = (f * 2 + tt) % 2
                    pg, pu = par * 2, par * 2 + 1
                    for gu, pb in ((0, pg), (1, pu)):
                        for kc in range(8):
                            self.mm(self.ps[pb][:, :], fw.wg[s][:, kc, gu, :], fw.h[:, tt, kc, :], kc == 0, kc == 7, [fw.b_wg[s], fw.b_h[tt]], [self.psb[pb]])
                    self.act(fw.sg[par][:, :], self.ps[pg][:, :], AF.Silu, [self.psb[pg]], [fw.b_sg[par]])
                    self.tt(fw.act[:, f, tt * 512:(tt + 1) * 512], fw.sg[par][:, :], self.ps[pu][:, :], ALU.mult, [fw.b_sg[par], self.psb[pu]], [fw.b_act[f][tt]])
            for j in range(8):
                s = j % 2
                self.load(fw.wd[s][:, :, :], wdn_d[j, :, :, :], fw.b_wd[s], eng="pool")
                for tt in range(2):
                    pb = 4 + (j * 2 + tt) % 2
                    t0 = half * 1024 + tt * 512
                    for f in range(NF):
                        self.mm(self.ps[pb][:, :], fw.wd[s][:, f, :], fw.act[:, f, tt * 512:(tt + 1) * 512], f == 0, f == NF - 1, [fw.b_wd[s], fw.b_act[f][tt]], [self.psb[pb]])
                    self.stt(x[:, j, t0:t0 + 512], self.ps[pb][:, :], gate[:, j:j + 1], x[:, j, t0:t0 + 512], ALU.mult, ALU.add, [self.psb[pb], m.b, bx], [bx])


def emit_mod(Bd, cT, wmod, bmod, ngd, stage_bf, stage_bufs, a):
    cond = a.cond; b_cond = Buf()
    Bd.load(cond[:, :], cT[:, :], b_cond)
    Bd.act(cond[:, :], cond[:, :], AF.Silu, [b_cond], [b_cond])
    bm = a.bm; b_bm = Buf()
    Bd.load(bm[:, :], bmod[:, :], b_bm)
    ng = a.ng; b_ng = Buf()
    Bd.load(ng[:, :, :], ngd[:, :, :], b_ng)
    modsb = a.modsb; b_modsb = Buf()
    pm = 6
    for j9 in range(9):
        for hh in range(2):
            s = (j9 * 2 + hh) % 2
            stage = stage_bf[s]
            col0 = j9 * 1024 + hh * 512
            Bd.load(stage[:, :, :], wmod[0:128, :, col0:col0 + 512], stage_bufs[s])
            for jc in range(4):
                oc = j9 * 8 + hh * 4 + jc
                for kc in range(8):
                    Bd.mm(Bd.ps[pm][:, oc:oc + 1], stage[:, kc, jc * 128:(jc + 1) * 128], cond[:, kc:kc + 1], kc == 0, kc == 7, [stage_bufs[s], b_cond], [Bd.psb[pm]])
    Bd.tt(modsb[:, :], Bd.ps[pm][:, 0:72], bm[:, :], ALU.add, [Bd.psb[pm], b_bm], [b_modsb])
    return modsb, b_modsb, ng, b_ng


def emit_proj(Bd, x, bx, m, fw, di, do, final):
    S = Bd.S
    ps, psb = Bd.ps, Bd.psb
    ones = Bd.ones_bf
    b_ones = Bd.b_ones

    def wload(name, shape, dt, eng="pool"):
        t = Bd.sb(name, shape, dt); b = Buf(name)
        src = di[name]
        Bd.load(t[tuple(slice(None) for _ in shape)], src[(slice(0, shape[0]),) + tuple(slice(None) for _ in shape[1:])], b, eng=eng)
        return t, b
    win, b_win = wload("win", [128, 8, 1824], BF16)
    winv, b_winv = wload("winv", [128, 8, 384], BF16)
    wuq, b_wuq = wload("wuq", [128, 2, 576], BF16)
    wukvk, b_wukvk = wload("wukvk", [128, 6, 64], BF16)
    wukvv, b_wukvv = wload("wukvv", [128, 384], BF16)
    qan, b_qan = wload("qan", [128, 2], F32, "sp")
    kvan, b_kvan = wload("kvan", [128, 1], F32, "sp")
    hn, b_hn = wload("hn", [96, 4], F32, "sp")
    rot, b_rot = wload("rot", [96, 96], F32, "sp")
    freq, b_freq = wload("freq", [96, 1], F32, "sp")

    Ct = Bd.sb("ropeC", [96, T], F32); b_C = Buf()
    St = Bd.sb("ropeS", [96, T], F32); b_S = Buf()
    mk_rope = Bd.mark()
    ang = Bd.sb("ang", [96, T], F32); b_ang = Buf()
    rt = Bd.sb("rt", [96, T], F32); b_rt = Buf()
    ki = Bd.sb("ki", [96, T], I32); b_ki = Buf()
    posi = ki; b_posi = b_ki
    Bd.load(posi[:, :], di["pos"][0:1, :].partition_broadcast(96), b_posi)
    C1 = 6.28125
    C2 = TWO_PI - C1
    Bd.copy(ang[:, :], posi[:, :], [b_posi], [b_ang])
    Bd.ts(ang[:, :], ang[:, :], freq[:, 0:1], None, ALU.mult, None, [b_ang, b_freq], [b_ang])
    for (dst, bd, shift) in ((St, b_S, 0.0), (Ct, b_C, math.pi / 2)):
        Bd.ts(rt[:, :], ang[:, :], shift, 1.0 / TWO_PI, ALU.add, ALU.mult, [b_ang], [b_rt])
        Bd.copy(ki[:, :], rt[:, :], [b_rt], [b_ki])
        Bd.copy(rt[:, :], ki[:, :], [b_ki], [b_rt])
        Bd.stt(dst[:, :], rt[:, :], -C1, ang[:, :], ALU.mult, ALU.add, [b_rt, b_ang], [bd])
        Bd.stt(dst[:, :], rt[:, :], -C2, dst[:, :], ALU.mult, ALU.add, [b_rt, bd], [bd])
        Bd.ts(dst[:, :], dst[:, :], shift, math.pi, ALU.add, ALU.min, [bd], [bd])
        Bd.ts(dst[:, :], dst[:, :], -math.pi, None, ALU.max, None, [bd], [bd])
        Bd.act(dst[:, :], dst[:, :], AF.Sin, [bd], [bd])
    S.barrier()
    Bd.release(mk_rope)

    h2 = Bd.sb("h2", [128, 8, 512], BF16); b_h2 = Buf()
    cqn = Bd.sb("cqn", [128, 2, 512], BF16); b_cqn = Buf()
    ckvn = Bd.sb("ckvn", [128, 512], BF16); b_ckvn = Buf()
    sqw = [Bd.sb(f"sqw{i}", [128, 2, 512], BF16) for i in range(2)]; b_sqw = [Buf() for i in range(2)]
    rsw = [Bd.sb(f"rsw{i}", [128, 512], F32) for i in range(2)]; b_rsw = [Buf() for i in range(2)]
    xnw = [Bd.sb(f"xnw{i}", [96, 512], F32) for i in range(2)]; b_xnw = [Buf() for i in range(2)]
    t1w = [Bd.sb(f"t1w{i}", [96, 512], F32) for i in range(2)]; b_t1w = [Buf() for i in range(2)]

    def stg(name, shape):
        t = Bd.sb(name, shape, BF16); b = Buf()
        return [t, t], [b, b]
    st_qm, b_stqm = stg("stqm", [96, 6, 512])
    st_km, b_stkm = stg("stkm", [96, 6, 512])
    st_qs, b_stqs = stg("stqs", [64, 6, 512])
    st_ks, b_stks = stg("stks", [64, 2, 512])
    st_qc, b_stqc = stg("stqc", [64, 4, 512])
    st_kc, b_stkc = stg("stkc", [64, 4, 512])
    st_vm, b_stvm = stg("stvm", [128, 6, 65])
    st_vs, b_stvs = stg("stvs", [128, 2, 65])
    st_vc, b_stvc = stg("stvc", [128, 4, 65])
    Bd.memset(st_vs[0][:, :, :], 1.0, [b_stvs[0]])
    Bd.memset(st_vm[0][:, :, :], 1.0, [b_stvm[0]])
    Bd.memset(st_vc[0][:, :, :], 1.0, [b_stvc[0]])

    A = m.A[:, 1, :]
    Bv = m.B(1)
    cnt = [0]

    def nxt():
        cnt[0] += 1
        return cnt[0] % 2

    PSS = 6

    def head_norm(pb, rows, gain, n, out_ap, outbuf, rope, t0):
        w = nxt()
        pin = ps[pb][0:rows, :]
        Bd.act(sqw[w][0:rows, 0, :], pin, AF.Square, [psb[pb]], [b_sqw[w]])
        Bd.mm(ps[PSS][0:rows, :], ones[0:rows, 0:rows], sqw[w][0:rows, 0, :], True, True, [b_sqw[w], b_ones], [psb[PSS]])
        Bd.rstd(ps[PSS][0:rows, :], float(n), rsw[w][0:rows, :], rsw[w][0:rows, :], [psb[PSS]], b_rsw[w], b_rsw[w])
        if not rope:
            Bd.stt(out_ap, pin, gain, rsw[w][0:rows, :], ALU.mult, ALU.mult, [psb[pb], b_rsw[w], b_hn], [outbuf])
            return
        Bd.stt(xnw[w][:, :], pin, gain, rsw[w][0:rows, :], ALU.mult, ALU.mult, [psb[pb], b_rsw[w], b_hn], [b_xnw[w]])
        Bd.mm(ps[PSS][0:96, :], rot[:, :], xnw[w][:, :], True, True, [b_rot, b_xnw[w]], [psb[PSS]])
        Bd.tt(t1w[w][:, :], xnw[w][:, :], Ct[:, t0:t0 + 512], ALU.mult, [b_xnw[w], b_C], [b_t1w[w]])
        Bd.tt(xnw[w][:, :], ps[PSS][0:96, :], St[:, t0:t0 + 512], ALU.mult, [psb[PSS], b_S], [b_xnw[w]])
        Bd.tt(out_ap, t1w[w][:, :], xnw[w][:, :], ALU.add, [b_t1w[w], b_xnw[w]], [outbuf])

    def units(name):
        u = do[name]
        return u if isinstance(u, list) else [(u, 0, u.shape[0])]

    def store3(name, t0, src, srcbuf):
        o = do.get(name + "_off", 0)
        for (ap, h0, nh) in units(name):
            Bd.store(ap.rearrange("h p t -> p h t")[:, :, o + t0:o + t0 + 512], src[:, h0:h0 + nh, :], srcbuf, final=final, dstbuf=do.get(name + "_buf"))

    def storev(name, mblk, src, srcbuf):
        o = do.get(name + "_off", 0)
        for (ap, h0, nh) in units(name):
            Bd.store(ap.rearrange("h p m d -> p h m d")[:, :, o + mblk, :], src[:, h0:h0 + nh, :], srcbuf, final=final, dstbuf=do.get(name + "_buf"))

    for tt in range(4):
        t0 = tt * 512
        par = tt % 2
        Bd.modnorm_tile(x, bx, t0, A, Bv, m.b, h2, b_h2, fw.nw)
        for cc in range(2):
            for kc in range(8):
                Bd.mm(ps[cc][:, :], win[:, kc, cc * 128:(cc + 1) * 128], h2[:, kc, :], kc == 0, kc == 7, [b_win, b_h2], [psb[cc]])
        w = nxt()
        for cc in range(2):
            Bd.act(sqw[w][:, cc, :], ps[cc][:, :], AF.Square, [psb[cc]], [b_sqw[w]])
        for cc in range(2):
            Bd.mm(ps[PSS][:, :], ones[:, :], sqw[w][:, cc, :], cc == 0, cc == 1, [b_sqw[w], b_ones], [psb[PSS]])
        Bd.rstd(ps[PSS][:, :], 256.0, rsw[w][:, :], rsw[w][:, :], [psb[PSS]], b_rsw[w], b_rsw[w])
        for cc in range(2):
            Bd.stt(cqn[:, cc, :], ps[cc][:, :], qan[:, cc:cc + 1], rsw[w][:, :], ALU.mult, ALU.mult, [psb[cc], b_rsw[w], b_qan], [b_cqn])
        for kc in range(8):
            Bd.mm(ps[2][:, :], win[:, kc, 256:384], h2[:, kc, :], kc == 0, kc == 7, [b_win, b_h2], [psb[2]])
        w = nxt()
        Bd.act(sqw[w][:, 0, :], ps[2][:, :], AF.Square, [psb[2]], [b_sqw[w]])
        Bd.mm(ps[PSS][:, :], ones[:, :], sqw[w][:, 0, :], True, True, [b_sqw[w], b_ones], [psb[PSS]])
        Bd.rstd(ps[PSS][:, :], 128.0, rsw[w][:, :], rsw[w][:, :], [psb[PSS]], b_rsw[w], b_rsw[w])
        Bd.stt(ckvn[:, :], ps[2][:, :], kvan[:, 0:1], rsw[w][:, :], ALU.mult, ALU.mult, [psb[2], b_rsw[w], b_kvan], [b_ckvn])
        for hh in range(6):
            pb = hh % 2
            for kc in range(2):
                Bd.mm(ps[pb][0:96, :], wuq[:, kc, hh * 96:(hh + 1) * 96], cqn[:, kc, :], kc == 0, kc == 1, [b_wuq, b_cqn], [psb[pb]])
            head_norm(pb, 96, hn[0:96, 0:1], 96, st_qm[par][:, hh, :], b_stqm[par], True, t0)
        store3("qm", t0, st_qm[par], b_stqm[par])
        for hh in range(6):
            pb = 2 + hh % 2
            Bd.mm(ps[pb][0:64, :], wukvk[:, hh, :], ckvn[:, :], True, True, [b_wukvk, b_ckvn], [psb[pb]])
            for kc in range(8):
                Bd.mm(ps[pb][64:96, :], win[:, kc, 384:416], h2[:, kc, :], kc == 0, kc == 7, [b_win, b_h2], [psb[pb]])
            head_norm(pb, 96, hn[0:96, 1:2], 96, st_km[par][:, hh, :], b_stkm[par], True, t0)
        store3("km", t0, st_km[par], b_stkm[par])
        for hh in range(6):
            pb = hh % 2
            for kc in range(8):
                Bd.mm(ps[pb][0:64, :], win[:, kc, 416 + hh * 64:416 + (hh + 1) * 64], h2[:, kc, :], kc == 0, kc == 7, [b_win, b_h2], [psb[pb]])
            head_norm(pb, 64, hn[0:64, 2:3], 64, st_qs[par][:, hh, :], b_stqs[par], False, t0)
        store3("qs", t0, st_qs[par], b_stqs[par])
        for hh in range(2):
            pb = 2 + hh % 2
            for kc in range(8):
                Bd.mm(ps[pb][0:64, :], win[:, kc, 800 + hh * 64:800 + (hh + 1) * 64], h2[:, kc, :], kc == 0, kc == 7, [b_win, b_h2], [psb[pb]])
            head_norm(pb, 64, hn[0:64, 3:4], 64, st_ks[par][:, hh, :], b_stks[par], False, t0)
        store3("ks", t0, st_ks[par], b_stks[par])
        for hh in range(4):
            pb = hh % 2
            for kc in range(8):
                Bd.mm(ps[pb][0:64, :], win[:, kc, 1056 + hh * 64:1056 + (hh + 1) * 64], h2[:, kc, :], kc == 0, kc == 7, [b_win, b_h2], [psb[pb]])
            Bd.act(st_qc[par][:, hh, :], ps[pb][0:64, :], AF.Copy, [psb[pb]], [b_stqc[par]], scale=0.125)
        store3("qc", t0, st_qc[par], b_stqc[par])
        for hh in range(4):
            pb = 2 + hh % 2
            for kc in range(8):
                Bd.mm(ps[pb][0:64, :], win[:, kc, 1312 + hh * 64:1312 + (hh + 1) * 64], h2[:, kc, :], kc == 0, kc == 7, [b_win, b_h2], [psb[pb]])
            Bd.copy(st_kc[par][:, hh, :], ps[pb][0:64, :], [psb[pb]], [b_stkc[par]])
        store3("kc", t0, st_kc[par], b_stkc[par])
        for blk in range(4):
            mblk = tt * 4 + blk
            bp = blk % 2
            pb = 4 + bp
            Bd.mm(ps[pb][:, 0:384], ckvn[:, blk * 128:(blk + 1) * 128], wukvv[:, :], True, True, [b_ckvn, b_wukvv], [psb[pb]])
            Bd.act(st_vm[bp][:, :, 0:64], ps[pb][:, 0:384].rearrange("p (h d) -> p h d", h=6), AF.Copy, [psb[pb]], [b_stvm[bp]])
            storev("vm", mblk, st_vm[bp], b_stvm[bp])
            pb2 = bp
            for kc in range(8):
                Bd.mm(ps[pb2][:, 0:384], h2[:, kc, blk * 128:(blk + 1) * 128], winv[:, kc, :], kc == 0, kc == 7, [b_h2, b_winv], [psb[pb2]])
            Bd.copy(st_vs[bp][:, :, 0:64], ps[pb2][:, 0:128].rearrange("p (h d) -> p h d", h=2), [psb[pb2]], [b_stvs[bp]])
            Bd.copy(st_vc[bp][:, :, 0:64], ps[pb2][:, 128:384].rearrange("p (h d) -> p h d", h=4), [psb[pb2]], [b_stvc[bp]])
            storev("vs", mblk, st_vs[bp], b_stvs[bp])
            storev("vc", mblk, st_vc[bp], b_stvc[bp])


LO_B = _bucket_thresholds()
MLA_SCALE = 96.0 ** -0.5


def attn_consts(Bd, di, w, bias_cache=None, first=True):
    S = Bd.S
    a = Ctx()
    a.zer = Bd.sb("zer", [128, 128], BF16); a.b_zer = Buf()
    Bd.memset(a.zer[:, :], 0.0, [a.b_zer])
    a.zrhs = Bd.sb("zrhs", [128, 512], BF16); a.b_zrhs = Buf()
    Bd.memset(a.zrhs[:, :], 0.0, [a.b_zrhs])
    a.tri = Bd.sb("tri", [128, 2, 128], BF16); a.b_tri = Buf()
    Bd.load(a.tri[:, :, :], di["tricomp"][:, :, :], a.b_tri, eng="pool")
    a.mincl = Bd.sb("mincl", [128, 4, 128], F32); a.b_mincl = Buf()
    Bd.load(a.mincl[:, :, :], di["mincl"][:, :, :], a.b_mincl)
    a.mstr = Bd.sb("mstr", [128, 4, 128], F32); a.b_mstr = Buf()
    Bd.load(a.mstr[:, :, :], di["mstrict"][:, :, :], a.b_mstr)
    a.sel = Bd.sb("sel", [65, 64], F32); a.b_sel = Buf()
    Bd.memset(a.sel[:, :], 0.0, [a.b_sel])
    Bd.memset(a.sel[64:65, :], 1.0, [a.b_sel])
    a.vsink = Bd.sb("vsink", [1, 65], F32); a.b_vsink = Buf()
    Bd.memset(a.vsink[:, :], 0.0, [a.b_vsink])
    Bd.memset(a.vsink[0:1, 64:65], 1.0, [a.b_vsink])
    a.wsel = Bd.sb("wsel", [128, 4], F32); a.b_wsel = Buf()
    Bd.load(a.wsel[:, :], di["wsel"][:, :], a.b_wsel)
    a.onorm = Bd.sb("onorm", [64, 16], F32); a.b_onorm = Buf()
    Bd.load(a.onorm[:, :], di["onorm"][:, :], a.b_onorm)
    es = Bd.sb("es", [1, 6], F32); b_es = Buf()
    Bd.load(es[:, :], di["sinks"][:, :], b_es)
    Bd.act(es[:, :], es[:, :], AF.Exp, [b_es], [b_es])
    a.esr = Bd.sb("esr", [1, 6, 128], F32); a.b_esr = Buf()
    Bd.copy(a.esr[:, :, :], es[0:1, :, None].to_broadcast([1, 6, 128]), [b_es], [a.b_esr])
    a.bias = Bd.sb("swabias", [128, 2, 6, 128], F32); a.b_bias = Buf()
    if bias_cache is not None and not first:
        Bd.load(a.bias[:, :, :, :].rearrange("p w h q -> p (w h q)"), bias_cache[0].ap(), a.b_bias, reads=[bias_cache[1]])
    else:
        tb = w.e[0][:, 0:192].rearrange("p (b h) -> p b h", h=6); b_tb = w.b_e[0]
        Bd.load(w.e[0][:, 0:192], di["relb"][0:1, :].partition_broadcast(128), b_tb)
        dtb = w.e[1][:, 0:186].rearrange("p (b h) -> p b h", h=6); b_dtb = w.b_e[1]
        Bd.tt(dtb[:, :, :], tb[:, 1:32, :], tb[:, 0:31, :], ALU.subtract, [b_tb], [b_dtb])
        pqi = w.e[3][:, 256:384].bitcast(I32); b_pqi = w.b_e[3]
        Bd.load(pqi, di["posrow"][0:1, :].partition_broadcast(128), b_pqi)
        pki = w.e[3][:, 384:386].bitcast(I32); b_pki = w.b_e[3]
        Bd.load(pki, di["poscol"][:, :], b_pki)
        pq = w.e[2][:, 0:128]; b_pq = w.b_e[2]
        pk = w.e[2][:, 128:130]; b_pk = w.b_e[2]
        Bd.copy(pq, pqi, [b_pqi], [b_pq])
        Bd.copy(pk, pki, [b_pki], [b_pk])
        rel = w.ec[0][:, 0:256].rearrange("p (w q) -> p w q", w=2); b_rel = w.b_ec[0]
        for wi in range(2):
            Bd.ts(rel[:, wi, :], pq, pk[:, wi:wi + 1], None, ALU.subtract, None, [b_pq, b_pk], [b_rel])
        ind = w.ec[1][:, 0:256].rearrange("p (w q) -> p w q", w=2); b_ind = w.b_ec[1]
        for h in range(6):
            Bd.ts(a.bias[:, :, h, :], rel[:, :, :], 0.0, tb[:, 0, h:h + 1], ALU.mult, ALU.add, [b_rel, b_tb], [a.b_bias])
        for b in range(1, 32):
            Bd.ts(ind[:, :, :], rel[:, :, :], float(LO_B[b - 1]), None, ALU.is_ge, None, [b_rel], [b_ind])
            for h in range(6):
                Bd.stt(a.bias[:, :, h, :], ind[:, :, :], dtb[:, b - 1, h:h + 1], a.bias[:, :, h, :], ALU.mult, ALU.add, [b_ind, b_dtb, a.b_bias], [a.b_bias])
        val = w.e[1][:, 256:512].rearrange("p (w q) -> p w q", w=2); b_val = w.b_e[1]
        Bd.ts(val[:, :, :], rel[:, :, :], 0.0, None, ALU.is_ge, None, [b_rel], [b_val])
        Bd.ts(ind[:, :, :], rel[:, :, :], 128.0, None, ALU.is_ge, None, [b_rel], [b_ind])
        Bd.ts(ind[:, :, :], ind[:, :, :], -1.0, 1.0, ALU.mult, ALU.add, [b_ind], [b_ind])
        Bd.tt(val[:, :, :], val[:, :, :], ind[:, :, :], ALU.mult, [b_val, b_ind], [b_val])
        Bd.ts(ind[:, :, :], val[:, :, :], 1.0e4, -1.0e4, ALU.mult, ALU.add, [b_val], [b_ind])
        for h in range(6):
            Bd.tt(a.bias[:, :, h, :], a.bias[:, :, h, :], val[:, :, :], ALU.mult, [a.b_bias, b_val], [a.b_bias])
            Bd.tt(a.bias[:, :, h, :], a.bias[:, :, h, :], ind[:, :, :], ALU.add, [a.b_bias, b_ind], [a.b_bias])

        if bias_cache is not None:
            Bd.store(bias_cache[0].ap(), a.bias[:, :, :, :].rearrange("p w h q -> p (w h q)"), a.b_bias, eng="sp", dstbuf=bias_cache[1])
    return a


def attn_work(Bd):
    w = Ctx()
    w.kbuf = [Bd.sb(f"kbuf{i}", [96, 4, T], BF16) for i in range(2)]; w.b_k = [Buf() for _ in range(2)]
    w.vbuf = [Bd.sb(f"vbuf{i}", [128, 4, NBLK, 65], BF16) for i in range(2)]; w.b_v = [Buf() for _ in range(2)]
    w.og = Bd.sb("ogrp", [64, 6, 512], F32); w.b_og = [Buf() for _ in range(6)]
    w.mix = Bd.sb("mix", [64, 16, 512], BF16); w.b_mix = [Buf() for _ in range(16)]
    w.qm = Bd.sb("qmg", [96, 6, 512], BF16); w.b_qm = Buf()
    w.qc = Bd.sb("qcg", [64, 4, 512], BF16); w.b_qc = Buf()
    w.qs = Bd.sb("qsg", [64, 6, 512], BF16); w.b_qs = Buf()
    w.p = [Bd.sb(f"pw{i}", [128, 512], BF16) for i in range(2)]; w.b_p = [Buf() for _ in range(2)]
    w.e = [Bd.sb(f"ew{i}", [128, 512], F32) for i in range(4)]; w.b_e = [Buf() for _ in range(4)]
    w.sp = [Bd.sb(f"spw{i}", [128, 512], BF16) for i in range(4)]; w.b_sp = [Buf() for _ in range(4)]
    w.ec = [Bd.sb(f"ecw{i}", [128, 512], F32) for i in range(2)]; w.b_ec = [Buf() for _ in range(2)]
    w.a = w.p; w.b_a = w.b_p
    w.oa = Bd.sb("oa", [65, 512], F32); w.b_oa = Buf()
    w.rden = Bd.sb("rden", [64, 512], F32); w.b_rden = Buf()
    w.ksw = Bd.sb("ksw", [64, 2, 2, 512], BF16); w.b_ksw = Buf()
    w.vsw = Bd.sb("vsw", [128, 2, 2, 4, 65], BF16); w.b_vsw = Buf()
    w.sarg = [w.e[i][:, 0:384].rearrange("p (j q) -> p j q", j=3) for i in range(2)]; w.b_sarg = w.b_e[0:2]
    w.psw = [w.sp[i][:, 0:384].rearrange("p (j q) -> p j q", j=3) for i in range(2)]; w.b_psw = w.b_sp[0:2]
    w.gsq = w.sp[2][0:64, :]; w.b_gsq = w.b_sp[2]
    w.grs = w.e[2][0:64, :]; w.b_grs = w.b_e[2]
    wo = Bd.sb("wo", [64, 16, 128], BF16); bwo = Buf()
    w.wo = [wo, wo]; w.b_wo = [bwo, bwo]
    return w


def emit_attention(Bd, x, bx, m, ac, w, di):
    S = Bd.S
    ps, psb = Bd.ps, Bd.psb
    ones = Bd.ones_bf
    for G in range(4):
        t0 = G * 512
        nk = 4 * G + 4
        ntok = nk * 128
        nsteps = 16 * G + 16
        Bd.load(w.qm[:, :, :], di["qm"].rearrange("h p t -> p h t")[:, :, t0:t0 + 512], w.b_qm, reads=[di["b_q"]])
        Bd.load(w.qc[:, :, :], di["qc"].rearrange("h p t -> p h t")[:, :, t0:t0 + 512], w.b_qc, reads=[di["b_q"]])
        Bd.load(w.qs[:, :, :], di["qs"].rearrange("h p t -> p h t")[:, :, t0:t0 + 512], w.b_qs, reads=[di["b_q"]])

        def kvload(slot, kf, vf, h, rows):
            Bd.load(w.kbuf[slot][0:rows, :, 0:ntok], di[kf][h][0].rearrange("r p t -> p r t")[:, :, 0:ntok], w.b_k[slot], reads=[di[kf][h][1]])
            Bd.load(w.vbuf[slot][:, :, 0:nk, :], di[vf][h][0].rearrange("r p m d -> p r m d")[:, :, 0:nk, :], w.b_v[slot], reads=[di[vf][h][1]])

        def step_geom(i):
            kb = nsteps - 1 - i
            kw = kb - 16 * G
            c0 = (kw // 4) * 128 if kw >= 0 else 0
            return kb % 4, kb // 4, c0, (kw % 4 if kw >= 0 else None)

        for pair in range(3):
            hs = (2 * pair, 2 * pair + 1)
            for hi, h in enumerate(hs):
                kvload(hi, "kmf", "vmf", h, 96)
            PS = {0: (0, 1), 1: (2, 7)}
            POm = {0: 3, 1: 4}
            PT = {0: ((w.p[0], w.b_p[0]), (w.p[1], w.b_p[1])), 1: ((w.sp[2], w.b_sp[2]), (w.sp[3], w.b_sp[3]))}
            for hi in range(2):
                Bd.mm(ps[POm[hi]][0:65, :], ac.zer[:, 0:65], ac.zrhs[:, :], True, False, [ac.b_zer, ac.b_zrhs], [psb[POm[hi]]])

            def m_s(hi, i):
                rk, mb, c0, dm = step_geom(i)
                pb = PS[hi][i % 2]
                Bd.mm(ps[pb][:, c0:512], w.kbuf[hi][0:96, rk, mb * 128:(mb + 1) * 128], w.qm[0:96, hs[hi], c0:512], True, True, [w.b_k[hi], w.b_qm], [psb[pb]])

            def m_e(hi, i):
                rk, mb, c0, dm = step_geom(i)
                pb = PS[hi][i % 2]
                pt, bpt = PT[hi][i % 2]
                Bd.act(pt[:, c0:512], ps[pb][:, c0:512], AF.Exp, [psb[pb]], [bpt], scale=MLA_SCALE)
                if dm is not None:
                    Bd.tt(pt[:, c0:c0 + 128], pt[:, c0:c0 + 128], ac.mincl[:, dm, :], ALU.mult, [bpt, ac.b_mincl], [bpt])

            def m_pv(hi, i):
                rk, mb, c0, dm = step_geom(i)
                pt, bpt = PT[hi][i % 2]
                Bd.mm(ps[POm[hi]][0:65, c0:512], w.vbuf[hi][:, rk, mb, 0:65], pt[:, c0:512], False, i == nsteps - 1, [w.b_v[hi], bpt], [psb[POm[hi]]])
            for hi in range(2):
                m_s(hi, 0)
            for hi in range(2):
                m_e(hi, 0)
            for i in range(nsteps):
                if i + 1 < nsteps:
                    for hi in range(2):
                        m_s(hi, i + 1)
                for hi in range(2):
                    m_pv(hi, i)
                if i + 1 < nsteps:
                    for hi in range(2):
                        m_e(hi, i + 1)
            for hi, h in enumerate(hs):
                po = POm[hi]
                Bd.act(w.oa[:, :], ps[po][0:65, :], AF.Copy, [psb[po]], [w.b_oa])
                Bd.mm(ps[5][0:64, :], ac.sel[:, :], w.oa[:, :], True, True, [ac.b_sel, w.b_oa], [psb[5]])
                Bd.recip(w.rden[:, :], ps[5][0:64, :], [psb[5]], [w.b_rden])
                Bd.tt(w.og[:, h, :], w.oa[0:64, :], w.rden[:, :], ALU.mult, [w.b_oa, w.b_rden], [w.b_og[h]])
        group_norm(Bd, w, ac, 0, 6, 384.0, G)

        Bd.load(w.ksw[:, :, 1, :], di["ks_own"].rearrange("h p t -> p h t")[:, :, 128 + t0:128 + t0 + 512], w.b_ksw, reads=[di["b_src"]])
        Bd.load(w.vsw[:, :, 1, :, :], di["vs_own"].rearrange("h p m d -> p h m d")[:, :, 1 + 4 * G:5 + 4 * G, :], w.b_vsw, reads=[di["b_src"]])
        for c in range(4):
            ko = 128 + t0 if c < 3 else t0
            vo = 1 + 4 * G if c < 3 else 4 * G
            kcand = w.kbuf[0][0:64, c, 0:1024].rearrange("p (k t) -> p k t", k=2)
            vcand = w.vbuf[0][:, c, 0:8, :].rearrange("p (k m) d -> p k m d", k=2)
            Bd.load(kcand, di["ksf"].rearrange("r h p t -> p r h t")[:, c, :, ko:ko + 512], w.b_k[0], reads=[di["b_ks"]])
            Bd.load(vcand, di["vsf"].rearrange("r h p m d -> p r h m d")[:, c, :, vo:vo + 4, :], w.b_v[0], reads=[di["b_vs"]])
        for c in range(4):
            kcand = w.kbuf[0][0:64, c, 0:1024].rearrange("p (k t) -> p k t", k=2)
            vcand = w.vbuf[0][:, c, 0:8, :].rearrange("p (k m) d -> p k m d", k=2)
            if c == 0:
                Bd.ts(w.ksw[:, :, 0, :], kcand, ac.wsel[0:64, 0:1], None, ALU.mult, None, [w.b_k[0], ac.b_wsel], [w.b_ksw])
                Bd.ts(w.vsw[:, :, 0, :, :], vcand, ac.wsel[:, 0:1], None, ALU.mult, None, [w.b_v[0], ac.b_wsel], [w.b_vsw])
            else:
                Bd.stt(w.ksw[:, :, 0, :], kcand, ac.wsel[0:64, c:c + 1], w.ksw[:, :, 0, :], ALU.mult, ALU.add, [w.b_k[0], ac.b_wsel, w.b_ksw], [w.b_ksw])
                Bd.stt(w.vsw[:, :, 0, :, :], vcand, ac.wsel[:, c:c + 1], w.vsw[:, :, 0, :, :], ALU.mult, ALU.add, [w.b_v[0], ac.b_wsel, w.b_vsw], [w.b_vsw])
        for blk in range(4):
            q0 = blk * 128
            for kvh in range(2):
                pso = 3 + kvh
                Bd.mm(ps[pso][0:65, 0:384], ac.zer[:, 0:65], ac.zrhs[:, 0:384], True, False, [ac.b_zer, ac.b_zrhs], [psb[pso]])
                for wh in range(2):
                    pss = (blk * 4 + kvh * 2 + wh) % 2
                    for j in range(3):
                        hq = kvh * 3 + j
                        Bd.mm(ps[pss][:, j * 128:(j + 1) * 128], w.ksw[:, kvh, wh, q0:q0 + 128], w.qs[:, hq, q0:q0 + 128], True, True, [w.b_ksw, w.b_qs], [psb[pss]])
                    Bd.stt(w.sarg[pss][:, :, :], ps[pss][:, 0:384].rearrange("p (j q) -> p j q", j=3), 0.125, ac.bias[:, wh, kvh * 3:(kvh + 1) * 3, :], ALU.mult, ALU.add, [psb[pss], ac.b_bias], [w.b_sarg[pss]])
                    Bd.act(w.psw[pss][:, :, :], w.sarg[pss][:, :, :], AF.Exp, [w.b_sarg[pss]], [w.b_psw[pss]])
                    for j in range(3):
                        Bd.mm(ps[pso][0:65, j * 128:(j + 1) * 128], w.vsw[:, kvh, wh, blk, 0:65], w.psw[pss][:, j, :], False, False, [w.b_vsw, w.b_psw[pss]], [psb[pso]])
                for j in range(3):
                    hq = kvh * 3 + j
                    Bd.mm(ps[pso][0:65, j * 128:(j + 1) * 128], ac.vsink[0:1, :], ac.esr[0:1, hq, :], False, True, [ac.b_vsink, ac.b_esr], [psb[pso]])
                Bd.act(w.oa[:, 0:384], ps[pso][0:65, 0:384], AF.Copy, [psb[pso]], [w.b_oa])
                Bd.mm(ps[5][0:64, 0:384], ac.sel[:, :], w.oa[:, 0:384], True, True, [ac.b_sel, w.b_oa], [psb[5]])
                Bd.recip(w.rden[:, 0:384], ps[5][0:64, 0:384], [psb[5]], [w.b_rden])
                for j in range(3):
                    hq = kvh * 3 + j
                    Bd.tt(w.og[:, hq, q0:q0 + 128], w.oa[0:64, j * 128:(j + 1) * 128], w.rden[:, j * 128:(j + 1) * 128], ALU.mult, [w.b_oa, w.b_rden], [w.b_og[hq]])
        group_norm(Bd, w, ac, 6, 6, 384.0, G)

        for pair in range(2):
            hs = (2 * pair, 2 * pair + 1)
            for hi, h in enumerate(hs):
                kvload(hi, "kcf", "vcf", h, 64)
            PZ = {0: (0, 1), 1: (2, 7)}
            PC = {0: 3, 1: 4}
            PO = {0: 5, 1: 6}
            for hi in range(2):
                Bd.mm(ps[PC[hi]][:, :], ac.zer[:, :], ac.zrhs[:, :], True, False, [ac.b_zer, ac.b_zrhs], [psb[PC[hi]]])
                Bd.mm(ps[PO[hi]][0:64, :], ac.zer[:, 0:64], ac.zrhs[:, :], True, False, [ac.b_zer, ac.b_zrhs], [psb[PO[hi]]])

            def s_z(hi, i):
                h = hs[hi]
                rk, mb, c0, dm = step_geom(i)
                pz = PZ[hi][i % 2]
                Bd.mm(ps[pz][:, c0:512], w.kbuf[hi][0:64, rk, mb * 128:(mb + 1) * 128], w.qc[0:64, h, c0:512], True, True, [w.b_k[hi], w.b_qc], [psb[pz]])

            def s_e(hi, i):
                rk, mb, c0, dm = step_geom(i)
                pz = PZ[hi][i % 2]
                ew = hi * 2 + i % 2
                Bd.act(w.e[ew][:, c0:512], ps[pz][:, c0:512], AF.Exp, [psb[pz]], [w.b_e[ew]])
                if dm is not None:
                    Bd.tt(w.e[ew][:, c0:c0 + 128], w.e[ew][:, c0:c0 + 128], ac.mstr[:, dm, :], ALU.mult, [w.b_e[ew], ac.b_mstr], [w.b_e[ew]])

            def s_sp(hi, i):
                rk, mb, c0, dm = step_geom(i)
                ew = hi * 2 + i % 2
                Bd.act(w.sp[ew][:, c0:512], w.e[ew][:, c0:512], AF.Ln, [w.b_e[ew]], [w.b_sp[ew]], bias=1.0)

            def s_tri(hi, i):
                rk, mb, c0, dm = step_geom(i)
                ew = hi * 2 + i % 2
                Bd.mm(ps[PC[hi]][:, c0:512], ac.tri[:, 0, :], w.sp[ew][:, c0:512], False, False, [ac.b_tri, w.b_sp[ew]], [psb[PC[hi]]])

            def s_ec(hi, i):
                rk, mb, c0, dm = step_geom(i)
                Bd.act(w.ec[hi][:, c0:512], ps[PC[hi]][:, c0:512], AF.Exp, [psb[PC[hi]]], [w.b_ec[hi]], scale=-1.0)

            def s_a(hi, i):
                rk, mb, c0, dm = step_geom(i)
                ew = hi * 2 + i % 2
                Bd.tt(w.a[hi][:, c0:512], w.e[ew][:, c0:512], w.ec[hi][:, c0:512], ALU.mult, [w.b_e[ew], w.b_ec[hi]], [w.b_a[hi]])

            def s_co(hi, i):
                rk, mb, c0, dm = step_geom(i)
                ew = hi * 2 + i % 2
                pc, po = PC[hi], PO[hi]
                Bd.mm(ps[pc][:, c0:512], ac.tri[:, 1, :], w.sp[ew][:, c0:512], False, i == nsteps - 1, [ac.b_tri, w.b_sp[ew]], [psb[pc]])
                Bd.mm(ps[po][0:64, c0:512], w.vbuf[hi][:, rk, mb, 0:64], w.a[hi][:, c0:512], False, i == nsteps - 1, [w.b_v[hi], w.b_a[hi]], [psb[po]])
            for hi in range(2):
                s_z(hi, 0)
            for hi in range(2):
                s_e(hi, 0)
            for hi in range(2):
                s_sp(hi, 0)
            for i in range(nsteps):
                nxt_ = i + 1 < nsteps
                for hi in range(2):
                    s_tri(hi, i)
                if nxt_:
                    for hi in range(2):
                        s_z(hi, i + 1)
                for hi in range(2):
                    s_ec(hi, i)
                if nxt_:
                    for hi in range(2):
                        s_e(hi, i + 1)
                for hi in range(2):
                    s_a(hi, i)
                if nxt_:
                    for hi in range(2):
                        s_sp(hi, i + 1)
                for hi in range(2):
                    s_co(hi, i)
            for hi, h in enumerate(hs):
                Bd.act(w.og[:, h, :], ps[PO[hi]][0:64, :], AF.Copy, [psb[PO[hi]]], [w.b_og[h]])
        group_norm(Bd, w, ac, 12, 4, 256.0, G)

        for j in range(8):
            s = j % 2
            Bd.load(w.wo[s][:, :, :], di["wout"][j, :, :, :], w.b_wo[s], eng="pool")
            pb = j % 2
            for hh in range(16):
                Bd.mm(ps[pb][:, :], w.wo[s][:, hh, :], w.mix[:, hh, :], hh == 0, hh == 15, [w.b_wo[s], w.b_mix[hh]], [psb[pb]])
            Bd.stt(x[:, j, t0:t0 + 512], ps[pb][:, :], m.gate[:, 1, j:j + 1], x[:, j, t0:t0 + 512], ALU.mult, ALU.add, [psb[pb], m.b, bx], [bx])


DEBUG = False


def group_norm(Bd, w, ac, h0, nh, width, G=0):
    ps, psb = Bd.ps, Bd.psb
    if DEBUG:
        for i in range(nh):
            Bd.store(Bd.outs["dbg_og"][G, :, h0 + i, :], w.og[:, i, :], w.b_og[i], eng="sp", final=True)
    for i in range(nh):
        Bd.act(w.gsq[:, :], w.og[:, i, :], AF.Square, [w.b_og[i]], [w.b_gsq])
        Bd.mm(ps[7][0:64, :], Bd.ones_bf[0:64, 0:64], w.gsq[:, :], i == 0, i == nh - 1, [w.b_gsq, Bd.b_ones], [psb[7]])
    Bd.rstd(ps[7][0:64, :], width, w.grs[:, :], w.grs[:, :], [psb[7]], w.b_grs, w.b_grs)
    for i in range(nh):
        Bd.stt(w.mix[:, h0 + i, :], w.og[:, i, :], ac.onorm[:, h0 + i:h0 + i + 1], w.grs[:, :], ALU.mult, ALU.mult, [w.b_og[i], ac.b_onorm, w.b_grs], [w.b_mix[h0 + i]])


LAYER_IN = dict(wmod=[129, 8, 9216], bmod=[128, 72], ng=[128, 3, 8], wgu1=[NF + 1, 128, 8, 2, 128], wdn1=[9, 128, NF, 128],
                win=[129, 8, 1824], winv=[129, 8, 384], qan=[128, 2], kvan=[128, 1], wuq=[128, 2, 576], wukvk=[128, 6, 64],
                wukvv=[128, 384], hnorms=[96, 4], wout=[9, 64, 16, 128], onorm=[64, 16], sinks=[1, 6],
                wgu2=[NF + 1, 128, 8, 2, 128], wdn2=[9, 128, NF, 128])
RG = [[0, 1, 2, 3], [4, 5, 6, 7]]


def build_F():
    Bd = Builder()
    S = Bd.S
    nc = Bd.nc
    xT = Bd.inp("xT", [D, T])
    xoT = Bd.outp("xoT", [D, T])
    cT = Bd.inp("cT", [128, 8])
    shared_in = dict(pos=Bd.inp("pos", [1, T], I32), freq=Bd.inp("freq", [96, 1]), rot=Bd.inp("rotT", [96, 96]),
                     tricomp=Bd.inp("tricomp", [128, 2, 128]), mincl=Bd.inp("mincl", [128, 4, 128]),
                     mstrict=Bd.inp("mstrict", [128, 4, 128]), relb=Bd.inp("relb", [1, 192]),
                     posrow=Bd.inp("posrow", [1, 128], I32), poscol=Bd.inp("poscol", [128, 2], I32),
                     wsel=Bd.inp("wsel", [128, 4]))
    L = [{k: Bd.inp(f"{k}_{l}", shp) for k, shp in LAYER_IN.items()} for l in range(2)]

    x = Bd.sb("x", [128, 8, T], F32); bx = Buf("x")
    for kc in range(8):
        Bd.load(x[:, kc, :], xT[kc * 128:(kc + 1) * 128, :], bx)
    ma = Bd.mod_alloc()
    zt = Bd.sb("zt", [128, 1105], BF16); b_zt = Buf()
    Bd.memset(zt[:, :], 0.0, [b_zt])
    mk0 = Bd.mark()
    bias_cache = (nc.dram_tensor("bias_scr", [128, 2 * 6 * 128], F32), Buf("bias_scr"))
    for l in range(2):
        W = L[l]
        def dt2(name, rows, cols):
            return nc.dram_tensor(f"{name}_{l}", [rows, cols], BF16)
        q_km = dt2("sq_m", 6 * 96, T); q_qs = dt2("sq_s", 6 * 64, T); q_qc = dt2("sq_c", 4 * 64, T)
        UR = 192
        units_s, units_g = [], []

        units_b = []

        def unit():
            i = len(units_s)
            units_s.append(dt2(f"su{i}", UR, T))
            units_g.append(dt2(f"gu{i}", 4 * UR, T))
            units_b.append(Buf(f"gu{i}"))
            return units_s[-1].ap(), units_g[-1].ap().rearrange("(r n) t -> r n t", r=4)
        o_km, g_km, o_kc, g_kc, o_vm, g_vm, o_vc, g_vc = [], [], [], [], [], [], [], []
        for i in range(3):
            su, gu = unit()
            o_km.append((su[0:192, :].rearrange("(h p) t -> h p t", h=2), 2 * i, 2))
            g_km += [(gu[:, hh * 96:(hh + 1) * 96, :], units_b[-1]) for hh in range(2)]
        for i in range(2):
            su, gu = unit()
            o_kc.append((su[0:128, :].rearrange("(h p) t -> h p t", h=2), 2 * i, 2))
            g_kc += [(gu[:, hh * 64:(hh + 1) * 64, :], units_b[-1]) for hh in range(2)]
        for (ol, gl, n) in ((o_vm, g_vm, 3), (o_vc, g_vc, 2)):
            for i in range(n):
                su, gu = unit()
                ol.append((su[0:130, :].rearrange("n t -> (n t)").rearrange("(h p m d) -> h p m d", h=2, p=128, d=65), 2 * i, 2))
                gv = gu[:, 0:130, :].rearrange("r n t -> r (n t)").rearrange("r (h p m d) -> r h p m d", h=2, p=128, d=65)
                gl += [(gv[:, hh, :, :, :], units_b[-1]) for hh in range(2)]
        su, gu = unit()
        o_ks = su[0:136, :].rearrange("n t -> (n t)").rearrange("(h p u) -> h p u", h=2, p=64)
        g_ks = gu[:, 0:136, :].rearrange("r n t -> r (n t)").rearrange("r (h p u) -> r h p u", h=2, p=64)
        b_gks = units_b[-1]
        su, gu = unit()
        o_vs = su[0:139, :].rearrange("n t -> (n t)")[0:2 * 128 * 17 * 65].rearrange("(h p m d) -> h p m d", h=2, p=128, d=65)
        g_vs = gu[:, 0:139, :].rearrange("r n t -> r (n t)")[:, 0:2 * 128 * 17 * 65].rearrange("r (h p m d) -> r h p m d", h=2, p=128, d=65)
        b_gvs = units_b[-1]
        b_src = Buf("src", multi=True)
        b_q = Buf("qscr", multi=True)
        b_g = Buf("gath")
        Bd.store(o_ks.rearrange("h p u -> p h u")[:, :, 0:128], zt[0:64, 0:256].rearrange("p (h u) -> p h u", h=2), b_zt, dstbuf=b_src)
        Bd.store(o_vs.rearrange("h p m d -> p h m d")[:, :, 0, :], zt[:, 0:130].rearrange("p (h d) -> p h d", h=2), b_zt, dstbuf=b_src)
        nw = Bd.norm_work()
        fw = Bd.ffn_work(nw)
        stage = [fw.act[:, 0:8, :].bitcast(F32), fw.act[:, 8:16, :].bitcast(F32)]
        modsb, b_modsb, ng, b_ng = emit_mod(Bd, cT, W["wmod"], W["bmod"], W["ng"], stage, [Buf(), Buf()], ma)
        m = Bd.derive_mod(modsb, b_modsb, ng, b_ng, ma)
        S.barrier()
        Bd.ffn(x, bx, W["wgu1"], W["wdn1"], m, 0, fw)
        S.barrier()
        Bd.release(mk0)
        nw = Bd.norm_work()
        fw2 = Ctx(); fw2.nw = nw
        di = dict(win=W["win"], winv=W["winv"], qan=W["qan"], kvan=W["kvan"], wuq=W["wuq"], wukvk=W["wukvk"], wukvv=W["wukvv"],
                  hn=W["hnorms"], pos=shared_in["pos"], freq=shared_in["freq"], rot=shared_in["rot"])
        do = dict(qm=q_km.ap().rearrange("(h p) t -> h p t", h=6), qs=q_qs.ap().rearrange("(h p) t -> h p t", h=6),
                  qc=q_qc.ap().rearrange("(h p) t -> h p t", h=4),
                  km=o_km, kc=o_kc, ks=o_ks, ks_off=128, vm=o_vm, vc=o_vc, vs=o_vs, vs_off=1,
                  qm_buf=b_q, qs_buf=b_q, qc_buf=b_q, km_buf=b_src, kc_buf=b_src, ks_buf=b_src, vm_buf=b_src, vc_buf=b_src, vs_buf=b_src)
        emit_proj(Bd, x, bx, m, fw2, di, do, final=False)
        S.barrier()
        Bd.release(mk0)
        da = dict(shared_in)
        da.update(onorm=W["onorm"], sinks=W["sinks"], wout=W["wout"],
                  qm=do["qm"], qs=do["qs"], qc=do["qc"], b_q=b_q, b_src=b_src, b_ks=b_gks, b_vs=b_gvs,
                  kmf=g_km, vmf=g_vm, kcf=g_kc, vcf=g_vc, ksf=g_ks, vsf=g_vs, ks_own=o_ks, vs_own=o_vs)
        w = attn_work(Bd)
        ac = attn_consts(Bd, da, w, bias_cache, first=(l == 0))
        for ui in (0, 5, 1, 6, 2, 7, 10, 11, 3, 8, 4, 9):
            S.coll("pool", (lambda a_, b_: lambda h: h.collective_compute("AllGather", ALU.bypass, replica_groups=RG, ins=[a_.ap()], outs=[b_.ap()]))(units_s[ui], units_g[ui]),
                   [b_src], [units_b[ui]], lambda h: h.memset(zt[0:1, 0:8], 0.0))
        emit_attention(Bd, x, bx, m, ac, w, da)
        S.barrier()
        Bd.release(mk0)
        nw = Bd.norm_work()
        fw = Bd.ffn_work(nw)
        Bd.ffn(x, bx, W["wgu2"], W["wdn2"], m, 2, fw)
        S.barrier()
        Bd.release(mk0)
    for kc in range(8):
        Bd.store(xoT[kc * 128:(kc + 1) * 128, :], x[:, kc, :], bx, eng="sp", final=True)
    S.wait_all("sp", Bd.finals)
    print("F sbuf peak", Bd.peak, {e: len(S.q[e]) for e in ENGS})
    S.emit()
    return Bd


_CACHE = {}


def _get(name, fn):
    if name not in _CACHE:
        _CACHE[name] = fn()
    return _CACHE[name]


def _core_tokens(r):
    idx = (np.arange(NBLK)[:, None] * 4 + r) * 128 + np.arange(128)[None, :]
    return idx.reshape(-1)


def _freq_rot():
    half = 16
    fr = (np.float32(10000.0) ** (-np.arange(half, dtype=np.float32) / np.float32(half))).astype(np.float32)
    freq = np.zeros((96, 1), np.float32)
    freq[64:80, 0] = fr
    freq[80:96, 0] = fr
    rotT = np.zeros((96, 96), np.float32)
    for i in range(16):
        rotT[80 + i, 64 + i] = -1.0
        rotT[64 + i, 80 + i] = 1.0
    return freq, rotT


def _ffn_layout(wgu_l, wdn_l):
    f = np.ascontiguousarray
    wgu = wgu_l.reshape(8, 128, 2, NF, 128)
    wdn = wdn_l.reshape(NF, 128, 8, 128)
    return f(wgu.transpose(3, 1, 0, 2, 4)), f(wdn.transpose(2, 1, 0, 3))


def layer_inputs(inp, l):
    f = np.ascontiguousarray
    d = {}
    d["wmod"] = f(inp["w_mod"][l].reshape(8, 128, 9216).transpose(1, 0, 2))
    d["bmod"] = f(inp["b_mod"][l].reshape(72, 128).T)
    d["ng"] = f(inp["norm_g"][l].reshape(3, 8, 128).transpose(2, 0, 1))
    d["wgu1"], d["wdn1"] = _ffn_layout(inp["w_ffn1_gu"][l], inp["w_ffn1_down"][l])
    d["wgu2"], d["wdn2"] = _ffn_layout(inp["w_ffn2_gu"][l], inp["w_ffn2_down"][l])
    win = inp["w_in"][l]
    d["win"] = f(win.reshape(8, 128, 1824).transpose(1, 0, 2))
    winv = np.concatenate([win[:, 928:1056], win[:, 1568:1824]], axis=1)
    d["winv"] = f(winv.reshape(8, 128, 384).transpose(1, 0, 2))
    d["qan"] = f(inp["q_a_norm"][l].reshape(2, 128).T)
    d["kvan"] = f(inp["kv_a_norm"][l].reshape(128, 1))
    d["wuq"] = f(inp["w_uq"][l].reshape(2, 128, 576).transpose(1, 0, 2))
    wukv = inp["w_ukv"][l].reshape(128, 6, 128)
    d["wukvk"] = f(wukv[:, :, 0:64])
    d["wukvv"] = f(wukv[:, :, 64:128].reshape(128, 384))
    hn = np.zeros((96, 4), np.float32)
    hn[:, 0] = inp["mla_q_norm"][l]
    hn[:, 1] = inp["mla_k_norm"][l]
    hn[0:64, 2] = inp["swa_q_norm"][l]
    hn[0:64, 3] = inp["swa_k_norm"][l]
    d["hnorms"] = hn
    d["wout"] = f(inp["w_out"][l].reshape(16, 64, 8, 128).transpose(2, 1, 0, 3))
    d["onorm"] = f(inp["out_norm"][l].reshape(16, 64).T)
    d["sinks"] = f(inp["sinks"][l].reshape(1, 6))
    return d


def _masks(r):
    j = np.arange(128)[:, None]
    q = np.arange(128)[None, :]
    mi = np.zeros((128, 4, 128), np.float32)
    ms = np.zeros((128, 4, 128), np.float32)
    for d in range(4):
        if d < r:
            mi[:, d, :] = 1.0
            ms[:, d, :] = 1.0
        elif d == r:
            mi[:, d, :] = (j <= q)
            ms[:, d, :] = (j < q)
    tc = np.zeros((128, 2, 128), np.float32)
    tc[:, 0, :] = (j >= q)
    tc[:, 1, :] = (j < q)
    return mi, ms, tc


CORES = [(b, r) for b in range(2) for r in range(4)]
_PAD_KEYS = ("wmod", "wgu1", "wdn1", "wgu2", "wdn2", "win", "winv", "wout")


def _pad(a, ci):
    return np.concatenate([a, np.full((1,) + a.shape[1:], float(ci), a.dtype)], axis=0)


def kernel(**inp):
    inp = {k: np.asarray(v) for k, v in inp.items()}
    prog = _get("F", build_F)
    x = inp["x"]
    pos = inp["positions"]
    freq, rotT = _freq_rot()
    lay = [layer_inputs(inp, l) for l in range(2)]
    in_maps = []
    for ci, (b, r) in enumerate(CORES):
        mi, ms, tc = _masks(r)
        wsel = np.zeros((128, 4), np.float32)
        wsel[:, (r + 3) % 4] = 1.0
        dct = dict(xT=np.ascontiguousarray(x[b][_core_tokens(r)].T),
                   cT=np.ascontiguousarray(inp["c"][b].reshape(8, 128).T),
                   pos=np.ascontiguousarray(pos[b][_core_tokens(r)].reshape(1, T).astype(np.int32)),
                   freq=freq, rotT=rotT, tricomp=tc, mincl=mi, mstrict=ms,
                   relb=np.ascontiguousarray(inp["rel_bias"].reshape(1, 192)),
                   posrow=np.ascontiguousarray(pos[b][(4 + r) * 128:(5 + r) * 128].reshape(1, 128).astype(np.int32)),
                   poscol=np.ascontiguousarray(np.stack([pos[b][(3 + r) * 128:(4 + r) * 128], pos[b][(4 + r) * 128:(5 + r) * 128]], axis=1).astype(np.int32)),
                   wsel=wsel)
        for l in range(2):
            for k, a in lay[l].items():
                dct[f"{k}_{l}"] = _pad(a, ci) if k in _PAD_KEYS else a
        in_maps.append(dct)
    res = run_bass_kernel_spmd(prog.nc, in_maps, core_ids=list(range(NCORES)))
    out = np.zeros((2, 8192, 1024), np.float32)
    for ci, (b, r) in enumerate(CORES):
        out[b][_core_tokens(r)] = res.results[ci]["xoT"].T
    return out
```

```python
import math
import numpy as np
import ml_dtypes
import concourse.bass as bass
import concourse.mybir as mybir
from concourse.bass_utils import run_bass_kernel_spmd

F32 = mybir.dt.float32
BF16 = mybir.dt.bfloat16
I32 = mybir.dt.int32
AF = mybir.ActivationFunctionType
ALU = mybir.AluOpType

T = 2048
NBLK = 16
D = 1024
DFF = 2816
NF = 22
EPS = 1e-6
NCORES = 8
TWO_PI = 2.0 * math.pi

ENGS = ("pe", "act", "dve", "pool", "sp")
EPOCH = 30000


class Buf:
    __slots__ = ("name", "writers", "readers", "sem", "cnt", "multi")

    def __init__(self, name="", multi=False):
        self.name = name
        self.writers = []
        self.readers = []
        self.sem = None
        self.cnt = 0
        self.multi = multi


class Op:
    __slots__ = ("eng", "fn", "deps", "signal", "k", "dma_tok", "is_dma")

    def __init__(self, eng, fn):
        self.eng = eng
        self.fn = fn
        self.deps = []
        self.signal = False
        self.k = -1
        self.dma_tok = None
        self.is_dma = False


class Sched:
    def __init__(self, nc, same_engine_raw=True):
        self.nc = nc
        self.q = {e: [] for e in ENGS}
        self.same_engine_raw = same_engine_raw
        self.nsem = 0
        self.dmas = []
        self.csem = None
        self.ccnt = 0
        self.sem_pool = []
        self.sem_bufs = []

    def new_sem(self, name):
        self.nsem += 1
        return self.nc.alloc_semaphore(f"{name}_{self.nsem}")

    def _dep(self, o, p, raw):
        if p is None or p is o:
            return
        if (not p.is_dma) and (not o.is_dma) and p.eng == o.eng:
            if not (raw and self.same_engine_raw and o.eng != "pe"):
                return
        if p not in o.deps:
            o.deps.append(p)
            if not p.is_dma:
                p.signal = True

    def _track(self, o, reads, writes):
        for b in reads:
            for w in b.writers:
                self._dep(o, w, True)
        for b in writes:
            if not (b.multi and o.is_dma):
                for w in b.writers:
                    self._dep(o, w, False)
            for r in b.readers:
                self._dep(o, r, False)
        for b in reads:
            b.readers.append(o)
        for b in writes:
            if b.multi and o.is_dma and not b.readers:
                b.writers.append(o)
            else:
                b.writers = [o]
            b.readers = []

    def op(self, eng, fn, reads=(), writes=()):
        o = Op(eng, fn)
        self._track(o, reads, writes)
        self.q[eng].append(o)
        return o

    def dma(self, eng, fn, reads, writes, semb=None):
        o = Op(eng, fn)
        o.is_dma = True
        self._track(o, reads, writes)
        semb = semb if semb is not None else writes[0]
        if semb.sem is None:
            if self.sem_pool:
                semb.sem, semb.cnt = self.sem_pool.pop()
            else:
                semb.sem = self.new_sem("d")
            self.sem_bufs.append(semb)
        semb.cnt += 16
        o.dma_tok = (semb.sem, semb.cnt)
        self.q[eng].append(o)
        self.dmas.append(o)
        return o

    def coll(self, eng, fn, reads, writes, dummy_fn):
        o = Op(eng, fn)
        o.is_dma = True
        self._track(o, reads, [])
        if self.csem is None:
            self.csem = self.new_sem("c")
        self.ccnt += 1
        o.dma_tok = (self.csem, self.ccnt)
        o.k = -2
        self.q[eng].append(o)
        return self.op(eng, dummy_fn, [], writes)

    def wait_all(self, eng, ops):
        o = Op(eng, None)
        for p in ops:
            self._dep(o, p, True)
        self.q[eng].append(o)

    def barrier(self):
        lasts = []
        for e in ENGS:
            for o in reversed(self.q[e]):
                if not o.is_dma and o.fn is not None:
                    lasts.append(o)
                    break
        pend = list(self.dmas)
        self.dmas = []
        for b in self.sem_bufs:
            self.sem_pool.append((b.sem, b.cnt))
            b.sem = None
        self.sem_bufs = []
        for e in ENGS:
            o = Op(e, None)
            for p in lasts:
                if p.eng != e:
                    o.deps.append(p)
                    p.signal = True
            for p in pend:
                o.deps.append(p)
            self.q[e].append(o)

    def emit(self):
        nc = self.nc
        esems = {}
        for e in ENGS:
            k = 0
            for o in self.q[e]:
                if o.signal and not o.is_dma:
                    o.k = k
                    k += 1
            esems[e] = [self.new_sem(e) for _ in range(k // EPOCH + 1)]

        def tok(p):
            if p.is_dma:
                return p.dma_tok
            return (esems[p.eng][p.k // EPOCH], p.k % EPOCH + 1)

        def run(e, h):
            waited = {}
            for o in self.q[e]:
                for p in o.deps:
                    s, v = tok(p)
                    key = id(s)
                    if waited.get(key, 0) < v:
                        h.wait_ge(s, v)
                        waited[key] = v
                if o.fn is None:
                    continue
                ins = o.fn(h)
                if o.is_dma and o.k == -2:
                    ins.then_inc(o.dma_tok[0])
                    h.wait_ge(o.dma_tok[0], o.dma_tok[1])
                elif o.is_dma:
                    ins.then_inc(o.dma_tok[0], 16)
                elif o.signal:
                    s, v = tok(o)
                    ins.then_inc(s, 1)

        with nc.Block() as block:
            @block.tensor
            def _(h):
                run("pe", h)

            @block.scalar
            def _(h):
                run("act", h)

            @block.vector
            def _(h):
                run("dve", h)

            @block.gpsimd
            def _(h):
                run("pool", h)

            @block.sync
            def _(h):
                run("sp", h)


def _bucket_thresholds():
    n = np.arange(0, 128)
    nf = np.maximum(n, 1).astype(np.float32)
    large = 16 + (np.log(nf / np.float32(16)) / np.float32(math.log(128 / 16)) * np.float32(16)).astype(np.int32)
    large = np.minimum(large, 31)
    b = np.where(n < 16, n, large)
    lo = []
    for bb in range(1, 32):
        idx = np.nonzero(b >= bb)[0]
        lo.append(int(idx[0]) if len(idx) else 1 << 20)
    return lo


class Ctx:
    pass


class Builder:
    def __init__(self):
        self.nc = bass.Bass("TRN2", target_bir_lowering=False)
        self.S = Sched(self.nc)
        self.ins = {}
        self.outs = {}
        self.finals = []
        self.uid = 0
        self.off = self.SB_BASE
        self.peak = self.off
        self.ps = [self.nc.alloc_psum_tensor(f"psb{i}", [128, 512], F32) for i in range(8)]
        self.psb = [Buf(f"ps{i}") for i in range(8)]
        self.ones_bf = self.sb("ones", [128, 128], BF16)
        self.b_ones = Buf("ones")
        self.memset(self.ones_bf[:, :], 1.0, [self.b_ones])

    def inp(self, name, shape, dt=F32):
        t = self.nc.dram_tensor(name, list(shape), dt, kind="ExternalInput")
        self.ins[name] = t
        return t

    def outp(self, name, shape, dt=F32):
        t = self.nc.dram_tensor(name, list(shape), dt, kind="ExternalOutput")
        self.outs[name] = t
        return t

    SB_BASE = 16512
    SB_TOP = 229344

    def sb(self, name, shape, dt=F32):
        self.uid += 1
        sz = int(np.prod(shape[1:])) * (4 if dt in (F32, I32) else 2)
        sz = (sz + 31) // 32 * 32
        assert self.off + sz <= self.SB_TOP, f"SBUF overflow allocating {name} {shape}: off={self.off} sz={sz}"
        t = self.nc.alloc_sbuf_tensor_at(f"{name}_{self.uid}", list(shape), dt, offset=self.off)
        self.off += sz
        self.peak = max(self.peak, self.off)
        return t

    def mark(self):
        return self.off

    def release(self, mk):
        self.off = mk

    def mm(self, out, lhsT, rhs, start, stop, reads, writes):
        return self.S.op("pe", lambda h: h.matmul(out, lhsT=lhsT, rhs=rhs, start=start, stop=stop), reads, writes)

    def act(self, out, in_, func, reads, writes, scale=1.0, bias=0.0):
        return self.S.op("act", lambda h: h.activation(out=out, in_=in_, func=func, scale=scale, bias=bias), reads, writes)

    def tt(self, out, in0, in1, op, reads, writes, eng="dve"):
        return self.S.op(eng, lambda h: h.tensor_tensor(out=out, in0=in0, in1=in1, op=op), reads, writes)

    def ts(self, out, in0, s1, s2, op0, op1, reads, writes, eng="dve"):
        if s2 is None:
            return self.S.op(eng, lambda h: h.tensor_scalar(out=out, in0=in0, scalar1=s1, scalar2=None, op0=op0), reads, writes)
        return self.S.op(eng, lambda h: h.tensor_scalar(out=out, in0=in0, scalar1=s1, scalar2=s2, op0=op0, op1=op1), reads, writes)

    def stt(self, out, in0, scalar, in1, op0, op1, reads, writes, eng="dve"):
        return self.S.op(eng, lambda h: h.scalar_tensor_tensor(out=out, in0=in0, scalar=scalar, in1=in1, op0=op0, op1=op1), reads, writes)

    def copy(self, out, in_, reads, writes, eng="dve"):
        return self.S.op(eng, lambda h: h.tensor_copy(out=out, in_=in_), reads, writes)

    def recip(self, out, in_, reads, writes):
        return self.S.op("dve", lambda h: h.reciprocal(out=out, in_=in_), reads, writes)

    def memset(self, ap, val, writes, eng="pool"):
        return self.S.op(eng, lambda h: h.memset(ap, val), [], writes)

    def load(self, dst, src, buf, eng="sp", reads=()):
        return self.S.dma(eng, lambda h: h.dma_start(out=dst, in_=src), list(reads), [buf])

    def store(self, dst, src, srcbuf, eng="pool", final=False, dstbuf=None):
        o = self.S.dma(eng, lambda h: h.dma_start(out=dst, in_=src), [srcbuf],
                       [dstbuf] if dstbuf is not None else [Buf()], semb=srcbuf)
        if final:
            self.finals.append(o)
        return o

    def rstd(self, ss, n, out, tmp, reads, tmpbuf, outbuf):
        self.act(tmp, ss, AF.Ln, reads, [tmpbuf], scale=1.0 / n, bias=EPS)
        self.act(out, tmp, AF.Exp, [tmpbuf], [outbuf], scale=-0.5)

    def norm_work(self):
        w = Ctx()
        w.sq = self.sb("nsq", [128, 8, 512], BF16); w.b_sq = Buf()
        w.tmp = self.sb("ntmp", [128, 512], F32); w.b_tmp = Buf()
        w.rstd = self.sb("nrstd", [128, 512], F32); w.b_rstd = Buf()
        w.xn = self.sb("nxn", [128, 8, 512], F32); w.b_xn = Buf()
        return w

    def modnorm_tile(self, x, bx, t0, A, Bv, bmod, hdst, bh, w, pb=7):
        self.act(w.sq[:, :, :], x[:, :, t0:t0 + 512], AF.Square, [bx], [w.b_sq])
        for kc in range(8):
            self.mm(self.ps[pb][:, :], self.ones_bf[:, :], w.sq[:, kc, :], kc == 0, kc == 7, [w.b_sq, self.b_ones], [self.psb[pb]])
        self.rstd(self.ps[pb][:, :], float(D), w.rstd[:, :], w.tmp[:, :], [self.psb[pb]], w.b_tmp, w.b_rstd)
        self.tt(w.xn[:, :, :], x[:, :, t0:t0 + 512], w.rstd[:, None, :].to_broadcast([128, 8, 512]), ALU.mult, [bx, w.b_rstd], [w.b_xn])
        for kc in range(8):
            self.act(hdst[:, kc, :], w.xn[:, kc, :], AF.Identity, [w.b_xn, bmod], [bh], scale=A[:, kc:kc + 1], bias=Bv[:, kc:kc + 1])

    def mod_alloc(self):
        a = Ctx()
        a.cond = self.sb("cond", [128, 8], F32)
        a.bm = self.sb("bmod", [128, 72], F32)
        a.ng = self.sb("ng", [128, 3, 8], F32)
        a.modsb = self.sb("mod", [128, 72], F32)
        a.A = self.sb("modA", [128, 3, 8], F32)
        a.gate = self.sb("modG", [128, 3, 8], F32)
        return a

    def derive_mod(self, mod, b_mod, ng, b_ng, a):
        m = Ctx()
        m.A = a.A
        m.gate = a.gate
        m.mod = mod
        m.b = Buf("modder")
        for i in range(3):
            self.stt(m.A[:, i, :], mod[:, (3 * i + 1) * 8:(3 * i + 2) * 8], 1.0, ng[:, i, :], ALU.add, ALU.mult, [b_mod, b_ng], [m.b])
            self.ts(m.gate[:, i, :], mod[:, (3 * i + 2) * 8:(3 * i + 3) * 8], 1.0 if i == 1 else 0.5, None, ALU.mult, None, [b_mod], [m.b])
        m.B = lambda i: mod[:, (3 * i) * 8:(3 * i + 1) * 8]
        return m

    def ffn_work(self, nw):
        fw = Ctx()
        fw.h = self.sb("ffh", [128, 2, 8, 512], BF16)
        fw.b_h = [Buf(), Buf()]
        fw.act = self.sb("ffact", [128, NF, 1024], BF16)
        fw.b_act = [[Buf() for t in range(2)] for f in range(NF)]
        fw.wg = [self.sb(f"wg{i}", [128, 8, 2, 128], BF16) for i in range(3)]
        fw.b_wg = [Buf() for i in range(3)]
        fw.wd = [self.sb(f"wd{i}", [128, NF, 128], BF16) for i in range(2)]
        fw.b_wd = [Buf() for i in range(2)]
        fw.sg = [self.sb(f"sg{i}", [128, 512], F32) for i in range(2)]
        fw.b_sg = [Buf() for i in range(2)]
        fw.nw = nw
        return fw

    def ffn(self, x, bx, wgu_d, wdn_d, m, i, fw):
        A = m.A[:, i, :]
        Bv = m.B(i)
        gate = m.gate[:, i, :]
        for half in range(2):
            for tt in range(2):
                self.modnorm_tile(x, bx, half * 1024 + tt * 512, A, Bv, m.b, fw.h[:, tt, :, :], fw.b_h[tt], fw.nw)
            for f in range(NF):
                s = f % 3
                self.load(fw.wg[s][:, :, :, :], wgu_d[f, :, :, :, :], fw.b_wg[s], eng="pool")
                for tt in range(2):
                    par = (f * 2 + tt) % 2
                    pg, pu = par * 2, par * 2 + 1
                    for gu, pb in ((0, pg), (1, pu)):
                        for kc in range(8):
                            self.mm(self.ps[pb][:, :], fw.wg[s][:, kc, gu, :], fw.h[:, tt, kc, :], kc == 0, kc == 7, [fw.b_wg[s], fw.b_h[tt]], [self.psb[pb]])
                    self.act(fw.sg[par][:, :], self.ps[pg][:, :], AF.Silu, [self.psb[pg]], [fw.b_sg[par]])
                    self.tt(fw.act[:, f, tt * 512:(tt + 1) * 512], fw.sg[par][:, :], self.ps[pu][:, :], ALU.mult, [fw.b_sg[par], self.psb[pu]], [fw.b_act[f][tt]])
            for j in range(8):
                s = j % 2
                self.load(fw.wd[s][:, :, :], wdn_d[j, :, :, :], fw.b_wd[s], eng="pool")
                for tt in range(2):
                    pb = 4 + (j * 2 + tt) % 2
                    t0 = half * 1024 + tt * 512
                    for f in range(NF):
                        self.mm(self.ps[pb][:, :], fw.wd[s][:, f, :], fw.act[:, f, tt * 512:(tt + 1) * 512], f == 0, f == NF - 1, [fw.b_wd[s], fw.b_act[f][tt]], [self.psb[pb]])
                    self.stt(x[:, j, t0:t0 + 512], self.ps[pb][:, :], gate[:, j:j + 1], x[:, j, t0:t0 + 512], ALU.mult, ALU.add, [self.psb[pb], m.b, bx], [bx])


def emit_mod(Bd, cT, wmod, bmod, ngd, stage_bf, stage_bufs, a):
    cond = a.cond; b_cond = Buf()
    Bd.load(cond[:, :], cT[:, :], b_cond)
    Bd.act(cond[:, :], cond[:, :], AF.Silu, [b_cond], [b_cond])
    bm = a.bm; b_bm = Buf()
    Bd.load(bm[:, :], bmod[:, :], b_bm)
    ng = a.ng; b_ng = Buf()
    Bd.load(ng[:, :, :], ngd[:, :, :], b_ng)
    modsb = a.modsb; b_modsb = Buf()
    pm = 6
    for j9 in range(9):
        for hh in range(2):
            s = (j9 * 2 + hh) % 2
            stage = stage_bf[s]
            col0 = j9 * 1024 + hh * 512
            Bd.load(stage[:, :, :], wmod[0:128, :, col0:col0 + 512], stage_bufs[s])
            for jc in range(4):
                oc = j9 * 8 + hh * 4 + jc
                for kc in range(8):
                    Bd.mm(Bd.ps[pm][:, oc:oc + 1], stage[:, kc, jc * 128:(jc + 1) * 128], cond[:, kc:kc + 1], kc == 0, kc == 7, [stage_bufs[s], b_cond], [Bd.psb[pm]])
    Bd.tt(modsb[:, :], Bd.ps[pm][:, 0:72], bm[:, :], ALU.add, [Bd.psb[pm], b_bm], [b_modsb])
    return modsb, b_modsb, ng, b_ng


def emit_proj(Bd, x, bx, m, fw, di, do, final):
    S = Bd.S
    ps, psb = Bd.ps, Bd.psb
    ones = Bd.ones_bf
    b_ones = Bd.b_ones

    def wload(name, shape, dt, eng="pool"):
        t = Bd.sb(name, shape, dt); b = Buf(name)
        src = di[name]
        Bd.load(t[tuple(slice(None) for _ in shape)], src[(slice(0, shape[0]),) + tuple(slice(None) for _ in shape[1:])], b, eng=eng)
        return t, b
    win, b_win = wload("win", [128, 8, 1824], BF16)
    winv, b_winv = wload("winv", [128, 8, 384], BF16)
    wuq, b_wuq = wload("wuq", [128, 2, 576], BF16)
    wukvk, b_wukvk = wload("wukvk", [128, 6, 64], BF16)
    wukvv, b_wukvv = wload("wukvv", [128, 384], BF16)
    qan, b_qan = wload("qan", [128, 2], F32, "sp")
    kvan, b_kvan = wload("kvan", [128, 1], F32, "sp")
    hn, b_hn = wload("hn", [96, 4], F32, "sp")
    rot, b_rot = wload("rot", [96, 96], F32, "sp")
    freq, b_freq = wload("freq", [96, 1], F32, "sp")

    Ct = Bd.sb("ropeC", [96, T], F32); b_C = Buf()
    St = Bd.sb("ropeS", [96, T], F32); b_S = Buf()
    mk_rope = Bd.mark()
    ang = Bd.sb("ang", [96, T], F32); b_ang = Buf()
    rt = Bd.sb("rt", [96, T], F32); b_rt = Buf()
    ki = Bd.sb("ki", [96, T], I32); b_ki = Buf()
    posi = ki; b_posi = b_ki
    Bd.load(posi[:, :], di["pos"][0:1, :].partition_broadcast(96), b_posi)
    C1 = 6.28125
    C2 = TWO_PI - C1
    Bd.copy(ang[:, :], posi[:, :], [b_posi], [b_ang])
    Bd.ts(ang[:, :], ang[:, :], freq[:, 0:1], None, ALU.mult, None, [b_ang, b_freq], [b_ang])
    for (dst, bd, shift) in ((St, b_S, 0.0), (Ct, b_C, math.pi / 2)):
        Bd.ts(rt[:, :], ang[:, :], shift, 1.0 / TWO_PI, ALU.add, ALU.mult, [b_ang], [b_rt])
        Bd.copy(ki[:, :], rt[:, :], [b_rt], [b_ki])
        Bd.copy(rt[:, :], ki[:, :], [b_ki], [b_rt])
        Bd.stt(dst[:, :], rt[:, :], -C1, ang[:, :], ALU.mult, ALU.add, [b_rt, b_ang], [bd])
        Bd.stt(dst[:, :], rt[:, :], -C2, dst[:, :], ALU.mult, ALU.add, [b_rt, bd], [bd])
        Bd.ts(dst[:, :], dst[:, :], shift, math.pi, ALU.add, ALU.min, [bd], [bd])
        Bd.ts(dst[:, :], dst[:, :], -math.pi, None, ALU.max, None, [bd], [bd])
        Bd.act(dst[:, :], dst[:, :], AF.Sin, [bd], [bd])
    S.barrier()
    Bd.release(mk_rope)

    h2 = Bd.sb("h2", [128, 8, 512], BF16); b_h2 = Buf()
    cqn = Bd.sb("cqn", [128, 2, 512], BF16); b_cqn = Buf()
    ckvn = Bd.sb("ckvn", [128, 512], BF16); b_ckvn = Buf()
    sqw = [Bd.sb(f"sqw{i}", [128, 2, 512], BF16) for i in range(2)]; b_sqw = [Buf() for i in range(2)]
    rsw = [Bd.sb(f"rsw{i}", [128, 512], F32) for i in range(2)]; b_rsw = [Buf() for i in range(2)]
    xnw = [Bd.sb(f"xnw{i}", [96, 512], F32) for i in range(2)]; b_xnw = [Buf() for i in range(2)]
    t1w = [Bd.sb(f"t1w{i}", [96, 512], F32) for i in range(2)]; b_t1w = [Buf() for i in range(2)]

    def stg(name, shape):
        t = Bd.sb(name, shape, BF16); b = Buf()
        return [t, t], [b, b]
    st_qm, b_stqm = stg("stqm", [96, 6, 512])
    st_km, b_stkm = stg("stkm", [96, 6, 512])
    st_qs, b_stqs = stg("stqs", [64, 6, 512])
    st_ks, b_stks = stg("stks", [64, 2, 512])
    st_qc, b_stqc = stg("stqc", [64, 4, 512])
    st_kc, b_stkc = stg("stkc", [64, 4, 512])
    st_vm, b_stvm = stg("stvm", [128, 6, 65])
    st_vs, b_stvs = stg("stvs", [128, 2, 65])
    st_vc, b_stvc = stg("stvc", [128, 4, 65])
    Bd.memset(st_vs[0][:, :, :], 1.0, [b_stvs[0]])
    Bd.memset(st_vm[0][:, :, :], 1.0, [b_stvm[0]])
    Bd.memset(st_vc[0][:, :, :], 1.0, [b_stvc[0]])

    A = m.A[:, 1, :]
    Bv = m.B(1)
    cnt = [0]

    def nxt():
        cnt[0] += 1
        return cnt[0] % 2

    PSS = 6

    def head_norm(pb, rows, gain, n, out_ap, outbuf, rope, t0):
        w = nxt()
        pin = ps[pb][0:rows, :]
        Bd.act(sqw[w][0:rows, 0, :], pin, AF.Square, [psb[pb]], [b_sqw[w]])
        Bd.mm(ps[PSS][0:rows, :], ones[0:rows, 0:rows], sqw[w][0:rows, 0, :], True, True, [b_sqw[w], b_ones], [psb[PSS]])
        Bd.rstd(ps[PSS][0:rows, :], float(n), rsw[w][0:rows, :], rsw[w][0:rows, :], [psb[PSS]], b_rsw[w], b_rsw[w])
        if not rope:
            Bd.stt(out_ap, pin, gain, rsw[w][0:rows, :], ALU.mult, ALU.mult, [psb[pb], b_rsw[w], b_hn], [outbuf])
            return
        Bd.stt(xnw[w][:, :], pin, gain, rsw[w][0:rows, :], ALU.mult, ALU.mult, [psb[pb], b_rsw[w], b_hn], [b_xnw[w]])
        Bd.mm(ps[PSS][0:96, :], rot[:, :], xnw[w][:, :], True, True, [b_rot, b_xnw[w]], [psb[PSS]])
        Bd.tt(t1w[w][:, :], xnw[w][:, :], Ct[:, t0:t0 + 512], ALU.mult, [b_xnw[w], b_C], [b_t1w[w]])
        Bd.tt(xnw[w][:, :], ps[PSS][0:96, :], St[:, t0:t0 + 512], ALU.mult, [psb[PSS], b_S], [b_xnw[w]])
        Bd.tt(out_ap, t1w[w][:, :], xnw[w][:, :], ALU.add, [b_t1w[w], b_xnw[w]], [outbuf])

    def units(name):
        u = do[name]
        return u if isinstance(u, list) else [(u, 0, u.shape[0])]

    def store3(name, t0, src, srcbuf):
        o = do.get(name + "_off", 0)
        for (ap, h0, nh) in units(name):
            Bd.store(ap.rearrange("h p t -> p h t")[:, :, o + t0:o + t0 + 512], src[:, h0:h0 + nh, :], srcbuf, final=final, dstbuf=do.get(name + "_buf"))

    def storev(name, mblk, src, srcbuf):
        o = do.get(name + "_off", 0)
        for (ap, h0, nh) in units(name):
            Bd.store(ap.rearrange("h p m d -> p h m d")[:, :, o + mblk, :], src[:, h0:h0 + nh, :], srcbuf, final=final, dstbuf=do.get(name + "_buf"))

    for tt in range(4):
        t0 = tt * 512
        par = tt % 2
        Bd.modnorm_tile(x, bx, t0, A, Bv, m.b, h2, b_h2, fw.nw)
        for cc in range(2):
            for kc in range(8):
                Bd.mm(ps[cc][:, :], win[:, kc, cc * 128:(cc + 1) * 128], h2[:, kc, :], kc == 0, kc == 7, [b_win, b_h2], [psb[cc]])
        w = nxt()
        for cc in range(2):
            Bd.act(sqw[w][:, cc, :], ps[cc][:, :], AF.Square, [psb[cc]], [b_sqw[w]])
        for cc in range(2):
            Bd.mm(ps[PSS][:, :], ones[:, :], sqw[w][:, cc, :], cc == 0, cc == 1, [b_sqw[w], b_ones], [psb[PSS]])
        Bd.rstd(ps[PSS][:, :], 256.0, rsw[w][:, :], rsw[w][:, :], [psb[PSS]], b_rsw[w], b_rsw[w])
        for cc in range(2):
            Bd.stt(cqn[:, cc, :], ps[cc][:, :], qan[:, cc:cc + 1], rsw[w][:, :], ALU.mult, ALU.mult, [psb[cc], b_rsw[w], b_qan], [b_cqn])
        for kc in range(8):
            Bd.mm(ps[2][:, :], win[:, kc, 256:384], h2[:, kc, :], kc == 0, kc == 7, [b_win, b_h2], [psb[2]])
        w = nxt()
        Bd.act(sqw[w][:, 0, :], ps[2][:, :], AF.Square, [psb[2]], [b_sqw[w]])
        Bd.mm(ps[PSS][:, :], ones[:, :], sqw[w][:, 0, :], True, True, [b_sqw[w], b_ones], [psb[PSS]])
        Bd.rstd(ps[PSS][:, :], 128.0, rsw[w][:, :], rsw[w][:, :], [psb[PSS]], b_rsw[w], b_rsw[w])
        Bd.stt(ckvn[:, :], ps[2][:, :], kvan[:, 0:1], rsw[w][:, :], ALU.mult, ALU.mult, [psb[2], b_rsw[w], b_kvan], [b_ckvn])
        for hh in range(6):
            pb = hh % 2
            for kc in range(2):
                Bd.mm(ps[pb][0:96, :], wuq[:, kc, hh * 96:(hh + 1) * 96], cqn[:, kc, :], kc == 0, kc == 1, [b_wuq, b_cqn], [psb[pb]])
            head_norm(pb, 96, hn[0:96, 0:1], 96, st_qm[par][:, hh, :], b_stqm[par], True, t0)
        store3("qm", t0, st_qm[par], b_stqm[par])
        for hh in range(6):
            pb = 2 + hh % 2
            Bd.mm(ps[pb][0:64, :], wukvk[:, hh, :], ckvn[:, :], True, True, [b_wukvk, b_ckvn], [psb[pb]])
            for kc in range(8):
                Bd.mm(ps[pb][64:96, :], win[:, kc, 384:416], h2[:, kc, :], kc == 0, kc == 7, [b_win, b_h2], [psb[pb]])
            head_norm(pb, 96, hn[0:96, 1:2], 96, st_km[par][:, hh, :], b_stkm[par], True, t0)
        store3("km", t0, st_km[par], b_stkm[par])
        for hh in range(6):
            pb = hh % 2
            for kc in range(8):
                Bd.mm(ps[pb][0:64, :], win[:, kc, 416 + hh * 64:416 + (hh + 1) * 64], h2[:, kc, :], kc == 0, kc == 7, [b_win, b_h2], [psb[pb]])
            head_norm(pb, 64, hn[0:64, 2:3], 64, st_qs[par][:, hh, :], b_stqs[par], False, t0)
        store3("qs", t0, st_qs[par], b_stqs[par])
        for hh in range(2):
            pb = 2 + hh % 2
            for kc in range(8):
                Bd.mm(ps[pb][0:64, :], win[:, kc, 800 + hh * 64:800 + (hh + 1) * 64], h2[:, kc, :], kc == 0, kc == 7, [b_win, b_h2], [psb[pb]])
            head_norm(pb, 64, hn[0:64, 3:4], 64, st_ks[par][:, hh, :], b_stks[par], False, t0)
        store3("ks", t0, st_ks[par], b_stks[par])
        for hh in range(4):
            pb = hh % 2
            for kc in range(8):
                Bd.mm(ps[pb][0:64, :], win[:, kc, 1056 + hh * 64:1056 + (hh + 1) * 64], h2[:, kc, :], kc == 0, kc == 7, [b_win, b_h2], [psb[pb]])
            Bd.act(st_qc[par][:, hh, :], ps[pb][0:64, :], AF.Copy, [psb[pb]], [b_stqc[par]], scale=0.125)
        store3("qc", t0, st_qc[par], b_stqc[par])
        for hh in range(4):
            pb = 2 + hh % 2
            for kc in range(8):
                Bd.mm(ps[pb][0:64, :], win[:, kc, 1312 + hh * 64:1312 + (hh + 1) * 64], h2[:, kc, :], kc == 0, kc == 7, [b_win, b_h2], [psb[pb]])
            Bd.copy(st_kc[par][:, hh, :], ps[pb][0:64, :], [psb[pb]], [b_stkc[par]])
        store3("kc", t0, st_kc[par], b_stkc[par])
        for blk in range(4):
            mblk = tt * 4 + blk
            bp = blk % 2
            pb = 4 + bp
            Bd.mm(ps[pb][:, 0:384], ckvn[:, blk * 128:(blk + 1) * 128], wukvv[:, :], True, True, [b_ckvn, b_wukvv], [psb[pb]])
            Bd.act(st_vm[bp][:, :, 0:64], ps[pb][:, 0:384].rearrange("p (h d) -> p h d", h=6), AF.Copy, [psb[pb]], [b_stvm[bp]])
            storev("vm", mblk, st_vm[bp], b_stvm[bp])
            pb2 = bp
            for kc in range(8):
                Bd.mm(ps[pb2][:, 0:384], h2[:, kc, blk * 128:(blk + 1) * 128], winv[:, kc, :], kc == 0, kc == 7, [b_h2, b_winv], [psb[pb2]])
            Bd.copy(st_vs[bp][:, :, 0:64], ps[pb2][:, 0:128].rearrange("p (h d) -> p h d", h=2), [psb[pb2]], [b_stvs[bp]])
            Bd.copy(st_vc[bp][:, :, 0:64], ps[pb2][:, 128:384].rearrange("p (h d) -> p h d", h=4), [psb[pb2]], [b_stvc[bp]])
            storev("vs", mblk, st_vs[bp], b_stvs[bp])
            storev("vc", mblk, st_vc[bp], b_stvc[bp])


LO_B = _bucket_thresholds()
MLA_SCALE = 96.0 ** -0.5


def attn_consts(Bd, di, w, bias_cache=None, first=True):
    S = Bd.S
    a = Ctx()
    a.zer = Bd.sb("zer", [128, 128], BF16); a.b_zer = Buf()
    Bd.memset(a.zer[:, :], 0.0, [a.b_zer])
    a.zrhs = Bd.sb("zrhs", [128, 512], BF16); a.b_zrhs = Buf()
    Bd.memset(a.zrhs[:, :], 0.0, [a.b_zrhs])
    a.tri = Bd.sb("tri", [128, 2, 128], BF16); a.b_tri = Buf()
    Bd.load(a.tri[:, :, :], di["tricomp"][:, :, :], a.b_tri, eng="pool")
    a.mincl = Bd.sb("mincl", [128, 4, 128], F32); a.b_mincl = Buf()
    Bd.load(a.mincl[:, :, :], di["mincl"][:, :, :], a.b_mincl)
    a.mstr = Bd.sb("mstr", [128, 4, 128], F32); a.b_mstr = Buf()
    Bd.load(a.mstr[:, :, :], di["mstrict"][:, :, :], a.b_mstr)
    a.sel = Bd.sb("sel", [65, 64], F32); a.b_sel = Buf()
    Bd.memset(a.sel[:, :], 0.0, [a.b_sel])
    Bd.memset(a.sel[64:65, :], 1.0, [a.b_sel])
    a.vsink = Bd.sb("vsink", [1, 65], F32); a.b_vsink = Buf()
    Bd.memset(a.vsink[:, :], 0.0, [a.b_vsink])
    Bd.memset(a.vsink[0:1, 64:65], 1.0, [a.b_vsink])
    a.wsel = Bd.sb("wsel", [128, 4], F32); a.b_wsel = Buf()
    Bd.load(a.wsel[:, :], di["wsel"][:, :], a.b_wsel)
    a.onorm = Bd.sb("onorm", [64, 16], F32); a.b_onorm = Buf()
    Bd.load(a.onorm[:, :], di["onorm"][:, :], a.b_onorm)
    es = Bd.sb("es", [1, 6], F32); b_es = Buf()
    Bd.load(es[:, :], di["sinks"][:, :], b_es)
    Bd.act(es[:, :], es[:, :], AF.Exp, [b_es], [b_es])
    a.esr = Bd.sb("esr", [1, 6, 128], F32); a.b_esr = Buf()
    Bd.copy(a.esr[:, :, :], es[0:1, :, None].to_broadcast([1, 6, 128]), [b_es], [a.b_esr])
    a.bias = Bd.sb("swabias", [128, 2, 6, 128], F32); a.b_bias = Buf()
    if bias_cache is not None and not first:
        Bd.load(a.bias[:, :, :, :].rearrange("p w h q -> p (w h q)"), bias_cache[0].ap(), a.b_bias, reads=[bias_cache[1]])
    else:
        tb = w.e[0][:, 0:192].rearrange("p (b h) -> p b h", h=6); b_tb = w.b_e[0]
        Bd.load(w.e[0][:, 0:192], di["relb"][0:1, :].partition_broadcast(128), b_tb)
        dtb = w.e[1][:, 0:186].rearrange("p (b h) -> p b h", h=6); b_dtb = w.b_e[1]
        Bd.tt(dtb[:, :, :], tb[:, 1:32, :], tb[:, 0:31, :], ALU.subtract, [b_tb], [b_dtb])
        pqi = w.e[3][:, 256:384].bitcast(I32); b_pqi = w.b_e[3]
        Bd.load(pqi, di["posrow"][0:1, :].partition_broadcast(128), b_pqi)
        pki = w.e[3][:, 384:386].bitcast(I32); b_pki = w.b_e[3]
        Bd.load(pki, di["poscol"][:, :], b_pki)
        pq = w.e[2][:, 0:128]; b_pq = w.b_e[2]
        pk = w.e[2][:, 128:130]; b_pk = w.b_e[2]
        Bd.copy(pq, pqi, [b_pqi], [b_pq])
        Bd.copy(pk, pki, [b_pki], [b_pk])
        rel = w.ec[0][:, 0:256].rearrange("p (w q) -> p w q", w=2); b_rel = w.b_ec[0]
        for wi in range(2):
            Bd.ts(rel[:, wi, :], pq, pk[:, wi:wi + 1], None, ALU.subtract, None, [b_pq, b_pk], [b_rel])
        ind = w.ec[1][:, 0:256].rearrange("p (w q) -> p w q", w=2); b_ind = w.b_ec[1]
        for h in range(6):
            Bd.ts(a.bias[:, :, h, :], rel[:, :, :], 0.0, tb[:, 0, h:h + 1], ALU.mult, ALU.add, [b_rel, b_tb], [a.b_bias])
        for b in range(1, 32):
            Bd.ts(ind[:, :, :], rel[:, :, :], float(LO_B[b - 1]), None, ALU.is_ge, None, [b_rel], [b_ind])
            for h in range(6):
                Bd.stt(a.bias[:, :, h, :], ind[:, :, :], dtb[:, b - 1, h:h + 1], a.bias[:, :, h, :], ALU.mult, ALU.add, [b_ind, b_dtb, a.b_bias], [a.b_bias])
        val = w.e[1][:, 256:512].rearrange("p (w q) -> p w q", w=2); b_val = w.b_e[1]
        Bd.ts(val[:, :, :], rel[:, :, :], 0.0, None, ALU.is_ge, None, [b_rel], [b_val])
        Bd.ts(ind[:, :, :], rel[:, :, :], 128.0, None, ALU.is_ge, None, [b_rel], [b_ind])
        Bd.ts(ind[:, :, :], ind[:, :, :], -1.0, 1.0, ALU.mult, ALU.add, [b_ind], [b_ind])
        Bd.tt(val[:, :, :], val[:, :, :], ind[:, :, :], ALU.mult, [b_val, b_ind], [b_val])
        Bd.ts(ind[:, :, :], val[:, :, :], 1.0e4, -1.0e4, ALU.mult, ALU.add, [b_val], [b_ind])
        for h in range(6):
            Bd.tt(a.bias[:, :, h, :], a.bias[:, :, h, :], val[:, :, :], ALU.mult, [a.b_bias, b_val], [a.b_bias])
            Bd.tt(a.bias[:, :, h, :], a.bias[:, :, h, :], ind[:, :, :], ALU.add, [a.b_bias, b_ind], [a.b_bias])

        if bias_cache is not None:
            Bd.store(bias_cache[0].ap(), a.bias[:, :, :, :].rearrange("p w h q -> p (w h q)"), a.b_bias, eng="sp", dstbuf=bias_cache[1])
    return a


def attn_work(Bd):
    w = Ctx()
    w.kbuf = [Bd.sb(f"kbuf{i}", [96, 4, T], BF16) for i in range(2)]; w.b_k = [Buf() for _ in range(2)]
    w.vbuf = [Bd.sb(f"vbuf{i}", [128, 4, NBLK, 65], BF16) for i in range(2)]; w.b_v = [Buf() for _ in range(2)]
    w.og = Bd.sb("ogrp", [64, 6, 512], F32); w.b_og = [Buf() for _ in range(6)]
    w.mix = Bd.sb("mix", [64, 16, 512], BF16); w.b_mix = [Buf() for _ in range(16)]
    w.qm = Bd.sb("qmg", [96, 6, 512], BF16); w.b_qm = Buf()
    w.qc = Bd.sb("qcg", [64, 4, 512], BF16); w.b_qc = Buf()
    w.qs = Bd.sb("qsg", [64, 6, 512], BF16); w.b_qs = Buf()
    w.p = [Bd.sb(f"pw{i}", [128, 512], BF16) for i in range(2)]; w.b_p = [Buf() for _ in range(2)]
    w.e = [Bd.sb(f"ew{i}", [128, 512], F32) for i in range(4)]; w.b_e = [Buf() for _ in range(4)]
    w.sp = [Bd.sb(f"spw{i}", [128, 512], BF16) for i in range(4)]; w.b_sp = [Buf() for _ in range(4)]
    w.ec = [Bd.sb(f"ecw{i}", [128, 512], F32) for i in range(2)]; w.b_ec = [Buf() for _ in range(2)]
    w.a = w.p; w.b_a = w.b_p
    w.oa = Bd.sb("oa", [65, 512], F32); w.b_oa = Buf()
    w.rden = Bd.sb("rden", [64, 512], F32); w.b_rden = Buf()
    w.ksw = Bd.sb("ksw", [64, 2, 2, 512], BF16); w.b_ksw = Buf()
    w.vsw = Bd.sb("vsw", [128, 2, 2, 4, 65], BF16); w.b_vsw = Buf()
    w.sarg = [w.e[i][:, 0:384].rearrange("p (j q) -> p j q", j=3) for i in range(2)]; w.b_sarg = w.b_e[0:2]
    w.psw = [w.sp[i][:, 0:384].rearrange("p (j q) -> p j q", j=3) for i in range(2)]; w.b_psw = w.b_sp[0:2]
    w.gsq = w.sp[2][0:64, :]; w.b_gsq = w.b_sp[2]
    w.grs = w.e[2][0:64, :]; w.b_grs = w.b_e[2]
    wo = Bd.sb("wo", [64, 16, 128], BF16); bwo = Buf()
    w.wo = [wo, wo]; w.b_wo = [bwo, bwo]
    return w


def emit_attention(Bd, x, bx, m, ac, w, di):
    S = Bd.S
    ps, psb = Bd.ps, Bd.psb
    ones = Bd.ones_bf
    for G in (3, 2, 1, 0):
        t0 = G * 512
        nk = 4 * G + 4
        ntok = nk * 128
        nsteps = 16 * G + 16
        Bd.load(w.qm[:, :, :], di["qm"].rearrange("h p t -> p h t")[:, :, t0:t0 + 512], w.b_qm, reads=[di["b_q"]])
        Bd.load(w.qc[:, :, :], di["qc"].rearrange("h p t -> p h t")[:, :, t0:t0 + 512], w.b_qc, reads=[di["b_q"]])
        Bd.load(w.qs[:, :, :], di["qs"].rearrange("h p t -> p h t")[:, :, t0:t0 + 512], w.b_qs, reads=[di["b_q"]])

        def kvload(slot, kf, vf, h, rows):
            Bd.load(w.kbuf[slot][0:rows, :, 0:ntok], di[kf][h][0].rearrange("r p t -> p r t")[:, :, 0:ntok], w.b_k[slot], reads=[di[kf][h][1]])
            Bd.load(w.vbuf[slot][:, :, 0:nk, :], di[vf][h][0].rearrange("r p m d -> p r m d")[:, :, 0:nk, :], w.b_v[slot], reads=[di[vf][h][1]])

        def step_geom(i):
            kb = nsteps - 1 - i
            kw = kb - 16 * G
            c0 = (kw // 4) * 128 if kw >= 0 else 0
            return kb % 4, kb // 4, c0, (kw % 4 if kw >= 0 else None)

        for pair in range(3):
            hs = (2 * pair, 2 * pair + 1)
            for hi, h in enumerate(hs):
                kvload(hi, "kmf", "vmf", h, 96)
            PS = {0: (0, 1), 1: (2, 7)}
            POm = {0: 3, 1: 4}
            PT = {0: ((w.p[0], w.b_p[0]), (w.p[1], w.b_p[1])), 1: ((w.sp[2], w.b_sp[2]), (w.sp[3], w.b_sp[3]))}
            for hi in range(2):
                Bd.mm(ps[POm[hi]][0:65, :], ac.zer[:, 0:65], ac.zrhs[:, :], True, False, [ac.b_zer, ac.b_zrhs], [psb[POm[hi]]])

            def m_s(hi, i):
                rk, mb, c0, dm = step_geom(i)
                pb = PS[hi][i % 2]
                Bd.mm(ps[pb][:, c0:512], w.kbuf[hi][0:96, rk, mb * 128:(mb + 1) * 128], w.qm[0:96, hs[hi], c0:512], True, True, [w.b_k[hi], w.b_qm], [psb[pb]])

            def m_e(hi, i):
                rk, mb, c0, dm = step_geom(i)
                pb = PS[hi][i % 2]
                pt, bpt = PT[hi][i % 2]
                Bd.act(pt[:, c0:512], ps[pb][:, c0:512], AF.Exp, [psb[pb]], [bpt], scale=MLA_SCALE)
                if dm is not None:
                    Bd.tt(pt[:, c0:c0 + 128], pt[:, c0:c0 + 128], ac.mincl[:, dm, :], ALU.mult, [bpt, ac.b_mincl], [bpt])

            def m_pv(hi, i):
                rk, mb, c0, dm = step_geom(i)
                pt, bpt = PT[hi][i % 2]
                Bd.mm(ps[POm[hi]][0:65, c0:512], w.vbuf[hi][:, rk, mb, 0:65], pt[:, c0:512], False, i == nsteps - 1, [w.b_v[hi], bpt], [psb[POm[hi]]])
            for hi in range(2):
                m_s(hi, 0)
            for hi in range(2):
                m_e(hi, 0)
            for i in range(nsteps):
                if i + 1 < nsteps:
                    for hi in range(2):
                        m_s(hi, i + 1)
                for hi in range(2):
                    m_pv(hi, i)
                if i + 1 < nsteps:
                    for hi in range(2):
                        m_e(hi, i + 1)
            for hi, h in enumerate(hs):
                po = POm[hi]
                Bd.act(w.oa[:, :], ps[po][0:65, :], AF.Copy, [psb[po]], [w.b_oa])
                Bd.mm(ps[5][0:64, :], ac.sel[:, :], w.oa[:, :], True, True, [ac.b_sel, w.b_oa], [psb[5]])
                Bd.recip(w.rden[:, :], ps[5][0:64, :], [psb[5]], [w.b_rden])
                Bd.tt(w.og[:, h, :], w.oa[0:64, :], w.rden[:, :], ALU.mult, [w.b_oa, w.b_rden], [w.b_og[h]])
        group_norm(Bd, w, ac, 0, 6, 384.0, G)

        Bd.load(w.ksw[:, :, 1, :], di["ks_own"].rearrange("h p t -> p h t")[:, :, 128 + t0:128 + t0 + 512], w.b_ksw, reads=[di["b_src"]])
        Bd.load(w.vsw[:, :, 1, :, :], di["vs_own"].rearrange("h p m d -> p h m d")[:, :, 1 + 4 * G:5 + 4 * G, :], w.b_vsw, reads=[di["b_src"]])
        for c in range(4):
            ko = 128 + t0 if c < 3 else t0
            vo = 1 + 4 * G if c < 3 else 4 * G
            kcand = w.kbuf[0][0:64, c, 0:1024].rearrange("p (k t) -> p k t", k=2)
            vcand = w.vbuf[0][:, c, 0:8, :].rearrange("p (k m) d -> p k m d", k=2)
            Bd.load(kcand, di["ksf"].rearrange("r h p t -> p r h t")[:, c, :, ko:ko + 512], w.b_k[0], reads=[di["b_ks"]])
            Bd.load(vcand, di["vsf"].rearrange("r h p m d -> p r h m d")[:, c, :, vo:vo + 4, :], w.b_v[0], reads=[di["b_vs"]])
        for c in range(4):
            kcand = w.kbuf[0][0:64, c, 0:1024].rearrange("p (k t) -> p k t", k=2)
            vcand = w.vbuf[0][:, c, 0:8, :].rearrange("p (k m) d -> p k m d", k=2)
            if c == 0:
                Bd.ts(w.ksw[:, :, 0, :], kcand, ac.wsel[0:64, 0:1], None, ALU.mult, None, [w.b_k[0], ac.b_wsel], [w.b_ksw])
                Bd.ts(w.vsw[:, :, 0, :, :], vcand, ac.wsel[:, 0:1], None, ALU.mult, None, [w.b_v[0], ac.b_wsel], [w.b_vsw])
            else:
                Bd.stt(w.ksw[:, :, 0, :], kcand, ac.wsel[0:64, c:c + 1], w.ksw[:, :, 0, :], ALU.mult, ALU.add, [w.b_k[0], ac.b_wsel, w.b_ksw], [w.b_ksw])
                Bd.stt(w.vsw[:, :, 0, :, :], vcand, ac.wsel[:, c:c + 1], w.vsw[:, :, 0, :, :], ALU.mult, ALU.add, [w.b_v[0], ac.b_wsel, w.b_vsw], [w.b_vsw])
        for blk in range(4):
            q0 = blk * 128
            for kvh in range(2):
                pso = 3 + kvh
                Bd.mm(ps[pso][0:65, 0:384], ac.zer[:, 0:65], ac.zrhs[:, 0:384], True, False, [ac.b_zer, ac.b_zrhs], [psb[pso]])
                for wh in range(2):
                    pss = (blk * 4 + kvh * 2 + wh) % 2
                    for j in range(3):
                        hq = kvh * 3 + j
                        Bd.mm(ps[pss][:, j * 128:(j + 1) * 128], w.ksw[:, kvh, wh, q0:q0 + 128], w.qs[:, hq, q0:q0 + 128], True, True, [w.b_ksw, w.b_qs], [psb[pss]])
                    Bd.stt(w.sarg[pss][:, :, :], ps[pss][:, 0:384].rearrange("p (j q) -> p j q", j=3), 0.125, ac.bias[:, wh, kvh * 3:(kvh + 1) * 3, :], ALU.mult, ALU.add, [psb[pss], ac.b_bias], [w.b_sarg[pss]])
                    Bd.act(w.psw[pss][:, :, :], w.sarg[pss][:, :, :], AF.Exp, [w.b_sarg[pss]], [w.b_psw[pss]])
                    for j in range(3):
                        Bd.mm(ps[pso][0:65, j * 128:(j + 1) * 128], w.vsw[:, kvh, wh, blk, 0:65], w.psw[pss][:, j, :], False, False, [w.b_vsw, w.b_psw[pss]], [psb[pso]])
                for j in range(3):
                    hq = kvh * 3 + j
                    Bd.mm(ps[pso][0:65, j * 128:(j + 1) * 128], ac.vsink[0:1, :], ac.esr[0:1, hq, :], False, True, [ac.b_vsink, ac.b_esr], [psb[pso]])
                Bd.act(w.oa[:, 0:384], ps[pso][0:65, 0:384], AF.Copy, [psb[pso]], [w.b_oa])
                Bd.mm(ps[5][0:64, 0:384], ac.sel[:, :], w.oa[:, 0:384], True, True, [ac.b_sel, w.b_oa], [psb[5]])
                Bd.recip(w.rden[:, 0:384], ps[5][0:64, 0:384], [psb[5]], [w.b_rden])
                for j in range(3):
                    hq = kvh * 3 + j
                    Bd.tt(w.og[:, hq, q0:q0 + 128], w.oa[0:64, j * 128:(j + 1) * 128], w.rden[:, j * 128:(j + 1) * 128], ALU.mult, [w.b_oa, w.b_rden], [w.b_og[hq]])
        group_norm(Bd, w, ac, 6, 6, 384.0, G)

        for pair in range(2):
            hs = (2 * pair, 2 * pair + 1)
            for hi, h in enumerate(hs):
                kvload(hi, "kcf", "vcf", h, 64)
            PZ = {0: (0, 1), 1: (2, 7)}
            PC = {0: 3, 1: 4}
            PO = {0: 5, 1: 6}
            for hi in range(2):
                Bd.mm(ps[PC[hi]][:, :], ac.zer[:, :], ac.zrhs[:, :], True, False, [ac.b_zer, ac.b_zrhs], [psb[PC[hi]]])
                Bd.mm(ps[PO[hi]][0:64, :], ac.zer[:, 0:64], ac.zrhs[:, :], True, False, [ac.b_zer, ac.b_zrhs], [psb[PO[hi]]])

            def s_z(hi, i):
                h = hs[hi]
                rk, mb, c0, dm = step_geom(i)
                pz = PZ[hi][i % 2]
                Bd.mm(ps[pz][:, c0:512], w.kbuf[hi][0:64, rk, mb * 128:(mb + 1) * 128], w.qc[0:64, h, c0:512], True, True, [w.b_k[hi], w.b_qc], [psb[pz]])

            def s_e(hi, i):
                rk, mb, c0, dm = step_geom(i)
                pz = PZ[hi][i % 2]
                ew = hi * 2 + i % 2
                Bd.act(w.e[ew][:, c0:512], ps[pz][:, c0:512], AF.Exp, [psb[pz]], [w.b_e[ew]])
                if dm is not None:
                    Bd.tt(w.e[ew][:, c0:c0 + 128], w.e[ew][:, c0:c0 + 128], ac.mstr[:, dm, :], ALU.mult, [w.b_e[ew], ac.b_mstr], [w.b_e[ew]])

            def s_sp(hi, i):
                rk, mb, c0, dm = step_geom(i)
                ew = hi * 2 + i % 2
                Bd.act(w.sp[ew][:, c0:512], w.e[ew][:, c0:512], AF.Ln, [w.b_e[ew]], [w.b_sp[ew]], bias=1.0)

            def s_tri(hi, i):
                rk, mb, c0, dm = step_geom(i)
                ew = hi * 2 + i % 2
                Bd.mm(ps[PC[hi]][:, c0:512], ac.tri[:, 0, :], w.sp[ew][:, c0:512], False, False, [ac.b_tri, w.b_sp[ew]], [psb[PC[hi]]])

            def s_ec(hi, i):
                rk, mb, c0, dm = step_geom(i)
                Bd.act(w.ec[hi][:, c0:512], ps[PC[hi]][:, c0:512], AF.Exp, [psb[PC[hi]]], [w.b_ec[hi]], scale=-1.0)

            def s_a(hi, i):
                rk, mb, c0, dm = step_geom(i)
                ew = hi * 2 + i % 2
                Bd.tt(w.a[hi][:, c0:512], w.e[ew][:, c0:512], w.ec[hi][:, c0:512], ALU.mult, [w.b_e[ew], w.b_ec[hi]], [w.b_a[hi]])

            def s_co(hi, i):
                rk, mb, c0, dm = step_geom(i)
                ew = hi * 2 + i % 2
                pc, po = PC[hi], PO[hi]
                Bd.mm(ps[pc][:, c0:512], ac.tri[:, 1, :], w.sp[ew][:, c0:512], False, i == nsteps - 1, [ac.b_tri, w.b_sp[ew]], [psb[pc]])
                Bd.mm(ps[po][0:64, c0:512], w.vbuf[hi][:, rk, mb, 0:64], w.a[hi][:, c0:512], False, i == nsteps - 1, [w.b_v[hi], w.b_a[hi]], [psb[po]])
            for hi in range(2):
                s_z(hi, 0)
            for hi in range(2):
                s_e(hi, 0)
            for hi in range(2):
                s_sp(hi, 0)
            for i in range(nsteps):
                nxt_ = i + 1 < nsteps
                for hi in range(2):
                    s_tri(hi, i)
                if nxt_:
                    for hi in range(2):
                        s_z(hi, i + 1)
                for hi in range(2):
                    s_ec(hi, i)
                if nxt_:
                    for hi in range(2):
                        s_e(hi, i + 1)
                for hi in range(2):
                    s_a(hi, i)
                if nxt_:
                    for hi in range(2):
                        s_sp(hi, i + 1)
                for hi in range(2):
                    s_co(hi, i)
            for hi, h in enumerate(hs):
                Bd.act(w.og[:, h, :], ps[PO[hi]][0:64, :], AF.Copy, [psb[PO[hi]]], [w.b_og[h]])
        group_norm(Bd, w, ac, 12, 4, 256.0, G)

        for j in range(8):
            s = j % 2
            Bd.load(w.wo[s][:, :, :], di["wout"][j, :, :, :], w.b_wo[s], eng="pool")
            pb = j % 2
            for hh in range(16):
                Bd.mm(ps[pb][:, :], w.wo[s][:, hh, :], w.mix[:, hh, :], hh == 0, hh == 15, [w.b_wo[s], w.b_mix[hh]], [psb[pb]])
            Bd.stt(x[:, j, t0:t0 + 512], ps[pb][:, :], m.gate[:, 1, j:j + 1], x[:, j, t0:t0 + 512], ALU.mult, ALU.add, [psb[pb], m.b, bx], [bx])


DEBUG = False


def group_norm(Bd, w, ac, h0, nh, width, G=0):
    ps, psb = Bd.ps, Bd.psb
    if DEBUG:
        for i in range(nh):
            Bd.store(Bd.outs["dbg_og"][G, :, h0 + i, :], w.og[:, i, :], w.b_og[i], eng="sp", final=True)
    for i in range(nh):
        Bd.act(w.gsq[:, :], w.og[:, i, :], AF.Square, [w.b_og[i]], [w.b_gsq])
        Bd.mm(ps[7][0:64, :], Bd.ones_bf[0:64, 0:64], w.gsq[:, :], i == 0, i == nh - 1, [w.b_gsq, Bd.b_ones], [psb[7]])
    Bd.rstd(ps[7][0:64, :], width, w.grs[:, :], w.grs[:, :], [psb[7]], w.b_grs, w.b_grs)
    for i in range(nh):
        Bd.stt(w.mix[:, h0 + i, :], w.og[:, i, :], ac.onorm[:, h0 + i:h0 + i + 1], w.grs[:, :], ALU.mult, ALU.mult, [w.b_og[i], ac.b_onorm, w.b_grs], [w.b_mix[h0 + i]])


LAYER_IN = dict(wmod=[129, 8, 9216], bmod=[128, 72], ng=[128, 3, 8], wgu1=[NF + 1, 128, 8, 2, 128], wdn1=[9, 128, NF, 128],
                win=[129, 8, 1824], winv=[129, 8, 384], qan=[128, 2], kvan=[128, 1], wuq=[128, 2, 576], wukvk=[128, 6, 64],
                wukvv=[128, 384], hnorms=[96, 4], wout=[9, 64, 16, 128], onorm=[64, 16], sinks=[1, 6],
                wgu2=[NF + 1, 128, 8, 2, 128], wdn2=[9, 128, NF, 128])
RG = [[0, 1, 2, 3], [4, 5, 6, 7]]


def build_F():
    Bd = Builder()
    S = Bd.S
    nc = Bd.nc
    xT = Bd.inp("xT", [D, T])
    xoT = Bd.outp("xoT", [D, T])
    cT = Bd.inp("cT", [128, 8])
    shared_in = dict(pos=Bd.inp("pos", [1, T], I32), freq=Bd.inp("freq", [96, 1]), rot=Bd.inp("rotT", [96, 96]),
                     tricomp=Bd.inp("tricomp", [128, 2, 128]), mincl=Bd.inp("mincl", [128, 4, 128]),
                     mstrict=Bd.inp("mstrict", [128, 4, 128]), relb=Bd.inp("relb", [1, 192]),
                     posrow=Bd.inp("posrow", [1, 128], I32), poscol=Bd.inp("poscol", [128, 2], I32),
                     wsel=Bd.inp("wsel", [128, 4]))
    L = [{k: Bd.inp(f"{k}_{l}", shp) for k, shp in LAYER_IN.items()} for l in range(2)]

    x = Bd.sb("x", [128, 8, T], F32); bx = Buf("x")
    for kc in range(8):
        Bd.load(x[:, kc, :], xT[kc * 128:(kc + 1) * 128, :], bx)
    ma = Bd.mod_alloc()
    zt = Bd.sb("zt", [128, 1105], BF16); b_zt = Buf()
    Bd.memset(zt[:, :], 0.0, [b_zt])
    mk0 = Bd.mark()
    bias_cache = (nc.dram_tensor("bias_scr", [128, 2 * 6 * 128], F32), Buf("bias_scr"))
    for l in range(2):
        W = L[l]
        def dt2(name, rows, cols):
            return nc.dram_tensor(f"{name}_{l}", [rows, cols], BF16)
        q_km = dt2("sq_m", 6 * 96, T); q_qs = dt2("sq_s", 6 * 64, T); q_qc = dt2("sq_c", 4 * 64, T)
        UR = 192
        units_s, units_g = [], []

        units_b = []

        def unit():
            i = len(units_s)
            units_s.append(dt2(f"su{i}", UR, T))
            units_g.append(dt2(f"gu{i}", 4 * UR, T))
            units_b.append(Buf(f"gu{i}"))
            return units_s[-1].ap(), units_g[-1].ap().rearrange("(r n) t -> r n t", r=4)
        o_km, g_km, o_kc, g_kc, o_vm, g_vm, o_vc, g_vc = [], [], [], [], [], [], [], []
        for i in range(3):
            su, gu = unit()
            o_km.append((su[0:192, :].rearrange("(h p) t -> h p t", h=2), 2 * i, 2))
            g_km += [(gu[:, hh * 96:(hh + 1) * 96, :], units_b[-1]) for hh in range(2)]
        for i in range(2):
            su, gu = unit()
            o_kc.append((su[0:128, :].rearrange("(h p) t -> h p t", h=2), 2 * i, 2))
            g_kc += [(gu[:, hh * 64:(hh + 1) * 64, :], units_b[-1]) for hh in range(2)]
        for (ol, gl, n) in ((o_vm, g_vm, 3), (o_vc, g_vc, 2)):
            for i in range(n):
                su, gu = unit()
                ol.append((su[0:130, :].rearrange("n t -> (n t)").rearrange("(h p m d) -> h p m d", h=2, p=128, d=65), 2 * i, 2))
                gv = gu[:, 0:130, :].rearrange("r n t -> r (n t)").rearrange("r (h p m d) -> r h p m d", h=2, p=128, d=65)
                gl += [(gv[:, hh, :, :, :], units_b[-1]) for hh in range(2)]
        su, gu = unit()
        o_ks = su[0:136, :].rearrange("n t -> (n t)").rearrange("(h p u) -> h p u", h=2, p=64)
        g_ks = gu[:, 0:136, :].rearrange("r n t -> r (n t)").rearrange("r (h p u) -> r h p u", h=2, p=64)
        b_gks = units_b[-1]
        su, gu = unit()
        o_vs = su[0:139, :].rearrange("n t -> (n t)")[0:2 * 128 * 17 * 65].rearrange("(h p m d) -> h p m d", h=2, p=128, d=65)
        g_vs = gu[:, 0:139, :].rearrange("r n t -> r (n t)")[:, 0:2 * 128 * 17 * 65].rearrange("r (h p m d) -> r h p m d", h=2, p=128, d=65)
        b_gvs = units_b[-1]
        b_src = Buf("src", multi=True)
        b_q = Buf("qscr", multi=True)
        b_g = Buf("gath")
        Bd.store(o_ks.rearrange("h p u -> p h u")[:, :, 0:128], zt[0:64, 0:256].rearrange("p (h u) -> p h u", h=2), b_zt, dstbuf=b_src)
        Bd.store(o_vs.rearrange("h p m d -> p h m d")[:, :, 0, :], zt[:, 0:130].rearrange("p (h d) -> p h d", h=2), b_zt, dstbuf=b_src)
        nw = Bd.norm_work()
        fw = Bd.ffn_work(nw)
        stage = [fw.act[:, 0:8, :].bitcast(F32), fw.act[:, 8:16, :].bitcast(F32)]
        modsb, b_modsb, ng, b_ng = emit_mod(Bd, cT, W["wmod"], W["bmod"], W["ng"], stage, [Buf(), Buf()], ma)
        m = Bd.derive_mod(modsb, b_modsb, ng, b_ng, ma)
        S.barrier()
        Bd.ffn(x, bx, W["wgu1"], W["wdn1"], m, 0, fw)
        S.barrier()
        Bd.release(mk0)
        nw = Bd.norm_work()
        fw2 = Ctx(); fw2.nw = nw
        di = dict(win=W["win"], winv=W["winv"], qan=W["qan"], kvan=W["kvan"], wuq=W["wuq"], wukvk=W["wukvk"], wukvv=W["wukvv"],
                  hn=W["hnorms"], pos=shared_in["pos"], freq=shared_in["freq"], rot=shared_in["rot"])
        do = dict(qm=q_km.ap().rearrange("(h p) t -> h p t", h=6), qs=q_qs.ap().rearrange("(h p) t -> h p t", h=6),
                  qc=q_qc.ap().rearrange("(h p) t -> h p t", h=4),
                  km=o_km, kc=o_kc, ks=o_ks, ks_off=128, vm=o_vm, vc=o_vc, vs=o_vs, vs_off=1,
                  qm_buf=b_q, qs_buf=b_q, qc_buf=b_q, km_buf=b_src, kc_buf=b_src, ks_buf=b_src, vm_buf=b_src, vc_buf=b_src, vs_buf=b_src)
        emit_proj(Bd, x, bx, m, fw2, di, do, final=False)
        S.barrier()
        Bd.release(mk0)
        da = dict(shared_in)
        da.update(onorm=W["onorm"], sinks=W["sinks"], wout=W["wout"],
                  qm=do["qm"], qs=do["qs"], qc=do["qc"], b_q=b_q, b_src=b_src, b_ks=b_gks, b_vs=b_gvs,
                  kmf=g_km, vmf=g_vm, kcf=g_kc, vcf=g_vc, ksf=g_ks, vsf=g_vs, ks_own=o_ks, vs_own=o_vs)
        w = attn_work(Bd)
        ac = attn_consts(Bd, da, w, bias_cache, first=(l == 0))
        for ui in (0, 5, 1, 6, 2, 7, 10, 11, 3, 8, 4, 9):
            S.coll("pool", (lambda a_, b_: lambda h: h.collective_compute("AllGather", ALU.bypass, replica_groups=RG, ins=[a_.ap()], outs=[b_.ap()]))(units_s[ui], units_g[ui]),
                   [b_src], [units_b[ui]], lambda h: h.memset(zt[0:1, 0:8], 0.0))
        emit_attention(Bd, x, bx, m, ac, w, da)
        S.barrier()
        Bd.release(mk0)
        nw = Bd.norm_work()
        fw = Bd.ffn_work(nw)
        Bd.ffn(x, bx, W["wgu2"], W["wdn2"], m, 2, fw)
        S.barrier()
        Bd.release(mk0)
    for kc in range(8):
        Bd.store(xoT[kc * 128:(kc + 1) * 128, :], x[:, kc, :], bx, eng="sp", final=True)
    S.wait_all("sp", Bd.finals)
    print("F sbuf peak", Bd.peak, {e: len(S.q[e]) for e in ENGS})
    S.emit()
    return Bd


_CACHE = {}


def _get(name, fn):
    if name not in _CACHE:
        _CACHE[name] = fn()
    return _CACHE[name]


def _core_tokens(r):
    idx = (np.arange(NBLK)[:, None] * 4 + r) * 128 + np.arange(128)[None, :]
    return idx.reshape(-1)


def _freq_rot():
    half = 16
    fr = (np.float32(10000.0) ** (-np.arange(half, dtype=np.float32) / np.float32(half))).astype(np.float32)
    freq = np.zeros((96, 1), np.float32)
    freq[64:80, 0] = fr
    freq[80:96, 0] = fr
    rotT = np.zeros((96, 96), np.float32)
    for i in range(16):
        rotT[80 + i, 64 + i] = -1.0
        rotT[64 + i, 80 + i] = 1.0
    return freq, rotT


def _ffn_layout(wgu_l, wdn_l):
    f = np.ascontiguousarray
    wgu = wgu_l.reshape(8, 128, 2, NF, 128)
    wdn = wdn_l.reshape(NF, 128, 8, 128)
    return f(wgu.transpose(3, 1, 0, 2, 4)), f(wdn.transpose(2, 1, 0, 3))


def layer_inputs(inp, l):
    f = np.ascontiguousarray
    d = {}
    d["wmod"] = f(inp["w_mod"][l].reshape(8, 128, 9216).transpose(1, 0, 2))
    d["bmod"] = f(inp["b_mod"][l].reshape(72, 128).T)
    d["ng"] = f(inp["norm_g"][l].reshape(3, 8, 128).transpose(2, 0, 1))
    d["wgu1"], d["wdn1"] = _ffn_layout(inp["w_ffn1_gu"][l], inp["w_ffn1_down"][l])
    d["wgu2"], d["wdn2"] = _ffn_layout(inp["w_ffn2_gu"][l], inp["w_ffn2_down"][l])
    win = inp["w_in"][l]
    d["win"] = f(win.reshape(8, 128, 1824).transpose(1, 0, 2))
    winv = np.concatenate([win[:, 928:1056], win[:, 1568:1824]], axis=1)
    d["winv"] = f(winv.reshape(8, 128, 384).transpose(1, 0, 2))
    d["qan"] = f(inp["q_a_norm"][l].reshape(2, 128).T)
    d["kvan"] = f(inp["kv_a_norm"][l].reshape(128, 1))
    d["wuq"] = f(inp["w_uq"][l].reshape(2, 128, 576).transpose(1, 0, 2))
    wukv = inp["w_ukv"][l].reshape(128, 6, 128)
    d["wukvk"] = f(wukv[:, :, 0:64])
    d["wukvv"] = f(wukv[:, :, 64:128].reshape(128, 384))
    hn = np.zeros((96, 4), np.float32)
    hn[:, 0] = inp["mla_q_norm"][l]
    hn[:, 1] = inp["mla_k_norm"][l]
    hn[0:64, 2] = inp["swa_q_norm"][l]
    hn[0:64, 3] = inp["swa_k_norm"][l]
    d["hnorms"] = hn
    d["wout"] = f(inp["w_out"][l].reshape(16, 64, 8, 128).transpose(2, 1, 0, 3))
    d["onorm"] = f(inp["out_norm"][l].reshape(16, 64).T)
    d["sinks"] = f(inp["sinks"][l].reshape(1, 6))
    return d


def _masks(r):
    j = np.arange(128)[:, None]
    q = np.arange(128)[None, :]
    mi = np.zeros((128, 4, 128), np.float32)
    ms = np.zeros((128, 4, 128), np.float32)
    for d in range(4):
        if d < r:
            mi[:, d, :] = 1.0
            ms[:, d, :] = 1.0
        elif d == r:
            mi[:, d, :] = (j <= q)
            ms[:, d, :] = (j < q)
    tc = np.zeros((128, 2, 128), np.float32)
    tc[:, 0, :] = (j >= q)
    tc[:, 1, :] = (j < q)
    return mi, ms, tc


CORES = [(b, r) for b in range(2) for r in range(4)]
_PAD_KEYS = ("wmod", "wgu1", "wdn1", "wgu2", "wdn2", "win", "winv", "wout")


def _pad(a, ci):
    return np.concatenate([a, np.full((1,) + a.shape[1:], float(ci), a.dtype)], axis=0)


def kernel(**inp):
    inp = {k: np.asarray(v) for k, v in inp.items()}
    prog = _get("F", build_F)
    x = inp["x"]
    pos = inp["positions"]
    freq, rotT = _freq_rot()
    lay = [layer_inputs(inp, l) for l in range(2)]
    in_maps = []
    for ci, (b, r) in enumerate(CORES):
        mi, ms, tc = _masks(r)
        wsel = np.zeros((128, 4), np.float32)
        wsel[:, (r + 3) % 4] = 1.0
        dct = dict(xT=np.ascontiguousarray(x[b][_core_tokens(r)].T),
                   cT=np.ascontiguousarray(inp["c"][b].reshape(8, 128).T),
                   pos=np.ascontiguousarray(pos[b][_core_tokens(r)].reshape(1, T).astype(np.int32)),
                   freq=freq, rotT=rotT, tricomp=tc, mincl=mi, mstrict=ms,
                   relb=np.ascontiguousarray(inp["rel_bias"].reshape(1, 192)),
                   posrow=np.ascontiguousarray(pos[b][(4 + r) * 128:(5 + r) * 128].reshape(1, 128).astype(np.int32)),
                   poscol=np.ascontiguousarray(np.stack([pos[b][(3 + r) * 128:(4 + r) * 128], pos[b][(4 + r) * 128:(5 + r) * 128]], axis=1).astype(np.int32)),
                   wsel=wsel)
        for l in range(2):
            for k, a in lay[l].items():
                dct[f"{k}_{l}"] = _pad(a, ci) if k in _PAD_KEYS else a
        in_maps.append(dct)
    res = run_bass_kernel_spmd(prog.nc, in_maps, core_ids=list(range(NCORES)))
    out = np.zeros((2, 8192, 1024), np.float32)
    for ci, (b, r) in enumerate(CORES):
        out[b][_core_tokens(r)] = res.results[ci]["xoT"].T
    return out
```

```python
import math
import numpy as np
import ml_dtypes
import concourse.bass as bass
import concourse.mybir as mybir
from concourse.bass_utils import run_bass_kernel_spmd

F32 = mybir.dt.float32
BF16 = mybir.dt.bfloat16
I32 = mybir.dt.int32
AF = mybir.ActivationFunctionType
ALU = mybir.AluOpType

T = 2048
NBLK = 16
D = 1024
DFF = 2816
NF = 22
EPS = 1e-6
NCORES = 8
TWO_PI = 2.0 * math.pi

ENGS = ("pe", "act", "dve", "pool", "sp")
EPOCH = 30000


class Buf:
    __slots__ = ("name", "writers", "readers", "sem", "cnt", "multi")

    def __init__(self, name="", multi=False):
        self.name = name
        self.writers = []
        self.readers = []
        self.sem = None
        self.cnt = 0
        self.multi = multi


class Op:
    __slots__ = ("eng", "fn", "deps", "signal", "k", "dma_tok", "is_dma")

    def __init__(self, eng, fn):
        self.eng = eng
        self.fn = fn
        self.deps = []
        self.signal = False
        self.k = -1
        self.dma_tok = None
        self.is_dma = False


class Sched:
    def __init__(self, nc, same_engine_raw=True):
        self.nc = nc
        self.q = {e: [] for e in ENGS}
        self.same_engine_raw = same_engine_raw
        self.nsem = 0
        self.dmas = []
        self.csem = None
        self.ccnt = 0
        self.sem_pool = []
        self.sem_bufs = []

    def new_sem(self, name):
        self.nsem += 1
        return self.nc.alloc_semaphore(f"{name}_{self.nsem}")

    def _dep(self, o, p, raw):
        if p is None or p is o:
            return
        if (not p.is_dma) and (not o.is_dma) and p.eng == o.eng:
            if not (raw and self.same_engine_raw and o.eng != "pe"):
                return
        if p not in o.deps:
            o.deps.append(p)
            if not p.is_dma:
                p.signal = True

    def _track(self, o, reads, writes):
        for b in reads:
            for w in b.writers:
                self._dep(o, w, True)
        for b in writes:
            if not (b.multi and o.is_dma):
                for w in b.writers:
                    self._dep(o, w, False)
            for r in b.readers:
                self._dep(o, r, False)
        for b in reads:
            b.readers.append(o)
        for b in writes:
            if b.multi and o.is_dma and not b.readers:
                b.writers.append(o)
            else:
                b.writers = [o]
            b.readers = []

    def op(self, eng, fn, reads=(), writes=()):
        o = Op(eng, fn)
        self._track(o, reads, writes)
        self.q[eng].append(o)
        return o

    def dma(self, eng, fn, reads, writes, semb=None):
        o = Op(eng, fn)
        o.is_dma = True
        self._track(o, reads, writes)
        semb = semb if semb is not None else writes[0]
        if semb.sem is None:
            if self.sem_pool:
                semb.sem, semb.cnt = self.sem_pool.pop()
            else:
                semb.sem = self.new_sem("d")
            self.sem_bufs.append(semb)
        semb.cnt += 16
        o.dma_tok = (semb.sem, semb.cnt)
        self.q[eng].append(o)
        self.dmas.append(o)
        return o

    def coll(self, eng, fn, reads, writes, dummy_fn):
        o = Op(eng, fn)
        o.is_dma = True
        self._track(o, reads, [])
        if self.csem is None:
            self.csem = self.new_sem("c")
        self.ccnt += 1
        o.dma_tok = (self.csem, self.ccnt)
        o.k = -2
        self.q[eng].append(o)
        return self.op(eng, dummy_fn, [], writes)

    def wait_all(self, eng, ops):
        o = Op(eng, None)
        for p in ops:
            self._dep(o, p, True)
        self.q[eng].append(o)

    def barrier(self):
        lasts = []
        for e in ENGS:
            for o in reversed(self.q[e]):
                if not o.is_dma and o.fn is not None:
                    lasts.append(o)
                    break
        pend = list(self.dmas)
        self.dmas = []
        for b in self.sem_bufs:
            self.sem_pool.append((b.sem, b.cnt))
            b.sem = None
        self.sem_bufs = []
        for e in ENGS:
            o = Op(e, None)
            for p in lasts:
                if p.eng != e:
                    o.deps.append(p)
                    p.signal = True
            for p in pend:
                o.deps.append(p)
            self.q[e].append(o)

    def emit(self):
        nc = self.nc
        esems = {}
        for e in ENGS:
            k = 0
            for o in self.q[e]:
                if o.signal and not o.is_dma:
                    o.k = k
                    k += 1
            esems[e] = [self.new_sem(e) for _ in range(k // EPOCH + 1)]

        def tok(p):
            if p.is_dma:
                return p.dma_tok
            return (esems[p.eng][p.k // EPOCH], p.k % EPOCH + 1)

        def run(e, h):
            waited = {}
            for o in self.q[e]:
                for p in o.deps:
                    s, v = tok(p)
                    key = id(s)
                    if waited.get(key, 0) < v:
                        h.wait_ge(s, v)
                        waited[key] = v
                if o.fn is None:
                    continue
                ins = o.fn(h)
                if o.is_dma and o.k == -2:
                    ins.then_inc(o.dma_tok[0])
                    h.wait_ge(o.dma_tok[0], o.dma_tok[1])
                elif o.is_dma:
                    ins.then_inc(o.dma_tok[0], 16)
                elif o.signal:
                    s, v = tok(o)
                    ins.then_inc(s, 1)

        with nc.Block() as block:
            @block.tensor
            def _(h):
                run("pe", h)

            @block.scalar
            def _(h):
                run("act", h)

            @block.vector
            def _(h):
                run("dve", h)

            @block.gpsimd
            def _(h):
                run("pool", h)

            @block.sync
            def _(h):
                run("sp", h)


def _bucket_thresholds():
    n = np.arange(0, 128)
    nf = np.maximum(n, 1).astype(np.float32)
    large = 16 + (np.log(nf / np.float32(16)) / np.float32(math.log(128 / 16)) * np.float32(16)).astype(np.int32)
    large = np.minimum(large, 31)
    b = np.where(n < 16, n, large)
    lo = []
    for bb in range(1, 32):
        idx = np.nonzero(b >= bb)[0]
        lo.append(int(idx[0]) if len(idx) else 1 << 20)
    return lo


class Ctx:
    pass


class Builder:
    def __init__(self):
        self.nc = bass.Bass("TRN2", target_bir_lowering=False)
        self.S = Sched(self.nc)
        self.ins = {}
        self.outs = {}
        self.finals = []
        self.uid = 0
        self.off = self.SB_BASE
        self.peak = self.off
        self.ps = [self.nc.alloc_psum_tensor(f"psb{i}", [128, 512], F32) for i in range(8)]
        self.psb = [Buf(f"ps{i}") for i in range(8)]
        self.ones_bf = self.sb("ones", [128, 128], BF16)
        self.b_ones = Buf("ones")
        self.memset(self.ones_bf[:, :], 1.0, [self.b_ones])

    def inp(self, name, shape, dt=F32):
        t = self.nc.dram_tensor(name, list(shape), dt, kind="ExternalInput")
        self.ins[name] = t
        return t

    def outp(self, name, shape, dt=F32):
        t = self.nc.dram_tensor(name, list(shape), dt, kind="ExternalOutput")
        self.outs[name] = t
        return t

    SB_BASE = 16512
    SB_TOP = 229344

    def sb(self, name, shape, dt=F32):
        self.uid += 1
        sz = int(np.prod(shape[1:])) * (4 if dt in (F32, I32) else 2)
        sz = (sz + 31) // 32 * 32
        assert self.off + sz <= self.SB_TOP, f"SBUF overflow allocating {name} {shape}: off={self.off} sz={sz}"
        t = self.nc.alloc_sbuf_tensor_at(f"{name}_{self.uid}", list(shape), dt, offset=self.off)
        self.off += sz
        self.peak = max(self.peak, self.off)
        return t

    def mark(self):
        return self.off

    def release(self, mk):
        self.off = mk

    def mm(self, out, lhsT, rhs, start, stop, reads, writes):
        return self.S.op("pe", lambda h: h.matmul(out, lhsT=lhsT, rhs=rhs, start=start, stop=stop), reads, writes)

    def act(self, out, in_, func, reads, writes, scale=1.0, bias=0.0):
        return self.S.op("act", lambda h: h.activation(out=out, in_=in_, func=func, scale=scale, bias=bias), reads, writes)

    def tt(self, out, in0, in1, op, reads, writes, eng="dve"):
        return self.S.op(eng, lambda h: h.tensor_tensor(out=out, in0=in0, in1=in1, op=op), reads, writes)

    def ts(self, out, in0, s1, s2, op0, op1, reads, writes, eng="dve"):
        if s2 is None:
            return self.S.op(eng, lambda h: h.tensor_scalar(out=out, in0=in0, scalar1=s1, scalar2=None, op0=op0), reads, writes)
        return self.S.op(eng, lambda h: h.tensor_scalar(out=out, in0=in0, scalar1=s1, scalar2=s2, op0=op0, op1=op1), reads, writes)

    def stt(self, out, in0, scalar, in1, op0, op1, reads, writes, eng="dve"):
        return self.S.op(eng, lambda h: h.scalar_tensor_tensor(out=out, in0=in0, scalar=scalar, in1=in1, op0=op0, op1=op1), reads, writes)

    def copy(self, out, in_, reads, writes, eng="dve"):
        return self.S.op(eng, lambda h: h.tensor_copy(out=out, in_=in_), reads, writes)

    def recip(self, out, in_, reads, writes):
        return self.S.op("dve", lambda h: h.reciprocal(out=out, in_=in_), reads, writes)

    def memset(self, ap, val, writes, eng="pool"):
        return self.S.op(eng, lambda h: h.memset(ap, val), [], writes)

    def load(self, dst, src, buf, eng="sp", reads=()):
        return self.S.dma(eng, lambda h: h.dma_start(out=dst, in_=src), list(reads), [buf])

    def store(self, dst, src, srcbuf, eng="pool", final=False, dstbuf=None):
        o = self.S.dma(eng, lambda h: h.dma_start(out=dst, in_=src), [srcbuf],
                       [dstbuf] if dstbuf is not None else [Buf()], semb=srcbuf)
        if final:
            self.finals.append(o)
        return o

    def rstd(self, ss, n, out, tmp, reads, tmpbuf, outbuf):
        self.act(tmp, ss, AF.Ln, reads, [tmpbuf], scale=1.0 / n, bias=EPS)
        self.act(out, tmp, AF.Exp, [tmpbuf], [outbuf], scale=-0.5)

    def norm_work(self):
        w = Ctx()
        w.sq = self.sb("nsq", [128, 8, 512], BF16); w.b_sq = Buf()
        w.tmp = self.sb("ntmp", [128, 512], F32); w.b_tmp = Buf()
        w.rstd = self.sb("nrstd", [128, 512], F32); w.b_rstd = Buf()
        w.xn = self.sb("nxn", [128, 8, 512], F32); w.b_xn = Buf()
        return w

    def modnorm_tile(self, x, bx, t0, A, Bv, bmod, hdst, bh, w, pb=7):
        self.act(w.sq[:, :, :], x[:, :, t0:t0 + 512], AF.Square, [bx], [w.b_sq])
        for kc in range(8):
            self.mm(self.ps[pb][:, :], self.ones_bf[:, :], w.sq[:, kc, :], kc == 0, kc == 7, [w.b_sq, self.b_ones], [self.psb[pb]])
        self.rstd(self.ps[pb][:, :], float(D), w.rstd[:, :], w.tmp[:, :], [self.psb[pb]], w.b_tmp, w.b_rstd)
        self.tt(w.xn[:, :, :], x[:, :, t0:t0 + 512], w.rstd[:, None, :].to_broadcast([128, 8, 512]), ALU.mult, [bx, w.b_rstd], [w.b_xn])
        for kc in range(8):
            self.act(hdst[:, kc, :], w.xn[:, kc, :], AF.Identity, [w.b_xn, bmod], [bh], scale=A[:, kc:kc + 1], bias=Bv[:, kc:kc + 1])

    def mod_alloc(self):
        a = Ctx()
        a.cond = self.sb("cond", [128, 8], F32)
        a.bm = self.sb("bmod", [128, 72], F32)
        a.ng = self.sb("ng", [128, 3, 8], F32)
        a.modsb = self.sb("mod", [128, 72], F32)
        a.A = self.sb("modA", [128, 3, 8], F32)
        a.gate = self.sb("modG", [128, 3, 8], F32)
        return a

    def derive_mod(self, mod, b_mod, ng, b_ng, a):
        m = Ctx()
        m.A = a.A
        m.gate = a.gate
        m.mod = mod
        m.b = Buf("modder")
        for i in range(3):
            self.stt(m.A[:, i, :], mod[:, (3 * i + 1) * 8:(3 * i + 2) * 8], 1.0, ng[:, i, :], ALU.add, ALU.mult, [b_mod, b_ng], [m.b])
            self.ts(m.gate[:, i, :], mod[:, (3 * i + 2) * 8:(3 * i + 3) * 8], 1.0 if i == 1 else 0.5, None, ALU.mult, None, [b_mod], [m.b])
        m.B = lambda i: mod[:, (3 * i) * 8:(3 * i + 1) * 8]
        return m

    def ffn_work(self, nw):
        fw = Ctx()
        fw.h = self.sb("ffh", [128, 2, 8, 512], BF16)
        fw.b_h = [Buf(), Buf()]
        fw.act = self.sb("ffact", [128, NF, 1024], BF16)
        fw.b_act = [[Buf() for t in range(2)] for f in range(NF)]
        fw.wg = [self.sb(f"wg{i}", [128, 8, 2, 128], BF16) for i in range(3)]
        fw.b_wg = [Buf() for i in range(3)]
        fw.wd = [self.sb(f"wd{i}", [128, NF, 128], BF16) for i in range(2)]
        fw.b_wd = [Buf() for i in range(2)]
        fw.sg = [self.sb(f"sg{i}", [128, 512], F32) for i in range(2)]
        fw.b_sg = [Buf() for i in range(2)]
        fw.nw = nw
        return fw

    def ffn(self, x, bx, wgu_d, wdn_d, m, i, fw):
        A = m.A[:, i, :]
        Bv = m.B(i)
        gate = m.gate[:, i, :]
        for half in range(2):
            for tt in range(2):
                self.modnorm_tile(x, bx, half * 1024 + tt * 512, A, Bv, m.b, fw.h[:, tt, :, :], fw.b_h[tt], fw.nw)
            for f in range(NF):
                s = f % 3
                self.load(fw.wg[s][:, :, :, :], wgu_d[f, :, :, :, :], fw.b_wg[s], eng="pool")
                for tt in range(2):
                    par = (f * 2 + tt) % 2
                    pg, pu = par * 2, par * 2 + 1
                    for gu, pb in ((0, pg), (1, pu)):
                        for kc in range(8):
                            self.mm(self.ps[pb][:, :], fw.wg[s][:, kc, gu, :], fw.h[:, tt, kc, :], kc == 0, kc == 7, [fw.b_wg[s], fw.b_h[tt]], [self.psb[pb]])
                    self.act(fw.sg[par][:, :], self.ps[pg][:, :], AF.Silu, [self.psb[pg]], [fw.b_sg[par]])
                    self.tt(fw.act[:, f, tt * 512:(tt + 1) * 512], fw.sg[par][:, :], self.ps[pu][:, :], ALU.mult, [fw.b_sg[par], self.psb[pu]], [fw.b_act[f][tt]])
            for j in range(8):
                s = j % 2
                self.load(fw.wd[s][:, :, :], wdn_d[j, :, :, :], fw.b_wd[s], eng="pool")
                for tt in range(2):
                    pb = 4 + (j * 2 + tt) % 2
                    t0 = half * 1024 + tt * 512
                    for f in range(NF):
                        self.mm(self.ps[pb][:, :], fw.wd[s][:, f, :], fw.act[:, f, tt * 512:(tt + 1) * 512], f == 0, f == NF - 1, [fw.b_wd[s], fw.b_act[f][tt]], [self.psb[pb]])
                    self.stt(x[:, j, t0:t0 + 512], self.ps[pb][:, :], gate[:, j:j + 1], x[:, j, t0:t0 + 512], ALU.mult, ALU.add, [self.psb[pb], m.b, bx], [bx])


def emit_mod(Bd, cT, wmod, bmod, ngd, stage_bf, stage_bufs, a):
    cond = a.cond; b_cond = Buf()
    Bd.load(cond[:, :], cT[:, :], b_cond)
    Bd.act(cond[:, :], cond[:, :], AF.Silu, [b_cond], [b_cond])
    bm = a.bm; b_bm = Buf()
    Bd.load(bm[:, :], bmod[:, :], b_bm)
    ng = a.ng; b_ng = Buf()
    Bd.load(ng[:, :, :], ngd[:, :, :], b_ng)
    modsb = a.modsb; b_modsb = Buf()
    pm = 6
    for j9 in range(9):
        for hh in range(2):
            s = (j9 * 2 + hh) % 2
            stage = stage_bf[s]
            col0 = j9 * 1024 + hh * 512
            Bd.load(stage[:, :, :], wmod[0:128, :, col0:col0 + 512], stage_bufs[s])
            for jc in range(4):
                oc = j9 * 8 + hh * 4 + jc
                for kc in range(8):
                    Bd.mm(Bd.ps[pm][:, oc:oc + 1], stage[:, kc, jc * 128:(jc + 1) * 128], cond[:, kc:kc + 1], kc == 0, kc == 7, [stage_bufs[s], b_cond], [Bd.psb[pm]])
    Bd.tt(modsb[:, :], Bd.ps[pm][:, 0:72], bm[:, :], ALU.add, [Bd.psb[pm], b_bm], [b_modsb])
    return modsb, b_modsb, ng, b_ng


def emit_proj(Bd, x, bx, m, fw, di, do, final):
    S = Bd.S
    ps, psb = Bd.ps, Bd.psb
    ones = Bd.ones_bf
    b_ones = Bd.b_ones

    def wload(name, shape, dt, eng="pool"):
        t = Bd.sb(name, shape, dt); b = Buf(name)
        src = di[name]
        Bd.load(t[tuple(slice(None) for _ in shape)], src[(slice(0, shape[0]),) + tuple(slice(None) for _ in shape[1:])], b, eng=eng)
        return t, b
    win, b_win = wload("win", [128, 8, 1824], BF16)
    winv, b_winv = wload("winv", [128, 8, 384], BF16)
    wuq, b_wuq = wload("wuq", [128, 2, 576], BF16)
    wukvk, b_wukvk = wload("wukvk", [128, 6, 64], BF16)
    wukvv, b_wukvv = wload("wukvv", [128, 384], BF16)
    qan, b_qan = wload("qan", [128, 2], F32, "sp")
    kvan, b_kvan = wload("kvan", [128, 1], F32, "sp")
    hn, b_hn = wload("hn", [96, 4], F32, "sp")
    rot, b_rot = wload("rot", [96, 96], F32, "sp")
    freq, b_freq = wload("freq", [96, 1], F32, "sp")

    Ct = Bd.sb("ropeC", [96, T], F32); b_C = Buf()
    St = Bd.sb("ropeS", [96, T], F32); b_S = Buf()
    mk_rope = Bd.mark()
    ang = Bd.sb("ang", [96, T], F32); b_ang = Buf()
    rt = Bd.sb("rt", [96, T], F32); b_rt = Buf()
    ki = Bd.sb("ki", [96, T], I32); b_ki = Buf()
    posi = ki; b_posi = b_ki
    Bd.load(posi[:, :], di["pos"][0:1, :].partition_broadcast(96), b_posi)
    C1 = 6.28125
    C2 = TWO_PI - C1
    Bd.copy(ang[:, :], posi[:, :], [b_posi], [b_ang])
    Bd.ts(ang[:, :], ang[:, :], freq[:, 0:1], None, ALU.mult, None, [b_ang, b_freq], [b_ang])
    for (dst, bd, shift) in ((St, b_S, 0.0), (Ct, b_C, math.pi / 2)):
        Bd.ts(rt[:, :], ang[:, :], shift, 1.0 / TWO_PI, ALU.add, ALU.mult, [b_ang], [b_rt])
        Bd.copy(ki[:, :], rt[:, :], [b_rt], [b_ki])
        Bd.copy(rt[:, :], ki[:, :], [b_ki], [b_rt])
        Bd.stt(dst[:, :], rt[:, :], -C1, ang[:, :], ALU.mult, ALU.add, [b_rt, b_ang], [bd])
        Bd.stt(dst[:, :], rt[:, :], -C2, dst[:, :], ALU.mult, ALU.add, [b_rt, bd], [bd])
        Bd.ts(dst[:, :], dst[:, :], shift, math.pi, ALU.add, ALU.min, [bd], [bd])
        Bd.ts(dst[:, :], dst[:, :], -math.pi, None, ALU.max, None, [bd], [bd])
        Bd.act(dst[:, :], dst[:, :], AF.Sin, [bd], [bd])
    S.barrier()
    Bd.release(mk_rope)

    h2 = Bd.sb("h2", [128, 8, 512], BF16); b_h2 = Buf()
    cqn = Bd.sb("cqn", [128, 2, 512], BF16); b_cqn = Buf()
    ckvn = Bd.sb("ckvn", [128, 512], BF16); b_ckvn = Buf()
    sqw = [Bd.sb(f"sqw{i}", [128, 2, 512], BF16) for i in range(2)]; b_sqw = [Buf() for i in range(2)]
    rsw = [Bd.sb(f"rsw{i}", [128, 512], F32) for i in range(2)]; b_rsw = [Buf() for i in range(2)]
    xnw = [Bd.sb(f"xnw{i}", [96, 512], F32) for i in range(2)]; b_xnw = [Buf() for i in range(2)]
    t1w = [Bd.sb(f"t1w{i}", [96, 512], F32) for i in range(2)]; b_t1w = [Buf() for i in range(2)]

    def stg(name, shape):
        t = Bd.sb(name, shape, BF16); b = Buf()
        return [t, t], [b, b]
    st_qm, b_stqm = stg("stqm", [96, 6, 512])
    st_km, b_stkm = stg("stkm", [96, 6, 512])
    st_qs, b_stqs = stg("stqs", [64, 6, 512])
    st_ks, b_stks = stg("stks", [64, 2, 512])
    st_qc, b_stqc = stg("stqc", [64, 4, 512])
    st_kc, b_stkc = stg("stkc", [64, 4, 512])
    st_vm, b_stvm = stg("stvm", [128, 6, 65])
    st_vs, b_stvs = stg("stvs", [128, 2, 65])
    st_vc, b_stvc = stg("stvc", [128, 4, 65])
    Bd.memset(st_vs[0][:, :, :], 1.0, [b_stvs[0]])
    Bd.memset(st_vm[0][:, :, :], 1.0, [b_stvm[0]])
    Bd.memset(st_vc[0][:, :, :], 1.0, [b_stvc[0]])

    A = m.A[:, 1, :]
    Bv = m.B(1)
    cnt = [0]

    def nxt():
        cnt[0] += 1
        return cnt[0] % 2

    PSS = 6

    def head_chain(proj, pb, rows, gain, n, out_ap, outbuf, rope, t0, w, pss):
        pin = ps[pb][0:rows, :]
        st = [proj,
              lambda: Bd.act(sqw[w][0:rows, 0, :], pin, AF.Square, [psb[pb]], [b_sqw[w]]),
              lambda: Bd.mm(ps[pss][0:rows, :], ones[0:rows, 0:rows], sqw[w][0:rows, 0, :], True, True, [b_sqw[w], b_ones], [psb[pss]]),
              lambda: Bd.act(rsw[w][0:rows, :], ps[pss][0:rows, :], AF.Ln, [psb[pss]], [b_rsw[w]], scale=1.0 / n, bias=EPS),
              lambda: Bd.act(rsw[w][0:rows, :], rsw[w][0:rows, :], AF.Exp, [b_rsw[w]], [b_rsw[w]], scale=-0.5)]
        if not rope:
            st.append(lambda: Bd.stt(out_ap, pin, gain, rsw[w][0:rows, :], ALU.mult, ALU.mult, [psb[pb], b_rsw[w], b_hn], [outbuf]))
            return st
        st += [lambda: Bd.stt(xnw[w][:, :], pin, gain, rsw[w][0:rows, :], ALU.mult, ALU.mult, [psb[pb], b_rsw[w], b_hn], [b_xnw[w]]),
               lambda: Bd.mm(ps[pss][0:96, :], rot[:, :], xnw[w][:, :], True, True, [b_rot, b_xnw[w]], [psb[pss]]),
               lambda: Bd.tt(t1w[w][:, :], xnw[w][:, :], Ct[:, t0:t0 + 512], ALU.mult, [b_xnw[w], b_C], [b_t1w[w]]),
               lambda: Bd.tt(xnw[w][:, :], ps[pss][0:96, :], St[:, t0:t0 + 512], ALU.mult, [psb[pss], b_S], [b_xnw[w]]),
               lambda: Bd.tt(out_ap, t1w[w][:, :], xnw[w][:, :], ALU.add, [b_t1w[w], b_xnw[w]], [outbuf])]
        return st

    def lockstep(chains):
        for s in range(max(len(c) for c in chains)):
            for c in chains:
                if s < len(c):
                    c[s]()

    def heads_lockstep(nheads, pb0, mk_proj, rows, gain, n, out_fn, outbuf, rope, t0):
        for h0 in range(0, nheads, 2):
            chains = []
            for k in range(min(2, nheads - h0)):
                hh = h0 + k
                pb = pb0 + k
                chains.append(head_chain((lambda hh=hh, pb=pb: mk_proj(hh, pb)), pb, rows, gain, n, out_fn(hh), outbuf, rope, t0, k, 6 + k))
            lockstep(chains)

    def units(name):
        u = do[name]
        return u if isinstance(u, list) else [(u, 0, u.shape[0])]

    def store3(name, t0, src, srcbuf):
        o = do.get(name + "_off", 0)
        for (ap, h0, nh) in units(name):
            Bd.store(ap.rearrange("h p t -> p h t")[:, :, o + t0:o + t0 + 512], src[:, h0:h0 + nh, :], srcbuf, final=final, dstbuf=do.get(name + "_buf"))

    def storev(name, mblk, src, srcbuf):
        o = do.get(name + "_off", 0)
        for (ap, h0, nh) in units(name):
            Bd.store(ap.rearrange("h p m d -> p h m d")[:, :, o + mblk, :], src[:, h0:h0 + nh, :], srcbuf, final=final, dstbuf=do.get(name + "_buf"))

    for tt in range(4):
        t0 = tt * 512
        par = tt % 2
        Bd.modnorm_tile(x, bx, t0, A, Bv, m.b, h2, b_h2, fw.nw)
        for cc in range(2):
            for kc in range(8):
                Bd.mm(ps[cc][:, :], win[:, kc, cc * 128:(cc + 1) * 128], h2[:, kc, :], kc == 0, kc == 7, [b_win, b_h2], [psb[cc]])
        w = nxt()
        for cc in range(2):
            Bd.act(sqw[w][:, cc, :], ps[cc][:, :], AF.Square, [psb[cc]], [b_sqw[w]])
        for cc in range(2):
            Bd.mm(ps[PSS][:, :], ones[:, :], sqw[w][:, cc, :], cc == 0, cc == 1, [b_sqw[w], b_ones], [psb[PSS]])
        Bd.rstd(ps[PSS][:, :], 256.0, rsw[w][:, :], rsw[w][:, :], [psb[PSS]], b_rsw[w], b_rsw[w])
        for cc in range(2):
            Bd.stt(cqn[:, cc, :], ps[cc][:, :], qan[:, cc:cc + 1], rsw[w][:, :], ALU.mult, ALU.mult, [psb[cc], b_rsw[w], b_qan], [b_cqn])
        for kc in range(8):
            Bd.mm(ps[2][:, :], win[:, kc, 256:384], h2[:, kc, :], kc == 0, kc == 7, [b_win, b_h2], [psb[2]])
        w = nxt()
        Bd.act(sqw[w][:, 0, :], ps[2][:, :], AF.Square, [psb[2]], [b_sqw[w]])
        Bd.mm(ps[PSS][:, :], ones[:, :], sqw[w][:, 0, :], True, True, [b_sqw[w], b_ones], [psb[PSS]])
        Bd.rstd(ps[PSS][:, :], 128.0, rsw[w][:, :], rsw[w][:, :], [psb[PSS]], b_rsw[w], b_rsw[w])
        Bd.stt(ckvn[:, :], ps[2][:, :], kvan[:, 0:1], rsw[w][:, :], ALU.mult, ALU.mult, [psb[2], b_rsw[w], b_kvan], [b_ckvn])
        def proj_q(hh, pb):
            for kc in range(2):
                Bd.mm(ps[pb][0:96, :], wuq[:, kc, hh * 96:(hh + 1) * 96], cqn[:, kc, :], kc == 0, kc == 1, [b_wuq, b_cqn], [psb[pb]])
        heads_lockstep(6, 0, proj_q, 96, hn[0:96, 0:1], 96, lambda hh: st_qm[par][:, hh, :], b_stqm[par], True, t0)
        store3("qm", t0, st_qm[par], b_stqm[par])

        def proj_k(hh, pb):
            Bd.mm(ps[pb][0:64, :], wukvk[:, hh, :], ckvn[:, :], True, True, [b_wukvk, b_ckvn], [psb[pb]])
            for kc in range(8):
                Bd.mm(ps[pb][64:96, :], win[:, kc, 384:416], h2[:, kc, :], kc == 0, kc == 7, [b_win, b_h2], [psb[pb]])
        heads_lockstep(6, 2, proj_k, 96, hn[0:96, 1:2], 96, lambda hh: st_km[par][:, hh, :], b_stkm[par], True, t0)
        store3("km", t0, st_km[par], b_stkm[par])

        def proj_sq(hh, pb):
            for kc in range(8):
                Bd.mm(ps[pb][0:64, :], win[:, kc, 416 + hh * 64:416 + (hh + 1) * 64], h2[:, kc, :], kc == 0, kc == 7, [b_win, b_h2], [psb[pb]])
        heads_lockstep(6, 0, proj_sq, 64, hn[0:64, 2:3], 64, lambda hh: st_qs[par][:, hh, :], b_stqs[par], False, t0)
        store3("qs", t0, st_qs[par], b_stqs[par])

        def proj_sk(hh, pb):
            for kc in range(8):
                Bd.mm(ps[pb][0:64, :], win[:, kc, 800 + hh * 64:800 + (hh + 1) * 64], h2[:, kc, :], kc == 0, kc == 7, [b_win, b_h2], [psb[pb]])
        heads_lockstep(2, 2, proj_sk, 64, hn[0:64, 3:4], 64, lambda hh: st_ks[par][:, hh, :], b_stks[par], False, t0)
        store3("ks", t0, st_ks[par], b_stks[par])
        for hh in range(4):
            pb = hh % 2
            for kc in range(8):
                Bd.mm(ps[pb][0:64, :], win[:, kc, 1056 + hh * 64:1056 + (hh + 1) * 64], h2[:, kc, :], kc == 0, kc == 7, [b_win, b_h2], [psb[pb]])
            Bd.act(st_qc[par][:, hh, :], ps[pb][0:64, :], AF.Copy, [psb[pb]], [b_stqc[par]], scale=0.125)
        store3("qc", t0, st_qc[par], b_stqc[par])
        for hh in range(4):
            pb = 2 + hh % 2
            for kc in range(8):
                Bd.mm(ps[pb][0:64, :], win[:, kc, 1312 + hh * 64:1312 + (hh + 1) * 64], h2[:, kc, :], kc == 0, kc == 7, [b_win, b_h2], [psb[pb]])
            Bd.copy(st_kc[par][:, hh, :], ps[pb][0:64, :], [psb[pb]], [b_stkc[par]])
        store3("kc", t0, st_kc[par], b_stkc[par])
        for blk in range(4):
            mblk = tt * 4 + blk
            bp = blk % 2
            pb = 4 + bp
            Bd.mm(ps[pb][:, 0:384], ckvn[:, blk * 128:(blk + 1) * 128], wukvv[:, :], True, True, [b_ckvn, b_wukvv], [psb[pb]])
            Bd.act(st_vm[bp][:, :, 0:64], ps[pb][:, 0:384].rearrange("p (h d) -> p h d", h=6), AF.Copy, [psb[pb]], [b_stvm[bp]])
            storev("vm", mblk, st_vm[bp], b_stvm[bp])
            pb2 = bp
            for kc in range(8):
                Bd.mm(ps[pb2][:, 0:384], h2[:, kc, blk * 128:(blk + 1) * 128], winv[:, kc, :], kc == 0, kc == 7, [b_h2, b_winv], [psb[pb2]])
            Bd.copy(st_vs[bp][:, :, 0:64], ps[pb2][:, 0:128].rearrange("p (h d) -> p h d", h=2), [psb[pb2]], [b_stvs[bp]])
            Bd.copy(st_vc[bp][:, :, 0:64], ps[pb2][:, 128:384].rearrange("p (h d) -> p h d", h=4), [psb[pb2]], [b_stvc[bp]])
            storev("vs", mblk, st_vs[bp], b_stvs[bp])
            storev("vc", mblk, st_vc[bp], b_stvc[bp])


LO_B = _bucket_thresholds()
MLA_SCALE = 96.0 ** -0.5


def attn_consts(Bd, di, w, bias_cache=None, first=True):
    S = Bd.S
    a = Ctx()
    a.zer = Bd.sb("zer", [128, 128], BF16); a.b_zer = Buf()
    Bd.memset(a.zer[:, :], 0.0, [a.b_zer])
    a.zrhs = Bd.sb("zrhs", [128, 512], BF16); a.b_zrhs = Buf()
    Bd.memset(a.zrhs[:, :], 0.0, [a.b_zrhs])
    a.tri = Bd.sb("tri", [128, 2, 128], BF16); a.b_tri = Buf()
    Bd.load(a.tri[:, :, :], di["tricomp"][:, :, :], a.b_tri, eng="pool")
    a.mincl = Bd.sb("mincl", [128, 4, 128], F32); a.b_mincl = Buf()
    Bd.load(a.mincl[:, :, :], di["mincl"][:, :, :], a.b_mincl)
    a.mstr = Bd.sb("mstr", [128, 4, 128], F32); a.b_mstr = Buf()
    Bd.load(a.mstr[:, :, :], di["mstrict"][:, :, :], a.b_mstr)
    a.sel = Bd.sb("sel", [65, 64], F32); a.b_sel = Buf()
    Bd.memset(a.sel[:, :], 0.0, [a.b_sel])
    Bd.memset(a.sel[64:65, :], 1.0, [a.b_sel])
    a.vsink = Bd.sb("vsink", [1, 65], F32); a.b_vsink = Buf()
    Bd.memset(a.vsink[:, :], 0.0, [a.b_vsink])
    Bd.memset(a.vsink[0:1, 64:65], 1.0, [a.b_vsink])
    a.wsel = Bd.sb("wsel", [128, 4], F32); a.b_wsel = Buf()
    Bd.load(a.wsel[:, :], di["wsel"][:, :], a.b_wsel)
    a.onorm = Bd.sb("onorm", [64, 16], F32); a.b_onorm = Buf()
    Bd.load(a.onorm[:, :], di["onorm"][:, :], a.b_onorm)
    es = Bd.sb("es", [1, 6], F32); b_es = Buf()
    Bd.load(es[:, :], di["sinks"][:, :], b_es)
    Bd.act(es[:, :], es[:, :], AF.Exp, [b_es], [b_es])
    a.esr = Bd.sb("esr", [1, 6, 128], F32); a.b_esr = Buf()
    Bd.copy(a.esr[:, :, :], es[0:1, :, None].to_broadcast([1, 6, 128]), [b_es], [a.b_esr])
    a.bias = Bd.sb("swabias", [128, 2, 6, 128], F32); a.b_bias = Buf()
    if bias_cache is not None and not first:
        Bd.load(a.bias[:, :, :, :].rearrange("p w h q -> p (w h q)"), bias_cache[0].ap(), a.b_bias, reads=[bias_cache[1]])
    else:
        tb = w.e[0][:, 0:192].rearrange("p (b h) -> p b h", h=6); b_tb = w.b_e[0]
        Bd.load(w.e[0][:, 0:192], di["relb"][0:1, :].partition_broadcast(128), b_tb)
        dtb = w.e[1][:, 0:186].rearrange("p (b h) -> p b h", h=6); b_dtb = w.b_e[1]
        Bd.tt(dtb[:, :, :], tb[:, 1:32, :], tb[:, 0:31, :], ALU.subtract, [b_tb], [b_dtb])
        pqi = w.e[3][:, 256:384].bitcast(I32); b_pqi = w.b_e[3]
        Bd.load(pqi, di["posrow"][0:1, :].partition_broadcast(128), b_pqi)
        pki = w.e[3][:, 384:386].bitcast(I32); b_pki = w.b_e[3]
        Bd.load(pki, di["poscol"][:, :], b_pki)
        pq = w.e[2][:, 0:128]; b_pq = w.b_e[2]
        pk = w.e[2][:, 128:130]; b_pk = w.b_e[2]
        Bd.copy(pq, pqi, [b_pqi], [b_pq])
        Bd.copy(pk, pki, [b_pki], [b_pk])
        rel = w.ec[0][:, 0:256].rearrange("p (w q) -> p w q", w=2); b_rel = w.b_ec[0]
        for wi in range(2):
            Bd.ts(rel[:, wi, :], pq, pk[:, wi:wi + 1], None, ALU.subtract, None, [b_pq, b_pk], [b_rel])
        ind = w.ec[1][:, 0:256].rearrange("p (w q) -> p w q", w=2); b_ind = w.b_ec[1]
        for h in range(6):
            Bd.ts(a.bias[:, :, h, :], rel[:, :, :], 0.0, tb[:, 0, h:h + 1], ALU.mult, ALU.add, [b_rel, b_tb], [a.b_bias])
        for b in range(1, 32):
            Bd.ts(ind[:, :, :], rel[:, :, :], float(LO_B[b - 1]), None, ALU.is_ge, None, [b_rel], [b_ind])
            for h in range(6):
                Bd.stt(a.bias[:, :, h, :], ind[:, :, :], dtb[:, b - 1, h:h + 1], a.bias[:, :, h, :], ALU.mult, ALU.add, [b_ind, b_dtb, a.b_bias], [a.b_bias])
        val = w.e[1][:, 256:512].rearrange("p (w q) -> p w q", w=2); b_val = w.b_e[1]
        Bd.ts(val[:, :, :], rel[:, :, :], 0.0, None, ALU.is_ge, None, [b_rel], [b_val])
        Bd.ts(ind[:, :, :], rel[:, :, :], 128.0, None, ALU.is_ge, None, [b_rel], [b_ind])
        Bd.ts(ind[:, :, :], ind[:, :, :], -1.0, 1.0, ALU.mult, ALU.add, [b_ind], [b_ind])
        Bd.tt(val[:, :, :], val[:, :, :], ind[:, :, :], ALU.mult, [b_val, b_ind], [b_val])
        Bd.ts(ind[:, :, :], val[:, :, :], 1.0e4, -1.0e4, ALU.mult, ALU.add, [b_val], [b_ind])
        for h in range(6):
            Bd.tt(a.bias[:, :, h, :], a.bias[:, :, h, :], val[:, :, :], ALU.mult, [a.b_bias, b_val], [a.b_bias])
            Bd.tt(a.bias[:, :, h, :], a.bias[:, :, h, :], ind[:, :, :], ALU.add, [a.b_bias, b_ind], [a.b_bias])

        if bias_cache is not None:
            Bd.store(bias_cache[0].ap(), a.bias[:, :, :, :].rearrange("p w h q -> p (w h q)"), a.b_bias, eng="sp", dstbuf=bias_cache[1])
    return a


def attn_work(Bd):
    w = Ctx()
    w.kbuf = [Bd.sb(f"kbuf{i}", [96, 4, T], BF16) for i in range(2)]; w.b_k = [Buf() for _ in range(2)]
    w.vbuf = [Bd.sb(f"vbuf{i}", [128, 4, NBLK, 65], BF16) for i in range(2)]; w.b_v = [Buf() for _ in range(2)]
    w.og = Bd.sb("ogrp", [64, 6, 512], F32); w.b_og = [Buf() for _ in range(6)]
    w.mix = Bd.sb("mix", [64, 16, 512], BF16); w.b_mix = [Buf() for _ in range(16)]
    w.qm = Bd.sb("qmg", [96, 6, 512], BF16); w.b_qm = Buf()
    w.qc = Bd.sb("qcg", [64, 4, 512], BF16); w.b_qc = Buf()
    w.qs = Bd.sb("qsg", [64, 6, 512], BF16); w.b_qs = Buf()
    w.p = [Bd.sb(f"pw{i}", [128, 512], BF16) for i in range(2)]; w.b_p = [Buf() for _ in range(2)]
    w.e = [Bd.sb(f"ew{i}", [128, 512], F32) for i in range(4)]; w.b_e = [Buf() for _ in range(4)]
    w.sp = [Bd.sb(f"spw{i}", [128, 512], BF16) for i in range(4)]; w.b_sp = [Buf() for _ in range(4)]
    w.ec = [Bd.sb(f"ecw{i}", [128, 512], F32) for i in range(2)]; w.b_ec = [Buf() for _ in range(2)]
    w.a = w.p; w.b_a = w.b_p
    w.oa = Bd.sb("oa", [65, 512], F32); w.b_oa = Buf()
    w.rden = Bd.sb("rden", [64, 512], F32); w.b_rden = Buf()
    w.ksw = Bd.sb("ksw", [64, 2, 2, 512], BF16); w.b_ksw = Buf()
    w.vsw = Bd.sb("vsw", [128, 2, 2, 4, 65], BF16); w.b_vsw = Buf()
    w.sarg = [w.e[i][:, 0:384].rearrange("p (j q) -> p j q", j=3) for i in range(2)]; w.b_sarg = w.b_e[0:2]
    w.psw = [w.sp[i][:, 0:384].rearrange("p (j q) -> p j q", j=3) for i in range(2)]; w.b_psw = w.b_sp[0:2]
    w.gsq = w.sp[2][0:64, :]; w.b_gsq = w.b_sp[2]
    w.grs = w.e[2][0:64, :]; w.b_grs = w.b_e[2]
    wo = Bd.sb("wo", [64, 16, 128], BF16); bwo = Buf()
    w.wo = [wo, wo]; w.b_wo = [bwo, bwo]
    return w


def emit_attention(Bd, x, bx, m, ac, w, di):
    S = Bd.S
    ps, psb = Bd.ps, Bd.psb
    ones = Bd.ones_bf
    for G in (3, 2, 1, 0):
        t0 = G * 512
        nk = 4 * G + 4
        ntok = nk * 128
        nsteps = 16 * G + 16
        Bd.load(w.qm[:, :, :], di["qm"].rearrange("h p t -> p h t")[:, :, t0:t0 + 512], w.b_qm, reads=[di["b_q"]])
        Bd.load(w.qc[:, :, :], di["qc"].rearrange("h p t -> p h t")[:, :, t0:t0 + 512], w.b_qc, reads=[di["b_q"]])
        Bd.load(w.qs[:, :, :], di["qs"].rearrange("h p t -> p h t")[:, :, t0:t0 + 512], w.b_qs, reads=[di["b_q"]])

        def kvload(slot, kf, vf, h, rows):
            Bd.load(w.kbuf[slot][0:rows, :, 0:ntok], di[kf][h][0].rearrange("r p t -> p r t")[:, :, 0:ntok], w.b_k[slot], reads=[di[kf][h][1]])
            Bd.load(w.vbuf[slot][:, :, 0:nk, :], di[vf][h][0].rearrange("r p m d -> p r m d")[:, :, 0:nk, :], w.b_v[slot], reads=[di[vf][h][1]])

        def step_geom(i):
            kb = nsteps - 1 - i
            kw = kb - 16 * G
            c0 = (kw // 4) * 128 if kw >= 0 else 0
            return kb % 4, kb // 4, c0, (kw % 4 if kw >= 0 else None)

        for pair in range(3):
            hs = (2 * pair, 2 * pair + 1)
            for hi, h in enumerate(hs):
                kvload(hi, "kmf", "vmf", h, 96)
            PS = {0: (0, 1), 1: (2, 7)}
            POm = {0: 3, 1: 4}
            PT = {0: ((w.p[0], w.b_p[0]), (w.p[1], w.b_p[1])), 1: ((w.sp[2], w.b_sp[2]), (w.sp[3], w.b_sp[3]))}
            for hi in range(2):
                Bd.mm(ps[POm[hi]][0:65, :], ac.zer[:, 0:65], ac.zrhs[:, :], True, False, [ac.b_zer, ac.b_zrhs], [psb[POm[hi]]])

            def m_s(hi, i):
                rk, mb, c0, dm = step_geom(i)
                pb = PS[hi][i % 2]
                Bd.mm(ps[pb][:, c0:512], w.kbuf[hi][0:96, rk, mb * 128:(mb + 1) * 128], w.qm[0:96, hs[hi], c0:512], True, True, [w.b_k[hi], w.b_qm], [psb[pb]])

            def m_e(hi, i):
                rk, mb, c0, dm = step_geom(i)
                pb = PS[hi][i % 2]
                pt, bpt = PT[hi][i % 2]
                Bd.act(pt[:, c0:512], ps[pb][:, c0:512], AF.Exp, [psb[pb]], [bpt], scale=MLA_SCALE)
                if dm is not None:
                    Bd.tt(pt[:, c0:c0 + 128], pt[:, c0:c0 + 128], ac.mincl[:, dm, :], ALU.mult, [bpt, ac.b_mincl], [bpt])

            def m_pv(hi, i):
                rk, mb, c0, dm = step_geom(i)
                pt, bpt = PT[hi][i % 2]
                Bd.mm(ps[POm[hi]][0:65, c0:512], w.vbuf[hi][:, rk, mb, 0:65], pt[:, c0:512], False, i == nsteps - 1, [w.b_v[hi], bpt], [psb[POm[hi]]])
            for hi in range(2):
                m_s(hi, 0)
            for hi in range(2):
                m_e(hi, 0)
            for i in range(nsteps):
                if i + 1 < nsteps:
                    for hi in range(2):
                        m_s(hi, i + 1)
                for hi in range(2):
                    m_pv(hi, i)
                if i + 1 < nsteps:
                    for hi in range(2):
                        m_e(hi, i + 1)
            for hi, h in enumerate(hs):
                po = POm[hi]
                Bd.act(w.oa[:, :], ps[po][0:65, :], AF.Copy, [psb[po]], [w.b_oa])
                Bd.mm(ps[5][0:64, :], ac.sel[:, :], w.oa[:, :], True, True, [ac.b_sel, w.b_oa], [psb[5]])
                Bd.recip(w.rden[:, :], ps[5][0:64, :], [psb[5]], [w.b_rden])
                Bd.tt(w.og[:, h, :], w.oa[0:64, :], w.rden[:, :], ALU.mult, [w.b_oa, w.b_rden], [w.b_og[h]])
        group_norm(Bd, w, ac, 0, 6, 384.0, G)

        Bd.load(w.ksw[:, :, 1, :], di["ks_own"].rearrange("h p t -> p h t")[:, :, 128 + t0:128 + t0 + 512], w.b_ksw, reads=[di["b_src"]])
        Bd.load(w.vsw[:, :, 1, :, :], di["vs_own"].rearrange("h p m d -> p h m d")[:, :, 1 + 4 * G:5 + 4 * G, :], w.b_vsw, reads=[di["b_src"]])
        for c in range(4):
            ko = 128 + t0 if c < 3 else t0
            vo = 1 + 4 * G if c < 3 else 4 * G
            kcand = w.kbuf[0][0:64, c, 0:1024].rearrange("p (k t) -> p k t", k=2)
            vcand = w.vbuf[0][:, c, 0:8, :].rearrange("p (k m) d -> p k m d", k=2)
            Bd.load(kcand, di["ksf"].rearrange("r h p t -> p r h t")[:, c, :, ko:ko + 512], w.b_k[0], reads=[di["b_ks"]])
            Bd.load(vcand, di["vsf"].rearrange("r h p m d -> p r h m d")[:, c, :, vo:vo + 4, :], w.b_v[0], reads=[di["b_vs"]])
        for c in range(4):
            kcand = w.kbuf[0][0:64, c, 0:1024].rearrange("p (k t) -> p k t", k=2)
            vcand = w.vbuf[0][:, c, 0:8, :].rearrange("p (k m) d -> p k m d", k=2)
            if c == 0:
                Bd.ts(w.ksw[:, :, 0, :], kcand, ac.wsel[0:64, 0:1], None, ALU.mult, None, [w.b_k[0], ac.b_wsel], [w.b_ksw])
                Bd.ts(w.vsw[:, :, 0, :, :], vcand, ac.wsel[:, 0:1], None, ALU.mult, None, [w.b_v[0], ac.b_wsel], [w.b_vsw])
            else:
                Bd.stt(w.ksw[:, :, 0, :], kcand, ac.wsel[0:64, c:c + 1], w.ksw[:, :, 0, :], ALU.mult, ALU.add, [w.b_k[0], ac.b_wsel, w.b_ksw], [w.b_ksw])
                Bd.stt(w.vsw[:, :, 0, :, :], vcand, ac.wsel[:, c:c + 1], w.vsw[:, :, 0, :, :], ALU.mult, ALU.add, [w.b_v[0], ac.b_wsel, w.b_vsw], [w.b_vsw])
        for blk in range(4):
            q0 = blk * 128
            for kvh in range(2):
                pso = 3 + kvh
                Bd.mm(ps[pso][0:65, 0:384], ac.zer[:, 0:65], ac.zrhs[:, 0:384], True, False, [ac.b_zer, ac.b_zrhs], [psb[pso]])
                for wh in range(2):
                    pss = (blk * 4 + kvh * 2 + wh) % 2
                    for j in range(3):
                        hq = kvh * 3 + j
                        Bd.mm(ps[pss][:, j * 128:(j + 1) * 128], w.ksw[:, kvh, wh, q0:q0 + 128], w.qs[:, hq, q0:q0 + 128], True, True, [w.b_ksw, w.b_qs], [psb[pss]])
                    Bd.stt(w.sarg[pss][:, :, :], ps[pss][:, 0:384].rearrange("p (j q) -> p j q", j=3), 0.125, ac.bias[:, wh, kvh * 3:(kvh + 1) * 3, :], ALU.mult, ALU.add, [psb[pss], ac.b_bias], [w.b_sarg[pss]])
                    Bd.act(w.psw[pss][:, :, :], w.sarg[pss][:, :, :], AF.Exp, [w.b_sarg[pss]], [w.b_psw[pss]])
                    for j in range(3):
                        Bd.mm(ps[pso][0:65, j * 128:(j + 1) * 128], w.vsw[:, kvh, wh, blk, 0:65], w.psw[pss][:, j, :], False, False, [w.b_vsw, w.b_psw[pss]], [psb[pso]])
                for j in range(3):
                    hq = kvh * 3 + j
                    Bd.mm(ps[pso][0:65, j * 128:(j + 1) * 128], ac.vsink[0:1, :], ac.esr[0:1, hq, :], False, True, [ac.b_vsink, ac.b_esr], [psb[pso]])
                Bd.act(w.oa[:, 0:384], ps[pso][0:65, 0:384], AF.Copy, [psb[pso]], [w.b_oa])
                Bd.mm(ps[5][0:64, 0:384], ac.sel[:, :], w.oa[:, 0:384], True, True, [ac.b_sel, w.b_oa], [psb[5]])
                Bd.recip(w.rden[:, 0:384], ps[5][0:64, 0:384], [psb[5]], [w.b_rden])
                for j in range(3):
                    hq = kvh * 3 + j
                    Bd.tt(w.og[:, hq, q0:q0 + 128], w.oa[0:64, j * 128:(j + 1) * 128], w.rden[:, j * 128:(j + 1) * 128], ALU.mult, [w.b_oa, w.b_rden], [w.b_og[hq]])
        group_norm(Bd, w, ac, 6, 6, 384.0, G)

        for pair in range(2):
            hs = (2 * pair, 2 * pair + 1)
            for hi, h in enumerate(hs):
                kvload(hi, "kcf", "vcf", h, 64)
            PZ = {0: (0, 1), 1: (2, 7)}
            PC = {0: 3, 1: 4}
            PO = {0: 5, 1: 6}
            for hi in range(2):
                Bd.mm(ps[PC[hi]][:, :], ac.zer[:, :], ac.zrhs[:, :], True, False, [ac.b_zer, ac.b_zrhs], [psb[PC[hi]]])
                Bd.mm(ps[PO[hi]][0:64, :], ac.zer[:, 0:64], ac.zrhs[:, :], True, False, [ac.b_zer, ac.b_zrhs], [psb[PO[hi]]])

            def s_z(hi, i):
                h = hs[hi]
                rk, mb, c0, dm = step_geom(i)
                pz = PZ[hi][i % 2]
                Bd.mm(ps[pz][:, c0:512], w.kbuf[hi][0:64, rk, mb * 128:(mb + 1) * 128], w.qc[0:64, h, c0:512], True, True, [w.b_k[hi], w.b_qc], [psb[pz]])

            def s_e(hi, i):
                rk, mb, c0, dm = step_geom(i)
                pz = PZ[hi][i % 2]
                ew = hi * 2 + i % 2
                Bd.act(w.e[ew][:, c0:512], ps[pz][:, c0:512], AF.Exp, [psb[pz]], [w.b_e[ew]])
                if dm is not None:
                    Bd.tt(w.e[ew][:, c0:c0 + 128], w.e[ew][:, c0:c0 + 128], ac.mstr[:, dm, :], ALU.mult, [w.b_e[ew], ac.b_mstr], [w.b_e[ew]])

            def s_sp(hi, i):
                rk, mb, c0, dm = step_geom(i)
                ew = hi * 2 + i % 2
                Bd.act(w.sp[ew][:, c0:512], w.e[ew][:, c0:512], AF.Ln, [w.b_e[ew]], [w.b_sp[ew]], bias=1.0)

            def s_tri(hi, i):
                rk, mb, c0, dm = step_geom(i)
                ew = hi * 2 + i % 2
                Bd.mm(ps[PC[hi]][:, c0:512], ac.tri[:, 0, :], w.sp[ew][:, c0:512], False, False, [ac.b_tri, w.b_sp[ew]], [psb[PC[hi]]])

            def s_ec(hi, i):
                rk, mb, c0, dm = step_geom(i)
                Bd.act(w.ec[hi][:, c0:512], ps[PC[hi]][:, c0:512], AF.Exp, [psb[PC[hi]]], [w.b_ec[hi]], scale=-1.0)

            def s_a(hi, i):
                rk, mb, c0, dm = step_geom(i)
                ew = hi * 2 + i % 2
                Bd.tt(w.a[hi][:, c0:512], w.e[ew][:, c0:512], w.ec[hi][:, c0:512], ALU.mult, [w.b_e[ew], w.b_ec[hi]], [w.b_a[hi]])

            def s_co(hi, i):
                rk, mb, c0, dm = step_geom(i)
                ew = hi * 2 + i % 2
                pc, po = PC[hi], PO[hi]
                Bd.mm(ps[pc][:, c0:512], ac.tri[:, 1, :], w.sp[ew][:, c0:512], False, i == nsteps - 1, [ac.b_tri, w.b_sp[ew]], [psb[pc]])
                Bd.mm(ps[po][0:64, c0:512], w.vbuf[hi][:, rk, mb, 0:64], w.a[hi][:, c0:512], False, i == nsteps - 1, [w.b_v[hi], w.b_a[hi]], [psb[po]])
            for hi in range(2):
                s_z(hi, 0)
            for hi in range(2):
                s_e(hi, 0)
            for hi in range(2):
                s_sp(hi, 0)
            for i in range(nsteps):
                nxt_ = i + 1 < nsteps
                for hi in range(2):
                    s_tri(hi, i)
                if nxt_:
                    for hi in range(2):
                        s_z(hi, i + 1)
                for hi in range(2):
                    s_ec(hi, i)
                if nxt_:
                    for hi in range(2):
                        s_e(hi, i + 1)
                for hi in range(2):
                    s_a(hi, i)
                if nxt_:
                    for hi in range(2):
                        s_sp(hi, i + 1)
                for hi in range(2):
                    s_co(hi, i)
            for hi, h in enumerate(hs):
                Bd.act(w.og[:, h, :], ps[PO[hi]][0:64, :], AF.Copy, [psb[PO[hi]]], [w.b_og[h]])
        group_norm(Bd, w, ac, 12, 4, 256.0, G)

        for j in range(8):
            s = j % 2
            Bd.load(w.wo[s][:, :, :], di["wout"][j, :, :, :], w.b_wo[s], eng="pool")
            pb = j % 2
            for hh in range(16):
                Bd.mm(ps[pb][:, :], w.wo[s][:, hh, :], w.mix[:, hh, :], hh == 0, hh == 15, [w.b_wo[s], w.b_mix[hh]], [psb[pb]])
            Bd.stt(x[:, j, t0:t0 + 512], ps[pb][:, :], m.gate[:, 1, j:j + 1], x[:, j, t0:t0 + 512], ALU.mult, ALU.add, [psb[pb], m.b, bx], [bx])


DEBUG = False


def group_norm(Bd, w, ac, h0, nh, width, G=0):
    ps, psb = Bd.ps, Bd.psb
    if DEBUG:
        for i in range(nh):
            Bd.store(Bd.outs["dbg_og"][G, :, h0 + i, :], w.og[:, i, :], w.b_og[i], eng="sp", final=True)
    for i in range(nh):
        Bd.act(w.gsq[:, :], w.og[:, i, :], AF.Square, [w.b_og[i]], [w.b_gsq])
        Bd.mm(ps[7][0:64, :], Bd.ones_bf[0:64, 0:64], w.gsq[:, :], i == 0, i == nh - 1, [w.b_gsq, Bd.b_ones], [psb[7]])
    Bd.rstd(ps[7][0:64, :], width, w.grs[:, :], w.grs[:, :], [psb[7]], w.b_grs, w.b_grs)
    for i in range(nh):
        Bd.stt(w.mix[:, h0 + i, :], w.og[:, i, :], ac.onorm[:, h0 + i:h0 + i + 1], w.grs[:, :], ALU.mult, ALU.mult, [w.b_og[i], ac.b_onorm, w.b_grs], [w.b_mix[h0 + i]])


LAYER_IN = dict(wmod=[129, 8, 9216], bmod=[128, 72], ng=[128, 3, 8], wgu1=[NF + 1, 128, 8, 2, 128], wdn1=[9, 128, NF, 128],
                win=[129, 8, 1824], winv=[129, 8, 384], qan=[128, 2], kvan=[128, 1], wuq=[128, 2, 576], wukvk=[128, 6, 64],
                wukvv=[128, 384], hnorms=[96, 4], wout=[9, 64, 16, 128], onorm=[64, 16], sinks=[1, 6],
                wgu2=[NF + 1, 128, 8, 2, 128], wdn2=[9, 128, NF, 128])
RG = [[0, 1, 2, 3], [4, 5, 6, 7]]


def build_F():
    Bd = Builder()
    S = Bd.S
    nc = Bd.nc
    xT = Bd.inp("xT", [D, T])
    xoT = Bd.outp("xoT", [D, T])
    cT = Bd.inp("cT", [128, 8])
    shared_in = dict(pos=Bd.inp("pos", [1, T], I32), freq=Bd.inp("freq", [96, 1]), rot=Bd.inp("rotT", [96, 96]),
                     tricomp=Bd.inp("tricomp", [128, 2, 128]), mincl=Bd.inp("mincl", [128, 4, 128]),
                     mstrict=Bd.inp("mstrict", [128, 4, 128]), relb=Bd.inp("relb", [1, 192]),
                     posrow=Bd.inp("posrow", [1, 128], I32), poscol=Bd.inp("poscol", [128, 2], I32),
                     wsel=Bd.inp("wsel", [128, 4]))
    L = [{k: Bd.inp(f"{k}_{l}", shp) for k, shp in LAYER_IN.items()} for l in range(2)]

    x = Bd.sb("x", [128, 8, T], F32); bx = Buf("x")
    for kc in range(8):
        Bd.load(x[:, kc, :], xT[kc * 128:(kc + 1) * 128, :], bx)
    ma = Bd.mod_alloc()
    zt = Bd.sb("zt", [128, 1105], BF16); b_zt = Buf()
    Bd.memset(zt[:, :], 0.0, [b_zt])
    mk0 = Bd.mark()
    bias_cache = (nc.dram_tensor("bias_scr", [128, 2 * 6 * 128], F32), Buf("bias_scr"))
    for l in range(2):
        W = L[l]
        def dt2(name, rows, cols):
            return nc.dram_tensor(f"{name}_{l}", [rows, cols], BF16)
        q_km = dt2("sq_m", 6 * 96, T); q_qs = dt2("sq_s", 6 * 64, T); q_qc = dt2("sq_c", 4 * 64, T)
        UR = 192
        units_s, units_g = [], []

        units_b = []

        def unit():
            i = len(units_s)
            units_s.append(dt2(f"su{i}", UR, T))
            units_g.append(dt2(f"gu{i}", 4 * UR, T))
            units_b.append(Buf(f"gu{i}"))
            return units_s[-1].ap(), units_g[-1].ap().rearrange("(r n) t -> r n t", r=4)
        o_km, g_km, o_kc, g_kc, o_vm, g_vm, o_vc, g_vc = [], [], [], [], [], [], [], []
        for i in range(3):
            su, gu = unit()
            o_km.append((su[0:192, :].rearrange("(h p) t -> h p t", h=2), 2 * i, 2))
            g_km += [(gu[:, hh * 96:(hh + 1) * 96, :], units_b[-1]) for hh in range(2)]
        for i in range(2):
            su, gu = unit()
            o_kc.append((su[0:128, :].rearrange("(h p) t -> h p t", h=2), 2 * i, 2))
            g_kc += [(gu[:, hh * 64:(hh + 1) * 64, :], units_b[-1]) for hh in range(2)]
        for (ol, gl, n) in ((o_vm, g_vm, 3), (o_vc, g_vc, 2)):
            for i in range(n):
                su, gu = unit()
                ol.append((su[0:130, :].rearrange("n t -> (n t)").rearrange("(h p m d) -> h p m d", h=2, p=128, d=65), 2 * i, 2))
                gv = gu[:, 0:130, :].rearrange("r n t -> r (n t)").rearrange("r (h p m d) -> r h p m d", h=2, p=128, d=65)
                gl += [(gv[:, hh, :, :, :], units_b[-1]) for hh in range(2)]
        su, gu = unit()
        o_ks = su[0:136, :].rearrange("n t -> (n t)").rearrange("(h p u) -> h p u", h=2, p=64)
        g_ks = gu[:, 0:136, :].rearrange("r n t -> r (n t)").rearrange("r (h p u) -> r h p u", h=2, p=64)
        b_gks = units_b[-1]
        su, gu = unit()
        o_vs = su[0:139, :].rearrange("n t -> (n t)")[0:2 * 128 * 17 * 65].rearrange("(h p m d) -> h p m d", h=2, p=128, d=65)
        g_vs = gu[:, 0:139, :].rearrange("r n t -> r (n t)")[:, 0:2 * 128 * 17 * 65].rearrange("r (h p m d) -> r h p m d", h=2, p=128, d=65)
        b_gvs = units_b[-1]
        b_src = Buf("src", multi=True)
        b_q = Buf("qscr", multi=True)
        b_g = Buf("gath")
        Bd.store(o_ks.rearrange("h p u -> p h u")[:, :, 0:128], zt[0:64, 0:256].rearrange("p (h u) -> p h u", h=2), b_zt, dstbuf=b_src)
        Bd.store(o_vs.rearrange("h p m d -> p h m d")[:, :, 0, :], zt[:, 0:130].rearrange("p (h d) -> p h d", h=2), b_zt, dstbuf=b_src)
        nw = Bd.norm_work()
        fw = Bd.ffn_work(nw)
        stage = [fw.act[:, 0:8, :].bitcast(F32), fw.act[:, 8:16, :].bitcast(F32)]
        modsb, b_modsb, ng, b_ng = emit_mod(Bd, cT, W["wmod"], W["bmod"], W["ng"], stage, [Buf(), Buf()], ma)
        m = Bd.derive_mod(modsb, b_modsb, ng, b_ng, ma)
        S.barrier()
        Bd.ffn(x, bx, W["wgu1"], W["wdn1"], m, 0, fw)
        S.barrier()
        Bd.release(mk0)
        nw = Bd.norm_work()
        fw2 = Ctx(); fw2.nw = nw
        di = dict(win=W["win"], winv=W["winv"], qan=W["qan"], kvan=W["kvan"], wuq=W["wuq"], wukvk=W["wukvk"], wukvv=W["wukvv"],
                  hn=W["hnorms"], pos=shared_in["pos"], freq=shared_in["freq"], rot=shared_in["rot"])
        do = dict(qm=q_km.ap().rearrange("(h p) t -> h p t", h=6), qs=q_qs.ap().rearrange("(h p) t -> h p t", h=6),
                  qc=q_qc.ap().rearrange("(h p) t -> h p t", h=4),
                  km=o_km, kc=o_kc, ks=o_ks, ks_off=128, vm=o_vm, vc=o_vc, vs=o_vs, vs_off=1,
                  qm_buf=b_q, qs_buf=b_q, qc_buf=b_q, km_buf=b_src, kc_buf=b_src, ks_buf=b_src, vm_buf=b_src, vc_buf=b_src, vs_buf=b_src)
        emit_proj(Bd, x, bx, m, fw2, di, do, final=False)
        S.barrier()
        Bd.release(mk0)
        da = dict(shared_in)
        da.update(onorm=W["onorm"], sinks=W["sinks"], wout=W["wout"],
                  qm=do["qm"], qs=do["qs"], qc=do["qc"], b_q=b_q, b_src=b_src, b_ks=b_gks, b_vs=b_gvs,
                  kmf=g_km, vmf=g_vm, kcf=g_kc, vcf=g_vc, ksf=g_ks, vsf=g_vs, ks_own=o_ks, vs_own=o_vs)
        w = attn_work(Bd)
        ac = attn_consts(Bd, da, w, bias_cache, first=(l == 0))
        for ui in (0, 5, 1, 6, 2, 7, 10, 11, 3, 8, 4, 9):
            S.coll("pool", (lambda a_, b_: lambda h: h.collective_compute("AllGather", ALU.bypass, replica_groups=RG, ins=[a_.ap()], outs=[b_.ap()]))(units_s[ui], units_g[ui]),
                   [b_src], [units_b[ui]], lambda h: h.memset(zt[0:1, 0:8], 0.0))
        emit_attention(Bd, x, bx, m, ac, w, da)
        S.barrier()
        Bd.release(mk0)
        nw = Bd.norm_work()
        fw = Bd.ffn_work(nw)
        Bd.ffn(x, bx, W["wgu2"], W["wdn2"], m, 2, fw)
        S.barrier()
        Bd.release(mk0)
    for kc in range(8):
        Bd.store(xoT[kc * 128:(kc + 1) * 128, :], x[:, kc, :], bx, eng="sp", final=True)
    S.wait_all("sp", Bd.finals)
    print("F sbuf peak", Bd.peak, {e: len(S.q[e]) for e in ENGS})
    S.emit()
    return Bd


_CACHE = {}


def _get(name, fn):
    if name not in _CACHE:
        _CACHE[name] = fn()
    return _CACHE[name]


def _core_tokens(r):
    idx = (np.arange(NBLK)[:, None] * 4 + r) * 128 + np.arange(128)[None, :]
    return idx.reshape(-1)


def _freq_rot():
    half = 16
    fr = (np.float32(10000.0) ** (-np.arange(half, dtype=np.float32) / np.float32(half))).astype(np.float32)
    freq = np.zeros((96, 1), np.float32)
    freq[64:80, 0] = fr
    freq[80:96, 0] = fr
    rotT = np.zeros((96, 96), np.float32)
    for i in range(16):
        rotT[80 + i, 64 + i] = -1.0
        rotT[64 + i, 80 + i] = 1.0
    return freq, rotT


def _ffn_layout(wgu_l, wdn_l):
    f = np.ascontiguousarray
    wgu = wgu_l.reshape(8, 128, 2, NF, 128)
    wdn = wdn_l.reshape(NF, 128, 8, 128)
    return f(wgu.transpose(3, 1, 0, 2, 4)), f(wdn.transpose(2, 1, 0, 3))


def layer_inputs(inp, l):
    f = np.ascontiguousarray
    d = {}
    d["wmod"] = f(inp["w_mod"][l].reshape(8, 128, 9216).transpose(1, 0, 2))
    d["bmod"] = f(inp["b_mod"][l].reshape(72, 128).T)
    d["ng"] = f(inp["norm_g"][l].reshape(3, 8, 128).transpose(2, 0, 1))
    d["wgu1"], d["wdn1"] = _ffn_layout(inp["w_ffn1_gu"][l], inp["w_ffn1_down"][l])
    d["wgu2"], d["wdn2"] = _ffn_layout(inp["w_ffn2_gu"][l], inp["w_ffn2_down"][l])
    win = inp["w_in"][l]
    d["win"] = f(win.reshape(8, 128, 1824).transpose(1, 0, 2))
    winv = np.concatenate([win[:, 928:1056], win[:, 1568:1824]], axis=1)
    d["winv"] = f(winv.reshape(8, 128, 384).transpose(1, 0, 2))
    d["qan"] = f(inp["q_a_norm"][l].reshape(2, 128).T)
    d["kvan"] = f(inp["kv_a_norm"][l].reshape(128, 1))
    d["wuq"] = f(inp["w_uq"][l].reshape(2, 128, 576).transpose(1, 0, 2))
    wukv = inp["w_ukv"][l].reshape(128, 6, 128)
    d["wukvk"] = f(wukv[:, :, 0:64])
    d["wukvv"] = f(wukv[:, :, 64:128].reshape(128, 384))
    hn = np.zeros((96, 4), np.float32)
    hn[:, 0] = inp["mla_q_norm"][l]
    hn[:, 1] = inp["mla_k_norm"][l]
    hn[0:64, 2] = inp["swa_q_norm"][l]
    hn[0:64, 3] = inp["swa_k_norm"][l]
    d["hnorms"] = hn
    d["wout"] = f(inp["w_out"][l].reshape(16, 64, 8, 128).transpose(2, 1, 0, 3))
    d["onorm"] = f(inp["out_norm"][l].reshape(16, 64).T)
    d["sinks"] = f(inp["sinks"][l].reshape(1, 6))
    return d


def _masks(r):
    j = np.arange(128)[:, None]
    q = np.arange(128)[None, :]
    mi = np.zeros((128, 4, 128), np.float32)
    ms = np.zeros((128, 4, 128), np.float32)
    for d in range(4):
        if d < r:
            mi[:, d, :] = 1.0
            ms[:, d, :] = 1.0
        elif d == r:
            mi[:, d, :] = (j <= q)
            ms[:, d, :] = (j < q)
    tc = np.zeros((128, 2, 128), np.float32)
    tc[:, 0, :] = (j >= q)
    tc[:, 1, :] = (j < q)
    return mi, ms, tc


CORES = [(b, r) for b in range(2) for r in range(4)]
_PAD_KEYS = ("wmod", "wgu1", "wdn1", "wgu2", "wdn2", "win", "winv", "wout")


def _pad(a, ci):
    return np.concatenate([a, np.full((1,) + a.shape[1:], float(ci), a.dtype)], axis=0)


def kernel(**inp):
    inp = {k: np.asarray(v) for k, v in inp.items()}
    prog = _get("F", build_F)
    x = inp["x"]
    pos = inp["positions"]
    freq, rotT = _freq_rot()
    lay = [layer_inputs(inp, l) for l in range(2)]
    in_maps = []
    for ci, (b, r) in enumerate(CORES):
        mi, ms, tc = _masks(r)
        wsel = np.zeros((128, 4), np.float32)
        wsel[:, (r + 3) % 4] = 1.0
        dct = dict(xT=np.ascontiguousarray(x[b][_core_tokens(r)].T),
                   cT=np.ascontiguousarray(inp["c"][b].reshape(8, 128).T),
                   pos=np.ascontiguousarray(pos[b][_core_tokens(r)].reshape(1, T).astype(np.int32)),
                   freq=freq, rotT=rotT, tricomp=tc, mincl=mi, mstrict=ms,
                   relb=np.ascontiguousarray(inp["rel_bias"].reshape(1, 192)),
                   posrow=np.ascontiguousarray(pos[b][(4 + r) * 128:(5 + r) * 128].reshape(1, 128).astype(np.int32)),
                   poscol=np.ascontiguousarray(np.stack([pos[b][(3 + r) * 128:(4 + r) * 128], pos[b][(4 + r) * 128:(5 + r) * 128]], axis=1).astype(np.int32)),
                   wsel=wsel)
        for l in range(2):
            for k, a in lay[l].items():
                dct[f"{k}_{l}"] = _pad(a, ci) if k in _PAD_KEYS else a
        in_maps.append(dct)
    res = run_bass_kernel_spmd(prog.nc, in_maps, core_ids=list(range(NCORES)))
    out = np.zeros((2, 8192, 1024), np.float32)
    for ci, (b, r) in enumerate(CORES):
        out[b][_core_tokens(r)] = res.results[ci]["xoT"].T
    return out
```

```python
import math
import numpy as np
import ml_dtypes
import concourse.bass as bass
import concourse.mybir as mybir
from concourse.bass_utils import run_bass_kernel_spmd

F32 = mybir.dt.float32
BF16 = mybir.dt.bfloat16
I32 = mybir.dt.int32
AF = mybir.ActivationFunctionType
ALU = mybir.AluOpType

T = 2048
NBLK = 16
D = 1024
DFF = 2816
NF = 22
EPS = 1e-6
NCORES = 8
TWO_PI = 2.0 * math.pi

ENGS = ("pe", "act", "dve", "pool", "sp")
EPOCH = 30000


class Buf:
    __slots__ = ("name", "writers", "readers", "sem", "cnt", "multi")

    def __init__(self, name="", multi=False):
        self.name = name
        self.writers = []
        self.readers = []
        self.sem = None
        self.cnt = 0
        self.multi = multi


class Op:
    __slots__ = ("eng", "fn", "deps", "signal", "k", "dma_tok", "is_dma")

    def __init__(self, eng, fn):
        self.eng = eng
        self.fn = fn
        self.deps = []
        self.signal = False
        self.k = -1
        self.dma_tok = None
        self.is_dma = False


class Sched:
    def __init__(self, nc, same_engine_raw=True):
        self.nc = nc
        self.q = {e: [] for e in ENGS}
        self.same_engine_raw = same_engine_raw
        self.nsem = 0
        self.dmas = []
        self.csem = None
        self.ccnt = 0
        self.sem_pool = []
        self.sem_bufs = []

    def new_sem(self, name):
        self.nsem += 1
        return self.nc.alloc_semaphore(f"{name}_{self.nsem}")

    def _dep(self, o, p, raw):
        if p is None or p is o:
            return
        if (not p.is_dma) and (not o.is_dma) and p.eng == o.eng:
            if not (raw and self.same_engine_raw and o.eng != "pe"):
                return
        if p not in o.deps:
            o.deps.append(p)
            if not p.is_dma:
                p.signal = True

    def _track(self, o, reads, writes):
        for b in reads:
            for w in b.writers:
                self._dep(o, w, True)
        for b in writes:
            if not (b.multi and o.is_dma):
                for w in b.writers:
                    self._dep(o, w, False)
            for r in b.readers:
                self._dep(o, r, False)
        for b in reads:
            b.readers.append(o)
        for b in writes:
            if b.multi and o.is_dma and not b.readers:
                b.writers.append(o)
            else:
                b.writers = [o]
            b.readers = []

    def op(self, eng, fn, reads=(), writes=()):
        o = Op(eng, fn)
        self._track(o, reads, writes)
        self.q[eng].append(o)
        return o

    def dma(self, eng, fn, reads, writes, semb=None):
        o = Op(eng, fn)
        o.is_dma = True
        self._track(o, reads, writes)
        semb = semb if semb is not None else writes[0]
        if semb.sem is None:
            if self.sem_pool:
                semb.sem, semb.cnt = self.sem_pool.pop()
            else:
                semb.sem = self.new_sem("d")
            self.sem_bufs.append(semb)
        semb.cnt += 16
        o.dma_tok = (semb.sem, semb.cnt)
        self.q[eng].append(o)
        self.dmas.append(o)
        return o

    def coll(self, eng, fn, reads, writes, dummy_fn):
        o = Op(eng, fn)
        o.is_dma = True
        self._track(o, reads, [])
        if self.csem is None:
            self.csem = self.new_sem("c")
        self.ccnt += 1
        o.dma_tok = (self.csem, self.ccnt)
        o.k = -2
        self.q[eng].append(o)
        return self.op(eng, dummy_fn, [], writes)

    def wait_all(self, eng, ops):
        o = Op(eng, None)
        for p in ops:
            self._dep(o, p, True)
        self.q[eng].append(o)

    def barrier(self):
        lasts = []
        for e in ENGS:
            for o in reversed(self.q[e]):
                if not o.is_dma and o.fn is not None:
                    lasts.append(o)
                    break
        pend = list(self.dmas)
        self.dmas = []
        for b in self.sem_bufs:
            self.sem_pool.append((b.sem, b.cnt))
            b.sem = None
        self.sem_bufs = []
        for e in ENGS:
            o = Op(e, None)
            for p in lasts:
                if p.eng != e:
                    o.deps.append(p)
                    p.signal = True
            for p in pend:
                o.deps.append(p)
            self.q[e].append(o)

    def emit(self):
        nc = self.nc
        esems = {}
        for e in ENGS:
            k = 0
            for o in self.q[e]:
                if o.signal and not o.is_dma:
                    o.k = k
                    k += 1
            esems[e] = [self.new_sem(e) for _ in range(k // EPOCH + 1)]

        def tok(p):
            if p.is_dma:
                return p.dma_tok
            return (esems[p.eng][p.k // EPOCH], p.k % EPOCH + 1)

        def run(e, h):
            waited = {}
            for o in self.q[e]:
                need = {}
                for p in o.deps:
                    s, v = tok(p)
                    key = id(s)
                    if key not in need or need[key][1] < v:
                        need[key] = (s, v)
                for key, (s, v) in need.items():
                    if waited.get(key, 0) < v:
                        h.wait_ge(s, v)
                        waited[key] = v
                if o.fn is None:
                    continue
                ins = o.fn(h)
                if o.is_dma and o.k == -2:
                    ins.then_inc(o.dma_tok[0])
                    h.wait_ge(o.dma_tok[0], o.dma_tok[1])
                elif o.is_dma:
                    ins.then_inc(o.dma_tok[0], 16)
                elif o.signal:
                    s, v = tok(o)
                    ins.then_inc(s, 1)

        with nc.Block() as block:
            @block.tensor
            def _(h):
                run("pe", h)

            @block.scalar
            def _(h):
                run("act", h)

            @block.vector
            def _(h):
                run("dve", h)

            @block.gpsimd
            def _(h):
                run("pool", h)

            @block.sync
            def _(h):
                run("sp", h)


def _bucket_thresholds():
    n = np.arange(0, 128)
    nf = np.maximum(n, 1).astype(np.float32)
    large = 16 + (np.log(nf / np.float32(16)) / np.float32(math.log(128 / 16)) * np.float32(16)).astype(np.int32)
    large = np.minimum(large, 31)
    b = np.where(n < 16, n, large)
    lo = []
    for bb in range(1, 32):
        idx = np.nonzero(b >= bb)[0]
        lo.append(int(idx[0]) if len(idx) else 1 << 20)
    return lo


class Ctx:
    pass


class Builder:
    def __init__(self):
        self.nc = bass.Bass("TRN2", target_bir_lowering=False)
        self.S = Sched(self.nc)
        self.ins = {}
        self.outs = {}
        self.finals = []
        self.uid = 0
        self.off = self.SB_BASE
        self.peak = self.off
        self.ps = [self.nc.alloc_psum_tensor(f"psb{i}", [128, 512], F32) for i in range(8)]
        self.psb = [Buf(f"ps{i}") for i in range(8)]
        self.ones_bf = self.sb("ones", [128, 128], BF16)
        self.b_ones = Buf("ones")
        self.memset(self.ones_bf[:, :], 1.0, [self.b_ones])

    def inp(self, name, shape, dt=F32):
        t = self.nc.dram_tensor(name, list(shape), dt, kind="ExternalInput")
        self.ins[name] = t
        return t

    def outp(self, name, shape, dt=F32):
        t = self.nc.dram_tensor(name, list(shape), dt, kind="ExternalOutput")
        self.outs[name] = t
        return t

    SB_BASE = 16512
    SB_TOP = 229344

    def sb(self, name, shape, dt=F32):
        self.uid += 1
        sz = int(np.prod(shape[1:])) * (4 if dt in (F32, I32) else 2)
        sz = (sz + 31) // 32 * 32
        assert self.off + sz <= self.SB_TOP, f"SBUF overflow allocating {name} {shape}: off={self.off} sz={sz}"
        t = self.nc.alloc_sbuf_tensor_at(f"{name}_{self.uid}", list(shape), dt, offset=self.off)
        self.off += sz
        self.peak = max(self.peak, self.off)
        return t

    def mark(self):
        return self.off

    def release(self, mk):
        self.off = mk

    def mm(self, out, lhsT, rhs, start, stop, reads, writes):
        return self.S.op("pe", lambda h: h.matmul(out, lhsT=lhsT, rhs=rhs, start=start, stop=stop), reads, writes)

    def act(self, out, in_, func, reads, writes, scale=1.0, bias=0.0):
        return self.S.op("act", lambda h: h.activation(out=out, in_=in_, func=func, scale=scale, bias=bias), reads, writes)

    def tt(self, out, in0, in1, op, reads, writes, eng="dve"):
        return self.S.op(eng, lambda h: h.tensor_tensor(out=out, in0=in0, in1=in1, op=op), reads, writes)

    def ts(self, out, in0, s1, s2, op0, op1, reads, writes, eng="dve"):
        if s2 is None:
            return self.S.op(eng, lambda h: h.tensor_scalar(out=out, in0=in0, scalar1=s1, scalar2=None, op0=op0), reads, writes)
        return self.S.op(eng, lambda h: h.tensor_scalar(out=out, in0=in0, scalar1=s1, scalar2=s2, op0=op0, op1=op1), reads, writes)

    def stt(self, out, in0, scalar, in1, op0, op1, reads, writes, eng="dve"):
        return self.S.op(eng, lambda h: h.scalar_tensor_tensor(out=out, in0=in0, scalar=scalar, in1=in1, op0=op0, op1=op1), reads, writes)

    def copy(self, out, in_, reads, writes, eng="dve"):
        return self.S.op(eng, lambda h: h.tensor_copy(out=out, in_=in_), reads, writes)

    def recip(self, out, in_, reads, writes):
        return self.S.op("dve", lambda h: h.reciprocal(out=out, in_=in_), reads, writes)

    def memset(self, ap, val, writes, eng="pool"):
        return self.S.op(eng, lambda h: h.memset(ap, val), [], writes)

    def load(self, dst, src, buf, eng="sp", reads=()):
        return self.S.dma(eng, lambda h: h.dma_start(out=dst, in_=src), list(reads), [buf])

    def store(self, dst, src, srcbuf, eng="pool", final=False, dstbuf=None):
        o = self.S.dma(eng, lambda h: h.dma_start(out=dst, in_=src), [srcbuf],
                       [dstbuf] if dstbuf is not None else [Buf()], semb=srcbuf)
        if final:
            self.finals.append(o)
        return o

    def rstd(self, ss, n, out, tmp, reads, tmpbuf, outbuf):
        self.act(tmp, ss, AF.Ln, reads, [tmpbuf], scale=1.0 / n, bias=EPS)
        self.act(out, tmp, AF.Exp, [tmpbuf], [outbuf], scale=-0.5)

    def norm_work(self):
        w = Ctx()
        w.sq = self.sb("nsq", [128, 8, 512], BF16); w.b_sq = Buf()
        w.tmp = self.sb("ntmp", [128, 512], F32); w.b_tmp = Buf()
        w.rstd = self.sb("nrstd", [128, 512], F32); w.b_rstd = Buf()
        w.xn = self.sb("nxn", [128, 8, 512], F32); w.b_xn = Buf()
        return w

    def modnorm_tile(self, x, bx, t0, A, Bv, bmod, hdst, bh, w, pb=7):
        self.act(w.sq[:, :, :], x[:, :, t0:t0 + 512], AF.Square, [bx], [w.b_sq])
        for kc in range(8):
            self.mm(self.ps[pb][:, :], self.ones_bf[:, :], w.sq[:, kc, :], kc == 0, kc == 7, [w.b_sq, self.b_ones], [self.psb[pb]])
        self.rstd(self.ps[pb][:, :], float(D), w.rstd[:, :], w.tmp[:, :], [self.psb[pb]], w.b_tmp, w.b_rstd)
        self.tt(w.xn[:, :, :], x[:, :, t0:t0 + 512], w.rstd[:, None, :].to_broadcast([128, 8, 512]), ALU.mult, [bx, w.b_rstd], [w.b_xn])
        for kc in range(8):
            self.act(hdst[:, kc, :], w.xn[:, kc, :], AF.Identity, [w.b_xn, bmod], [bh], scale=A[:, kc:kc + 1], bias=Bv[:, kc:kc + 1])

    def mod_alloc(self):
        a = Ctx()
        a.cond = self.sb("cond", [128, 8], F32)
        a.bm = self.sb("bmod", [128, 72], F32)
        a.ng = self.sb("ng", [128, 3, 8], F32)
        a.modsb = self.sb("mod", [128, 72], F32)
        a.A = self.sb("modA", [128, 3, 8], F32)
        a.gate = self.sb("modG", [128, 3, 8], F32)
        return a

    def derive_mod(self, mod, b_mod, ng, b_ng, a):
        m = Ctx()
        m.A = a.A
        m.gate = a.gate
        m.mod = mod
        m.b = Buf("modder")
        for i in range(3):
            self.stt(m.A[:, i, :], mod[:, (3 * i + 1) * 8:(3 * i + 2) * 8], 1.0, ng[:, i, :], ALU.add, ALU.mult, [b_mod, b_ng], [m.b])
            self.ts(m.gate[:, i, :], mod[:, (3 * i + 2) * 8:(3 * i + 3) * 8], 1.0 if i == 1 else 0.5, None, ALU.mult, None, [b_mod], [m.b])
        m.B = lambda i: mod[:, (3 * i) * 8:(3 * i + 1) * 8]
        return m

    def ffn_work(self, nw):
        fw = Ctx()
        fw.h = self.sb("ffh", [128, 2, 8, 512], BF16)
        fw.b_h = [Buf(), Buf()]
        fw.act = self.sb("ffact", [128, NF, 1024], BF16)
        fw.b_act = [[Buf() for t in range(2)] for f in range(NF)]
        fw.wg = [self.sb(f"wg{i}", [128, 8, 2, 128], BF16) for i in range(3)]
        fw.b_wg = [Buf() for i in range(3)]
        fw.wd = [self.sb(f"wd{i}", [128, NF, 128], BF16) for i in range(2)]
        fw.b_wd = [Buf() for i in range(2)]
        fw.sg = [self.sb(f"sg{i}", [128, 512], F32) for i in range(2)]
        fw.b_sg = [Buf() for i in range(2)]
        fw.nw = nw
        return fw

    def ffn(self, x, bx, wgu_d, wdn_d, m, i, fw):
        A = m.A[:, i, :]
        Bv = m.B(i)
        gate = m.gate[:, i, :]
        for half in range(2):
            for tt in range(2):
                self.modnorm_tile(x, bx, half * 1024 + tt * 512, A, Bv, m.b, fw.h[:, tt, :, :], fw.b_h[tt], fw.nw)
            for f in range(NF):
                s = f % 3
                self.load(fw.wg[s][:, :, :, :], wgu_d[f, :, :, :, :], fw.b_wg[s], eng="pool")
                for tt in range(2):
                    par = (f * 2 + tt) % 2
                    pg, pu = par * 2, par * 2 + 1
                    for gu, pb in ((0, pg), (1, pu)):
                        for kc in range(8):
                            self.mm(self.ps[pb][:, :], fw.wg[s][:, kc, gu, :], fw.h[:, tt, kc, :], kc == 0, kc == 7, [fw.b_wg[s], fw.b_h[tt]], [self.psb[pb]])
                    self.act(fw.sg[par][:, :], self.ps[pg][:, :], AF.Silu, [self.psb[pg]], [fw.b_sg[par]])
                    self.tt(fw.act[:, f, tt * 512:(tt + 1) * 512], fw.sg[par][:, :], self.ps[pu][:, :], ALU.mult, [fw.b_sg[par], self.psb[pu]], [fw.b_act[f][tt]])
            for j in range(8):
                s = j % 2
                self.load(fw.wd[s][:, :, :], wdn_d[j, :, :, :], fw.b_wd[s], eng="pool")
                for tt in range(2):
                    pb = 4 + (j * 2 + tt) % 2
                    t0 = half * 1024 + tt * 512
                    for f in range(NF):
                        self.mm(self.ps[pb][:, :], fw.wd[s][:, f, :], fw.act[:, f, tt * 512:(tt + 1) * 512], f == 0, f == NF - 1, [fw.b_wd[s], fw.b_act[f][tt]], [self.psb[pb]])
                    self.stt(x[:, j, t0:t0 + 512], self.ps[pb][:, :], gate[:, j:j + 1], x[:, j, t0:t0 + 512], ALU.mult, ALU.add, [self.psb[pb], m.b, bx], [bx])


def emit_mod(Bd, cT, wmod, bmod, ngd, stage_bf, stage_bufs, a):
    cond = a.cond; b_cond = Buf()
    Bd.load(cond[:, :], cT[:, :], b_cond)
    Bd.act(cond[:, :], cond[:, :], AF.Silu, [b_cond], [b_cond])
    bm = a.bm; b_bm = Buf()
    Bd.load(bm[:, :], bmod[:, :], b_bm)
    ng = a.ng; b_ng = Buf()
    Bd.load(ng[:, :, :], ngd[:, :, :], b_ng)
    modsb = a.modsb; b_modsb = Buf()
    pm = 6
    for j9 in range(9):
        for hh in range(2):
            s = (j9 * 2 + hh) % 2
            stage = stage_bf[s]
            col0 = j9 * 1024 + hh * 512
            Bd.load(stage[:, :, :], wmod[0:128, :, col0:col0 + 512], stage_bufs[s])
            for jc in range(4):
                oc = j9 * 8 + hh * 4 + jc
                for kc in range(8):
                    Bd.mm(Bd.ps[pm][:, oc:oc + 1], stage[:, kc, jc * 128:(jc + 1) * 128], cond[:, kc:kc + 1], kc == 0, kc == 7, [stage_bufs[s], b_cond], [Bd.psb[pm]])
    Bd.tt(modsb[:, :], Bd.ps[pm][:, 0:72], bm[:, :], ALU.add, [Bd.psb[pm], b_bm], [b_modsb])
    return modsb, b_modsb, ng, b_ng


def emit_proj(Bd, x, bx, m, fw, di, do, final):
    S = Bd.S
    ps, psb = Bd.ps, Bd.psb
    ones = Bd.ones_bf
    b_ones = Bd.b_ones

    def wload(name, shape, dt, eng="pool"):
        t = Bd.sb(name, shape, dt); b = Buf(name)
        src = di[name]
        Bd.load(t[tuple(slice(None) for _ in shape)], src[(slice(0, shape[0]),) + tuple(slice(None) for _ in shape[1:])], b, eng=eng)
        return t, b
    win, b_win = wload("win", [128, 8, 1824], BF16)
    winv, b_winv = wload("winv", [128, 8, 384], BF16)
    wuq, b_wuq = wload("wuq", [128, 2, 576], BF16)
    wukvk, b_wukvk = wload("wukvk", [128, 6, 64], BF16)
    wukvv, b_wukvv = wload("wukvv", [128, 384], BF16)
    qan, b_qan = wload("qan", [128, 2], F32, "sp")
    kvan, b_kvan = wload("kvan", [128, 1], F32, "sp")
    hn, b_hn = wload("hn", [96, 4], F32, "sp")
    rot, b_rot = wload("rot", [96, 96], F32, "sp")
    freq, b_freq = wload("freq", [96, 1], F32, "sp")

    Ct = Bd.sb("ropeC", [96, T], F32); b_C = Buf()
    St = Bd.sb("ropeS", [96, T], F32); b_S = Buf()
    mk_rope = Bd.mark()
    ang = Bd.sb("ang", [96, T], F32); b_ang = Buf()
    rt = Bd.sb("rt", [96, T], F32); b_rt = Buf()
    ki = Bd.sb("ki", [96, T], I32); b_ki = Buf()
    posi = ki; b_posi = b_ki
    Bd.load(posi[:, :], di["pos"][0:1, :].partition_broadcast(96), b_posi)
    C1 = 6.28125
    C2 = TWO_PI - C1
    Bd.copy(ang[:, :], posi[:, :], [b_posi], [b_ang])
    Bd.ts(ang[:, :], ang[:, :], freq[:, 0:1], None, ALU.mult, None, [b_ang, b_freq], [b_ang])
    for (dst, bd, shift) in ((St, b_S, 0.0), (Ct, b_C, math.pi / 2)):
        Bd.ts(rt[:, :], ang[:, :], shift, 1.0 / TWO_PI, ALU.add, ALU.mult, [b_ang], [b_rt])
        Bd.copy(ki[:, :], rt[:, :], [b_rt], [b_ki])
        Bd.copy(rt[:, :], ki[:, :], [b_ki], [b_rt])
        Bd.stt(dst[:, :], rt[:, :], -C1, ang[:, :], ALU.mult, ALU.add, [b_rt, b_ang], [bd])
        Bd.stt(dst[:, :], rt[:, :], -C2, dst[:, :], ALU.mult, ALU.add, [b_rt, bd], [bd])
        Bd.ts(dst[:, :], dst[:, :], shift, math.pi, ALU.add, ALU.min, [bd], [bd])
        Bd.ts(dst[:, :], dst[:, :], -math.pi, None, ALU.max, None, [bd], [bd])
        Bd.act(dst[:, :], dst[:, :], AF.Sin, [bd], [bd])
    S.barrier()
    Bd.release(mk_rope)

    h2 = Bd.sb("h2", [128, 8, 512], BF16); b_h2 = Buf()
    cqn = Bd.sb("cqn", [128, 2, 512], BF16); b_cqn = Buf()
    ckvn = Bd.sb("ckvn", [128, 512], BF16); b_ckvn = Buf()
    sqw = [Bd.sb(f"sqw{i}", [128, 2, 512], BF16) for i in range(2)]; b_sqw = [Buf() for i in range(2)]
    rsw = [Bd.sb(f"rsw{i}", [128, 512], F32) for i in range(2)]; b_rsw = [Buf() for i in range(2)]
    xnw = [Bd.sb(f"xnw{i}", [96, 512], F32) for i in range(2)]; b_xnw = [Buf() for i in range(2)]
    t1w = [Bd.sb(f"t1w{i}", [96, 512], F32) for i in range(2)]; b_t1w = [Buf() for i in range(2)]

    def stg(name, shape):
        t = Bd.sb(name, shape, BF16); b = Buf()
        return [t, t], [b, b]
    st_qm, b_stqm = stg("stqm", [96, 6, 512])
    st_km, b_stkm = stg("stkm", [96, 6, 512])
    st_qs, b_stqs = stg("stqs", [64, 6, 512])
    st_ks, b_stks = stg("stks", [64, 2, 512])
    st_qc, b_stqc = stg("stqc", [64, 4, 512])
    st_kc, b_stkc = stg("stkc", [64, 4, 512])
    st_vm, b_stvm = stg("stvm", [128, 6, 65])
    st_vs, b_stvs = stg("stvs", [128, 2, 65])
    st_vc, b_stvc = stg("stvc", [128, 4, 65])
    Bd.memset(st_vs[0][:, :, :], 1.0, [b_stvs[0]])
    Bd.memset(st_vm[0][:, :, :], 1.0, [b_stvm[0]])
    Bd.memset(st_vc[0][:, :, :], 1.0, [b_stvc[0]])

    A = m.A[:, 1, :]
    Bv = m.B(1)
    cnt = [0]

    def nxt():
        cnt[0] += 1
        return cnt[0] % 2

    PSS = 6

    def head_chain(proj, pb, rows, gain, n, out_ap, outbuf, rope, t0, w, pss):
        pin = ps[pb][0:rows, :]
        st = [proj,
              lambda: Bd.act(sqw[w][0:rows, 0, :], pin, AF.Square, [psb[pb]], [b_sqw[w]]),
              lambda: Bd.mm(ps[pss][0:rows, :], ones[0:rows, 0:rows], sqw[w][0:rows, 0, :], True, True, [b_sqw[w], b_ones], [psb[pss]]),
              lambda: Bd.act(rsw[w][0:rows, :], ps[pss][0:rows, :], AF.Ln, [psb[pss]], [b_rsw[w]], scale=1.0 / n, bias=EPS),
              lambda: Bd.act(rsw[w][0:rows, :], rsw[w][0:rows, :], AF.Exp, [b_rsw[w]], [b_rsw[w]], scale=-0.5)]
        if not rope:
            st.append(lambda: Bd.stt(out_ap, pin, gain, rsw[w][0:rows, :], ALU.mult, ALU.mult, [psb[pb], b_rsw[w], b_hn], [outbuf]))
            return st
        st += [lambda: Bd.stt(xnw[w][:, :], pin, gain, rsw[w][0:rows, :], ALU.mult, ALU.mult, [psb[pb], b_rsw[w], b_hn], [b_xnw[w]]),
               lambda: Bd.mm(ps[pss][0:96, :], rot[:, :], xnw[w][:, :], True, True, [b_rot, b_xnw[w]], [psb[pss]]),
               lambda: Bd.tt(t1w[w][:, :], xnw[w][:, :], Ct[:, t0:t0 + 512], ALU.mult, [b_xnw[w], b_C], [b_t1w[w]]),
               lambda: Bd.tt(xnw[w][:, :], ps[pss][0:96, :], St[:, t0:t0 + 512], ALU.mult, [psb[pss], b_S], [b_xnw[w]]),
               lambda: Bd.tt(out_ap, t1w[w][:, :], xnw[w][:, :], ALU.add, [b_t1w[w], b_xnw[w]], [outbuf])]
        return st

    def lockstep(chains):
        for s in range(max(len(c) for c in chains)):
            for c in chains:
                if s < len(c):
                    c[s]()

    def heads_lockstep(nheads, pb0, mk_proj, rows, gain, n, out_fn, outbuf, rope, t0):
        for h0 in range(0, nheads, 2):
            chains = []
            for k in range(min(2, nheads - h0)):
                hh = h0 + k
                pb = pb0 + k
                chains.append(head_chain((lambda hh=hh, pb=pb: mk_proj(hh, pb)), pb, rows, gain, n, out_fn(hh), outbuf, rope, t0, k, 6 + k))
            lockstep(chains)

    def units(name):
        u = do[name]
        return u if isinstance(u, list) else [(u, 0, u.shape[0])]

    def store3(name, t0, src, srcbuf):
        o = do.get(name + "_off", 0)
        for (ap, h0, nh) in units(name):
            Bd.store(ap.rearrange("h p t -> p h t")[:, :, o + t0:o + t0 + 512], src[:, h0:h0 + nh, :], srcbuf, final=final, dstbuf=do.get(name + "_buf"))

    def storev(name, mblk, src, srcbuf):
        o = do.get(name + "_off", 0)
        for (ap, h0, nh) in units(name):
            Bd.store(ap.rearrange("h p m d -> p h m d")[:, :, o + mblk, :], src[:, h0:h0 + nh, :], srcbuf, final=final, dstbuf=do.get(name + "_buf"))

    for tt in range(4):
        t0 = tt * 512
        par = tt % 2
        Bd.modnorm_tile(x, bx, t0, A, Bv, m.b, h2, b_h2, fw.nw)
        for cc in range(2):
            for kc in range(8):
                Bd.mm(ps[cc][:, :], win[:, kc, cc * 128:(cc + 1) * 128], h2[:, kc, :], kc == 0, kc == 7, [b_win, b_h2], [psb[cc]])
        w = nxt()
        for cc in range(2):
            Bd.act(sqw[w][:, cc, :], ps[cc][:, :], AF.Square, [psb[cc]], [b_sqw[w]])
        for cc in range(2):
            Bd.mm(ps[PSS][:, :], ones[:, :], sqw[w][:, cc, :], cc == 0, cc == 1, [b_sqw[w], b_ones], [psb[PSS]])
        Bd.rstd(ps[PSS][:, :], 256.0, rsw[w][:, :], rsw[w][:, :], [psb[PSS]], b_rsw[w], b_rsw[w])
        for cc in range(2):
            Bd.stt(cqn[:, cc, :], ps[cc][:, :], qan[:, cc:cc + 1], rsw[w][:, :], ALU.mult, ALU.mult, [psb[cc], b_rsw[w], b_qan], [b_cqn])
        for kc in range(8):
            Bd.mm(ps[2][:, :], win[:, kc, 256:384], h2[:, kc, :], kc == 0, kc == 7, [b_win, b_h2], [psb[2]])
        w = nxt()
        Bd.act(sqw[w][:, 0, :], ps[2][:, :], AF.Square, [psb[2]], [b_sqw[w]])
        Bd.mm(ps[PSS][:, :], ones[:, :], sqw[w][:, 0, :], True, True, [b_sqw[w], b_ones], [psb[PSS]])
        Bd.rstd(ps[PSS][:, :], 128.0, rsw[w][:, :], rsw[w][:, :], [psb[PSS]], b_rsw[w], b_rsw[w])
        Bd.stt(ckvn[:, :], ps[2][:, :], kvan[:, 0:1], rsw[w][:, :], ALU.mult, ALU.mult, [psb[2], b_rsw[w], b_kvan], [b_ckvn])
        def proj_q(hh, pb):
            for kc in range(2):
                Bd.mm(ps[pb][0:96, :], wuq[:, kc, hh * 96:(hh + 1) * 96], cqn[:, kc, :], kc == 0, kc == 1, [b_wuq, b_cqn], [psb[pb]])
        heads_lockstep(6, 0, proj_q, 96, hn[0:96, 0:1], 96, lambda hh: st_qm[par][:, hh, :], b_stqm[par], True, t0)
        store3("qm", t0, st_qm[par], b_stqm[par])

        def proj_k(hh, pb):
            Bd.mm(ps[pb][0:64, :], wukvk[:, hh, :], ckvn[:, :], True, True, [b_wukvk, b_ckvn], [psb[pb]])
            for kc in range(8):
                Bd.mm(ps[pb][64:96, :], win[:, kc, 384:416], h2[:, kc, :], kc == 0, kc == 7, [b_win, b_h2], [psb[pb]])
        heads_lockstep(6, 2, proj_k, 96, hn[0:96, 1:2], 96, lambda hh: st_km[par][:, hh, :], b_stkm[par], True, t0)
        store3("km", t0, st_km[par], b_stkm[par])

        def proj_sq(hh, pb):
            for kc in range(8):
                Bd.mm(ps[pb][0:64, :], win[:, kc, 416 + hh * 64:416 + (hh + 1) * 64], h2[:, kc, :], kc == 0, kc == 7, [b_win, b_h2], [psb[pb]])
        heads_lockstep(6, 0, proj_sq, 64, hn[0:64, 2:3], 64, lambda hh: st_qs[par][:, hh, :], b_stqs[par], False, t0)
        store3("qs", t0, st_qs[par], b_stqs[par])

        def proj_sk(hh, pb):
            for kc in range(8):
                Bd.mm(ps[pb][0:64, :], win[:, kc, 800 + hh * 64:800 + (hh + 1) * 64], h2[:, kc, :], kc == 0, kc == 7, [b_win, b_h2], [psb[pb]])
        heads_lockstep(2, 2, proj_sk, 64, hn[0:64, 3:4], 64, lambda hh: st_ks[par][:, hh, :], b_stks[par], False, t0)
        store3("ks", t0, st_ks[par], b_stks[par])
        for hh in range(4):
            pb = hh % 2
            for kc in range(8):
                Bd.mm(ps[pb][0:64, :], win[:, kc, 1056 + hh * 64:1056 + (hh + 1) * 64], h2[:, kc, :], kc == 0, kc == 7, [b_win, b_h2], [psb[pb]])
            Bd.act(st_qc[par][:, hh, :], ps[pb][0:64, :], AF.Copy, [psb[pb]], [b_stqc[par]], scale=0.125)
        store3("qc", t0, st_qc[par], b_stqc[par])
        for hh in range(4):
            pb = 2 + hh % 2
            for kc in range(8):
                Bd.mm(ps[pb][0:64, :], win[:, kc, 1312 + hh * 64:1312 + (hh + 1) * 64], h2[:, kc, :], kc == 0, kc == 7, [b_win, b_h2], [psb[pb]])
            Bd.copy(st_kc[par][:, hh, :], ps[pb][0:64, :], [psb[pb]], [b_stkc[par]])
        store3("kc", t0, st_kc[par], b_stkc[par])
        for blk in range(4):
            mblk = tt * 4 + blk
            bp = blk % 2
            pb = 4 + bp
            Bd.mm(ps[pb][:, 0:384], ckvn[:, blk * 128:(blk + 1) * 128], wukvv[:, :], True, True, [b_ckvn, b_wukvv], [psb[pb]])
            Bd.act(st_vm[bp][:, :, 0:64], ps[pb][:, 0:384].rearrange("p (h d) -> p h d", h=6), AF.Copy, [psb[pb]], [b_stvm[bp]])
            storev("vm", mblk, st_vm[bp], b_stvm[bp])
            pb2 = bp
            for kc in range(8):
                Bd.mm(ps[pb2][:, 0:384], h2[:, kc, blk * 128:(blk + 1) * 128], winv[:, kc, :], kc == 0, kc == 7, [b_h2, b_winv], [psb[pb2]])
            Bd.copy(st_vs[bp][:, :, 0:64], ps[pb2][:, 0:128].rearrange("p (h d) -> p h d", h=2), [psb[pb2]], [b_stvs[bp]])
            Bd.copy(st_vc[bp][:, :, 0:64], ps[pb2][:, 128:384].rearrange("p (h d) -> p h d", h=4), [psb[pb2]], [b_stvc[bp]])
            storev("vs", mblk, st_vs[bp], b_stvs[bp])
            storev("vc", mblk, st_vc[bp], b_stvc[bp])


LO_B = _bucket_thresholds()
MLA_SCALE = 96.0 ** -0.5


def attn_consts(Bd, di, w, bias_cache=None, first=True):
    S = Bd.S
    a = Ctx()
    a.zer = Bd.sb("zer", [128, 128], BF16); a.b_zer = Buf()
    Bd.memset(a.zer[:, :], 0.0, [a.b_zer])
    a.zrhs = Bd.sb("zrhs", [128, 512], BF16); a.b_zrhs = Buf()
    Bd.memset(a.zrhs[:, :], 0.0, [a.b_zrhs])
    a.tri = Bd.sb("tri", [128, 2, 128], BF16); a.b_tri = Buf()
    Bd.load(a.tri[:, :, :], di["tricomp"][:, :, :], a.b_tri, eng="pool")
    a.mincl = Bd.sb("mincl", [128, 4, 128], F32); a.b_mincl = Buf()
    Bd.load(a.mincl[:, :, :], di["mincl"][:, :, :], a.b_mincl)
    a.mstr = Bd.sb("mstr", [128, 4, 128], F32); a.b_mstr = Buf()
    Bd.load(a.mstr[:, :, :], di["mstrict"][:, :, :], a.b_mstr)
    a.sel = Bd.sb("sel", [65, 64], F32); a.b_sel = Buf()
    Bd.memset(a.sel[:, :], 0.0, [a.b_sel])
    Bd.memset(a.sel[64:65, :], 1.0, [a.b_sel])
    a.vsink = Bd.sb("vsink", [1, 65], F32); a.b_vsink = Buf()
    Bd.memset(a.vsink[:, :], 0.0, [a.b_vsink])
    Bd.memset(a.vsink[0:1, 64:65], 1.0, [a.b_vsink])
    a.wsel = Bd.sb("wsel", [128, 4], F32); a.b_wsel = Buf()
    Bd.load(a.wsel[:, :], di["wsel"][:, :], a.b_wsel)
    a.onorm = Bd.sb("onorm", [64, 16], F32); a.b_onorm = Buf()
    Bd.load(a.onorm[:, :], di["onorm"][:, :], a.b_onorm)
    es = Bd.sb("es", [1, 6], F32); b_es = Buf()
    Bd.load(es[:, :], di["sinks"][:, :], b_es)
    Bd.act(es[:, :], es[:, :], AF.Exp, [b_es], [b_es])
    a.esr = Bd.sb("esr", [1, 6, 128], F32); a.b_esr = Buf()
    Bd.copy(a.esr[:, :, :], es[0:1, :, None].to_broadcast([1, 6, 128]), [b_es], [a.b_esr])
    a.bias = Bd.sb("swabias", [128, 2, 6, 128], F32); a.b_bias = Buf()
    if bias_cache is not None and not first:
        Bd.load(a.bias[:, :, :, :].rearrange("p w h q -> p (w h q)"), bias_cache[0].ap(), a.b_bias, reads=[bias_cache[1]])
    else:
        tb = w.e[0][:, 0:192].rearrange("p (b h) -> p b h", h=6); b_tb = w.b_e[0]
        Bd.load(w.e[0][:, 0:192], di["relb"][0:1, :].partition_broadcast(128), b_tb)
        dtb = w.e[1][:, 0:186].rearrange("p (b h) -> p b h", h=6); b_dtb = w.b_e[1]
        Bd.tt(dtb[:, :, :], tb[:, 1:32, :], tb[:, 0:31, :], ALU.subtract, [b_tb], [b_dtb])
        pqi = w.e[3][:, 256:384].bitcast(I32); b_pqi = w.b_e[3]
        Bd.load(pqi, di["posrow"][0:1, :].partition_broadcast(128), b_pqi)
        pki = w.e[3][:, 384:386].bitcast(I32); b_pki = w.b_e[3]
        Bd.load(pki, di["poscol"][:, :], b_pki)
        pq = w.e[2][:, 0:128]; b_pq = w.b_e[2]
        pk = w.e[2][:, 128:130]; b_pk = w.b_e[2]
        Bd.copy(pq, pqi, [b_pqi], [b_pq])
        Bd.copy(pk, pki, [b_pki], [b_pk])
        rel = w.ec[0][:, 0:256].rearrange("p (w q) -> p w q", w=2); b_rel = w.b_ec[0]
        for wi in range(2):
            Bd.ts(rel[:, wi, :], pq, pk[:, wi:wi + 1], None, ALU.subtract, None, [b_pq, b_pk], [b_rel])
        ind = w.ec[1][:, 0:256].rearrange("p (w q) -> p w q", w=2); b_ind = w.b_ec[1]
        for h in range(6):
            Bd.ts(a.bias[:, :, h, :], rel[:, :, :], 0.0, tb[:, 0, h:h + 1], ALU.mult, ALU.add, [b_rel, b_tb], [a.b_bias])
        for b in range(1, 32):
            Bd.ts(ind[:, :, :], rel[:, :, :], float(LO_B[b - 1]), None, ALU.is_ge, None, [b_rel], [b_ind])
            for h in range(6):
                Bd.stt(a.bias[:, :, h, :], ind[:, :, :], dtb[:, b - 1, h:h + 1], a.bias[:, :, h, :], ALU.mult, ALU.add, [b_ind, b_dtb, a.b_bias], [a.b_bias])
        val = w.e[1][:, 256:512].rearrange("p (w q) -> p w q", w=2); b_val = w.b_e[1]
        Bd.ts(val[:, :, :], rel[:, :, :], 0.0, None, ALU.is_ge, None, [b_rel], [b_val])
        Bd.ts(ind[:, :, :], rel[:, :, :], 128.0, None, ALU.is_ge, None, [b_rel], [b_ind])
        Bd.ts(ind[:, :, :], ind[:, :, :], -1.0, 1.0, ALU.mult, ALU.add, [b_ind], [b_ind])
        Bd.tt(val[:, :, :], val[:, :, :], ind[:, :, :], ALU.mult, [b_val, b_ind], [b_val])
        Bd.ts(ind[:, :, :], val[:, :, :], 1.0e4, -1.0e4, ALU.mult, ALU.add, [b_val], [b_ind])
        for h in range(6):
            Bd.tt(a.bias[:, :, h, :], a.bias[:, :, h, :], val[:, :, :], ALU.mult, [a.b_bias, b_val], [a.b_bias])
            Bd.tt(a.bias[:, :, h, :], a.bias[:, :, h, :], ind[:, :, :], ALU.add, [a.b_bias, b_ind], [a.b_bias])

        if bias_cache is not None:
            Bd.store(bias_cache[0].ap(), a.bias[:, :, :, :].rearrange("p w h q -> p (w h q)"), a.b_bias, eng="sp", dstbuf=bias_cache[1])
    return a


def attn_work(Bd):
    w = Ctx()
    w.kbuf = [Bd.sb(f"kbuf{i}", [96, 4, T], BF16) for i in range(2)]; w.b_k = [Buf() for _ in range(2)]
    w.vbuf = [Bd.sb(f"vbuf{i}", [128, 4, NBLK, 65], BF16) for i in range(2)]; w.b_v = [Buf() for _ in range(2)]
    w.og = Bd.sb("ogrp", [64, 6, 512], F32); w.b_og = [Buf() for _ in range(6)]
    w.mix = Bd.sb("mix", [64, 16, 512], BF16); w.b_mix = [Buf() for _ in range(16)]
    w.qm = Bd.sb("qmg", [96, 6, 512], BF16); w.b_qm = Buf()
    w.qc = Bd.sb("qcg", [64, 4, 512], BF16); w.b_qc = Buf()
    w.qs = Bd.sb("qsg", [64, 6, 512], BF16); w.b_qs = Buf()
    w.p = [Bd.sb(f"pw{i}", [128, 512], BF16) for i in range(2)]; w.b_p = [Buf() for _ in range(2)]
    w.e = [Bd.sb(f"ew{i}", [128, 512], F32) for i in range(4)]; w.b_e = [Buf() for _ in range(4)]
    w.sp = [Bd.sb(f"spw{i}", [128, 512], BF16) for i in range(4)]; w.b_sp = [Buf() for _ in range(4)]
    w.ec = [Bd.sb(f"ecw{i}", [128, 512], F32) for i in range(2)]; w.b_ec = [Buf() for _ in range(2)]
    w.a = w.p; w.b_a = w.b_p
    w.oa = Bd.sb("oa", [65, 512], F32); w.b_oa = Buf()
    w.rden = Bd.sb("rden", [64, 512], F32); w.b_rden = Buf()
    w.ksw = Bd.sb("ksw", [64, 2, 2, 512], BF16); w.b_ksw = Buf()
    w.vsw = Bd.sb("vsw", [128, 2, 2, 4, 65], BF16); w.b_vsw = Buf()
    w.sarg = [w.e[i][:, 0:384].rearrange("p (j q) -> p j q", j=3) for i in range(2)]; w.b_sarg = w.b_e[0:2]
    w.psw = [w.sp[i][:, 0:384].rearrange("p (j q) -> p j q", j=3) for i in range(2)]; w.b_psw = w.b_sp[0:2]
    w.gsq = w.sp[2][0:64, :]; w.b_gsq = w.b_sp[2]
    w.grs = w.e[2][0:64, :]; w.b_grs = w.b_e[2]
    wo = Bd.sb("wo", [64, 16, 128], BF16); bwo = Buf()
    w.wo = [wo, wo]; w.b_wo = [bwo, bwo]
    return w


def emit_attention(Bd, x, bx, m, ac, w, di):
    S = Bd.S
    ps, psb = Bd.ps, Bd.psb
    ones = Bd.ones_bf
    for G in (3, 2, 1, 0):
        t0 = G * 512
        nk = 4 * G + 4
        ntok = nk * 128
        nsteps = 16 * G + 16
        Bd.load(w.qm[:, :, :], di["qm"].rearrange("h p t -> p h t")[:, :, t0:t0 + 512], w.b_qm, reads=[di["b_q"]])
        Bd.load(w.qc[:, :, :], di["qc"].rearrange("h p t -> p h t")[:, :, t0:t0 + 512], w.b_qc, reads=[di["b_q"]])
        Bd.load(w.qs[:, :, :], di["qs"].rearrange("h p t -> p h t")[:, :, t0:t0 + 512], w.b_qs, reads=[di["b_q"]])

        def kvload(slot, kf, vf, h, rows):
            Bd.load(w.kbuf[slot][0:rows, :, 0:ntok], di[kf][h][0].rearrange("r p t -> p r t")[:, :, 0:ntok], w.b_k[slot], reads=[di[kf][h][1]])
            Bd.load(w.vbuf[slot][:, :, 0:nk, :], di[vf][h][0].rearrange("r p m d -> p r m d")[:, :, 0:nk, :], w.b_v[slot], reads=[di[vf][h][1]])

        def step_geom(i):
            kb = nsteps - 1 - i
            kw = kb - 16 * G
            c0 = (kw // 4) * 128 if kw >= 0 else 0
            return kb % 4, kb // 4, c0, (kw % 4 if kw >= 0 else None)

        for pair in range(3):
            hs = (2 * pair, 2 * pair + 1)
            for hi, h in enumerate(hs):
                kvload(hi, "kmf", "vmf", h, 96)
            PS = {0: (0, 1), 1: (2, 7)}
            POm = {0: 3, 1: 4}
            PT = {0: ((w.p[0], w.b_p[0]), (w.p[1], w.b_p[1])), 1: ((w.sp[2], w.b_sp[2]), (w.sp[3], w.b_sp[3]))}
            for hi in range(2):
                Bd.mm(ps[POm[hi]][0:65, :], ac.zer[:, 0:65], ac.zrhs[:, :], True, False, [ac.b_zer, ac.b_zrhs], [psb[POm[hi]]])

            def m_s(hi, i):
                rk, mb, c0, dm = step_geom(i)
                pb = PS[hi][i % 2]
                Bd.mm(ps[pb][:, c0:512], w.kbuf[hi][0:96, rk, mb * 128:(mb + 1) * 128], w.qm[0:96, hs[hi], c0:512], True, True, [w.b_k[hi], w.b_qm], [psb[pb]])

            def m_e(hi, i):
                rk, mb, c0, dm = step_geom(i)
                pb = PS[hi][i % 2]
                pt, bpt = PT[hi][i % 2]
                Bd.act(pt[:, c0:512], ps[pb][:, c0:512], AF.Exp, [psb[pb]], [bpt], scale=MLA_SCALE)
                if dm is not None:
                    Bd.tt(pt[:, c0:c0 + 128], pt[:, c0:c0 + 128], ac.mincl[:, dm, :], ALU.mult, [bpt, ac.b_mincl], [bpt])

            def m_pv(hi, i):
                rk, mb, c0, dm = step_geom(i)
                pt, bpt = PT[hi][i % 2]
                Bd.mm(ps[POm[hi]][0:65, c0:512], w.vbuf[hi][:, rk, mb, 0:65], pt[:, c0:512], False, i == nsteps - 1, [w.b_v[hi], bpt], [psb[POm[hi]]])
            for hi in range(2):
                m_s(hi, 0)
            for hi in range(2):
                m_e(hi, 0)
            for i in range(nsteps):
                if i + 1 < nsteps:
                    for hi in range(2):
                        m_s(hi, i + 1)
                for hi in range(2):
                    m_pv(hi, i)
                if i + 1 < nsteps:
                    for hi in range(2):
                        m_e(hi, i + 1)
            for hi, h in enumerate(hs):
                po = POm[hi]
                Bd.act(w.oa[:, :], ps[po][0:65, :], AF.Copy, [psb[po]], [w.b_oa])
                Bd.mm(ps[5][0:64, :], ac.sel[:, :], w.oa[:, :], True, True, [ac.b_sel, w.b_oa], [psb[5]])
                Bd.recip(w.rden[:, :], ps[5][0:64, :], [psb[5]], [w.b_rden])
                Bd.tt(w.og[:, h, :], w.oa[0:64, :], w.rden[:, :], ALU.mult, [w.b_oa, w.b_rden], [w.b_og[h]])
        group_norm(Bd, w, ac, 0, 6, 384.0, G)

        Bd.load(w.ksw[:, :, 1, :], di["ks_own"].rearrange("h p t -> p h t")[:, :, 128 + t0:128 + t0 + 512], w.b_ksw, reads=[di["b_src"]])
        Bd.load(w.vsw[:, :, 1, :, :], di["vs_own"].rearrange("h p m d -> p h m d")[:, :, 1 + 4 * G:5 + 4 * G, :], w.b_vsw, reads=[di["b_src"]])
        for c in range(4):
            ko = 128 + t0 if c < 3 else t0
            vo = 1 + 4 * G if c < 3 else 4 * G
            kcand = w.kbuf[0][0:64, c, 0:1024].rearrange("p (k t) -> p k t", k=2)
            vcand = w.vbuf[0][:, c, 0:8, :].rearrange("p (k m) d -> p k m d", k=2)
            Bd.load(kcand, di["ksf"].rearrange("r h p t -> p r h t")[:, c, :, ko:ko + 512], w.b_k[0], reads=[di["b_ks"]])
            Bd.load(vcand, di["vsf"].rearrange("r h p m d -> p r h m d")[:, c, :, vo:vo + 4, :], w.b_v[0], reads=[di["b_vs"]])
        for c in range(4):
            kcand = w.kbuf[0][0:64, c, 0:1024].rearrange("p (k t) -> p k t", k=2)
            vcand = w.vbuf[0][:, c, 0:8, :].rearrange("p (k m) d -> p k m d", k=2)
            if c == 0:
                Bd.ts(w.ksw[:, :, 0, :], kcand, ac.wsel[0:64, 0:1], None, ALU.mult, None, [w.b_k[0], ac.b_wsel], [w.b_ksw])
                Bd.ts(w.vsw[:, :, 0, :, :], vcand, ac.wsel[:, 0:1], None, ALU.mult, None, [w.b_v[0], ac.b_wsel], [w.b_vsw])
            else:
                Bd.stt(w.ksw[:, :, 0, :], kcand, ac.wsel[0:64, c:c + 1], w.ksw[:, :, 0, :], ALU.mult, ALU.add, [w.b_k[0], ac.b_wsel, w.b_ksw], [w.b_ksw])
                Bd.stt(w.vsw[:, :, 0, :, :], vcand, ac.wsel[:, c:c + 1], w.vsw[:, :, 0, :, :], ALU.mult, ALU.add, [w.b_v[0], ac.b_wsel, w.b_vsw], [w.b_vsw])
        for blk in range(4):
            q0 = blk * 128
            for kvh in range(2):
                pso = 3 + kvh
                Bd.mm(ps[pso][0:65, 0:384], ac.zer[:, 0:65], ac.zrhs[:, 0:384], True, False, [ac.b_zer, ac.b_zrhs], [psb[pso]])
                for wh in range(2):
                    pss = (blk * 4 + kvh * 2 + wh) % 2
                    for j in range(3):
                        hq = kvh * 3 + j
                        Bd.mm(ps[pss][:, j * 128:(j + 1) * 128], w.ksw[:, kvh, wh, q0:q0 + 128], w.qs[:, hq, q0:q0 + 128], True, True, [w.b_ksw, w.b_qs], [psb[pss]])
                    Bd.stt(w.sarg[pss][:, :, :], ps[pss][:, 0:384].rearrange("p (j q) -> p j q", j=3), 0.125, ac.bias[:, wh, kvh * 3:(kvh + 1) * 3, :], ALU.mult, ALU.add, [psb[pss], ac.b_bias], [w.b_sarg[pss]])
                    Bd.act(w.psw[pss][:, :, :], w.sarg[pss][:, :, :], AF.Exp, [w.b_sarg[pss]], [w.b_psw[pss]])
                    for j in range(3):
                        Bd.mm(ps[pso][0:65, j * 128:(j + 1) * 128], w.vsw[:, kvh, wh, blk, 0:65], w.psw[pss][:, j, :], False, False, [w.b_vsw, w.b_psw[pss]], [psb[pso]])
                for j in range(3):
                    hq = kvh * 3 + j
                    Bd.mm(ps[pso][0:65, j * 128:(j + 1) * 128], ac.vsink[0:1, :], ac.esr[0:1, hq, :], False, True, [ac.b_vsink, ac.b_esr], [psb[pso]])
                Bd.act(w.oa[:, 0:384], ps[pso][0:65, 0:384], AF.Copy, [psb[pso]], [w.b_oa])
                Bd.mm(ps[5][0:64, 0:384], ac.sel[:, :], w.oa[:, 0:384], True, True, [ac.b_sel, w.b_oa], [psb[5]])
                Bd.recip(w.rden[:, 0:384], ps[5][0:64, 0:384], [psb[5]], [w.b_rden])
                for j in range(3):
                    hq = kvh * 3 + j
                    Bd.tt(w.og[:, hq, q0:q0 + 128], w.oa[0:64, j * 128:(j + 1) * 128], w.rden[:, j * 128:(j + 1) * 128], ALU.mult, [w.b_oa, w.b_rden], [w.b_og[hq]])
        group_norm(Bd, w, ac, 6, 6, 384.0, G)

        for pair in range(2):
            hs = (2 * pair, 2 * pair + 1)
            for hi, h in enumerate(hs):
                kvload(hi, "kcf", "vcf", h, 64)
            PZ = {0: (0, 1), 1: (2, 7)}
            PC = {0: 3, 1: 4}
            PO = {0: 5, 1: 6}
            for hi in range(2):
                Bd.mm(ps[PC[hi]][:, :], ac.zer[:, :], ac.zrhs[:, :], True, False, [ac.b_zer, ac.b_zrhs], [psb[PC[hi]]])
                Bd.mm(ps[PO[hi]][0:64, :], ac.zer[:, 0:64], ac.zrhs[:, :], True, False, [ac.b_zer, ac.b_zrhs], [psb[PO[hi]]])

            def s_z(hi, i):
                h = hs[hi]
                rk, mb, c0, dm = step_geom(i)
                pz = PZ[hi][i % 2]
                Bd.mm(ps[pz][:, c0:512], w.kbuf[hi][0:64, rk, mb * 128:(mb + 1) * 128], w.qc[0:64, h, c0:512], True, True, [w.b_k[hi], w.b_qc], [psb[pz]])

            def s_e(hi, i):
                rk, mb, c0, dm = step_geom(i)
                pz = PZ[hi][i % 2]
                ew = hi * 2 + i % 2
                Bd.act(w.e[ew][:, c0:512], ps[pz][:, c0:512], AF.Exp, [psb[pz]], [w.b_e[ew]])
                if dm is not None:
                    Bd.tt(w.e[ew][:, c0:c0 + 128], w.e[ew][:, c0:c0 + 128], ac.mstr[:, dm, :], ALU.mult, [w.b_e[ew], ac.b_mstr], [w.b_e[ew]])

            def s_sp(hi, i):
                rk, mb, c0, dm = step_geom(i)
                ew = hi * 2 + i % 2
                Bd.act(w.sp[ew][:, c0:512], w.e[ew][:, c0:512], AF.Ln, [w.b_e[ew]], [w.b_sp[ew]], bias=1.0)

            def s_tri(hi, i):
                rk, mb, c0, dm = step_geom(i)
                ew = hi * 2 + i % 2
                Bd.mm(ps[PC[hi]][:, c0:512], ac.tri[:, 0, :], w.sp[ew][:, c0:512], False, False, [ac.b_tri, w.b_sp[ew]], [psb[PC[hi]]])

            def s_ec(hi, i):
                rk, mb, c0, dm = step_geom(i)
                Bd.act(w.ec[hi][:, c0:512], ps[PC[hi]][:, c0:512], AF.Exp, [psb[PC[hi]]], [w.b_ec[hi]], scale=-1.0)

            def s_a(hi, i):
                rk, mb, c0, dm = step_geom(i)
                ew = hi * 2 + i % 2
                Bd.tt(w.a[hi][:, c0:512], w.e[ew][:, c0:512], w.ec[hi][:, c0:512], ALU.mult, [w.b_e[ew], w.b_ec[hi]], [w.b_a[hi]])

            def s_co(hi, i):
                rk, mb, c0, dm = step_geom(i)
                ew = hi * 2 + i % 2
                pc, po = PC[hi], PO[hi]
                Bd.mm(ps[pc][:, c0:512], ac.tri[:, 1, :], w.sp[ew][:, c0:512], False, i == nsteps - 1, [ac.b_tri, w.b_sp[ew]], [psb[pc]])
                Bd.mm(ps[po][0:64, c0:512], w.vbuf[hi][:, rk, mb, 0:64], w.a[hi][:, c0:512], False, i == nsteps - 1, [w.b_v[hi], w.b_a[hi]], [psb[po]])
            for hi in range(2):
                s_z(hi, 0)
            for hi in range(2):
                s_e(hi, 0)
            for hi in range(2):
                s_sp(hi, 0)
            for i in range(nsteps):
                nxt_ = i + 1 < nsteps
                for hi in range(2):
                    s_tri(hi, i)
                if nxt_:
                    for hi in range(2):
                        s_z(hi, i + 1)
                for hi in range(2):
                    s_ec(hi, i)
                if nxt_:
                    for hi in range(2):
                        s_e(hi, i + 1)
                for hi in range(2):
                    s_a(hi, i)
                if nxt_:
                    for hi in range(2):
                        s_sp(hi, i + 1)
                for hi in range(2):
                    s_co(hi, i)
            for hi, h in enumerate(hs):
                Bd.act(w.og[:, h, :], ps[PO[hi]][0:64, :], AF.Copy, [psb[PO[hi]]], [w.b_og[h]])
        group_norm(Bd, w, ac, 12, 4, 256.0, G)

        for j in range(8):
            s = j % 2
            Bd.load(w.wo[s][:, :, :], di["wout"][j, :, :, :], w.b_wo[s], eng="pool")
            pb = j % 2
            for hh in range(16):
                Bd.mm(ps[pb][:, :], w.wo[s][:, hh, :], w.mix[:, hh, :], hh == 0, hh == 15, [w.b_wo[s], w.b_mix[hh]], [psb[pb]])
            Bd.stt(x[:, j, t0:t0 + 512], ps[pb][:, :], m.gate[:, 1, j:j + 1], x[:, j, t0:t0 + 512], ALU.mult, ALU.add, [psb[pb], m.b, bx], [bx])


DEBUG = False


def group_norm(Bd, w, ac, h0, nh, width, G=0):
    ps, psb = Bd.ps, Bd.psb
    if DEBUG:
        for i in range(nh):
            Bd.store(Bd.outs["dbg_og"][G, :, h0 + i, :], w.og[:, i, :], w.b_og[i], eng="sp", final=True)
    for i in range(nh):
        Bd.act(w.gsq[:, :], w.og[:, i, :], AF.Square, [w.b_og[i]], [w.b_gsq])
        Bd.mm(ps[7][0:64, :], Bd.ones_bf[0:64, 0:64], w.gsq[:, :], i == 0, i == nh - 1, [w.b_gsq, Bd.b_ones], [psb[7]])
    Bd.rstd(ps[7][0:64, :], width, w.grs[:, :], w.grs[:, :], [psb[7]], w.b_grs, w.b_grs)
    for i in range(nh):
        Bd.stt(w.mix[:, h0 + i, :], w.og[:, i, :], ac.onorm[:, h0 + i:h0 + i + 1], w.grs[:, :], ALU.mult, ALU.mult, [w.b_og[i], ac.b_onorm, w.b_grs], [w.b_mix[h0 + i]])


LAYER_IN = dict(wmod=[129, 8, 9216], bmod=[128, 72], ng=[128, 3, 8], wgu1=[NF + 1, 128, 8, 2, 128], wdn1=[9, 128, NF, 128],
                win=[129, 8, 1824], winv=[129, 8, 384], qan=[128, 2], kvan=[128, 1], wuq=[128, 2, 576], wukvk=[128, 6, 64],
                wukvv=[128, 384], hnorms=[96, 4], wout=[9, 64, 16, 128], onorm=[64, 16], sinks=[1, 6],
                wgu2=[NF + 1, 128, 8, 2, 128], wdn2=[9, 128, NF, 128])
RG = [[0, 1, 2, 3], [4, 5, 6, 7]]


def build_F():
    Bd = Builder()
    S = Bd.S
    nc = Bd.nc
    xT = Bd.inp("xT", [D, T])
    xoT = Bd.outp("xoT", [D, T])
    cT = Bd.inp("cT", [128, 8])
    shared_in = dict(pos=Bd.inp("pos", [1, T], I32), freq=Bd.inp("freq", [96, 1]), rot=Bd.inp("rotT", [96, 96]),
                     tricomp=Bd.inp("tricomp", [128, 2, 128]), mincl=Bd.inp("mincl", [128, 4, 128]),
                     mstrict=Bd.inp("mstrict", [128, 4, 128]), relb=Bd.inp("relb", [1, 192]),
                     posrow=Bd.inp("posrow", [1, 128], I32), poscol=Bd.inp("poscol", [128, 2], I32),
                     wsel=Bd.inp("wsel", [128, 4]))
    L = [{k: Bd.inp(f"{k}_{l}", shp) for k, shp in LAYER_IN.items()} for l in range(2)]

    x = Bd.sb("x", [128, 8, T], F32); bx = Buf("x")
    for kc in range(8):
        Bd.load(x[:, kc, :], xT[kc * 128:(kc + 1) * 128, :], bx)
    ma = Bd.mod_alloc()
    zt = Bd.sb("zt", [128, 1105], BF16); b_zt = Buf()
    Bd.memset(zt[:, :], 0.0, [b_zt])
    mk0 = Bd.mark()
    bias_cache = (nc.dram_tensor("bias_scr", [128, 2 * 6 * 128], F32), Buf("bias_scr"))
    for l in range(2):
        W = L[l]
        def dt2(name, rows, cols):
            return nc.dram_tensor(f"{name}_{l}", [rows, cols], BF16)
        q_km = dt2("sq_m", 6 * 96, T); q_qs = dt2("sq_s", 6 * 64, T); q_qc = dt2("sq_c", 4 * 64, T)
        UR = 192
        units_s, units_g = [], []

        units_b = []

        def unit():
            i = len(units_s)
            units_s.append(dt2(f"su{i}", UR, T))
            units_g.append(dt2(f"gu{i}", 4 * UR, T))
            units_b.append(Buf(f"gu{i}"))
            return units_s[-1].ap(), units_g[-1].ap().rearrange("(r n) t -> r n t", r=4)
        o_km, g_km, o_kc, g_kc, o_vm, g_vm, o_vc, g_vc = [], [], [], [], [], [], [], []
        for i in range(3):
            su, gu = unit()
            o_km.append((su[0:192, :].rearrange("(h p) t -> h p t", h=2), 2 * i, 2))
            g_km += [(gu[:, hh * 96:(hh + 1) * 96, :], units_b[-1]) for hh in range(2)]
        for i in range(2):
            su, gu = unit()
            o_kc.append((su[0:128, :].rearrange("(h p) t -> h p t", h=2), 2 * i, 2))
            g_kc += [(gu[:, hh * 64:(hh + 1) * 64, :], units_b[-1]) for hh in range(2)]
        for (ol, gl, n) in ((o_vm, g_vm, 3), (o_vc, g_vc, 2)):
            for i in range(n):
                su, gu = unit()
                ol.append((su[0:130, :].rearrange("n t -> (n t)").rearrange("(h p m d) -> h p m d", h=2, p=128, d=65), 2 * i, 2))
                gv = gu[:, 0:130, :].rearrange("r n t -> r (n t)").rearrange("r (h p m d) -> r h p m d", h=2, p=128, d=65)
                gl += [(gv[:, hh, :, :, :], units_b[-1]) for hh in range(2)]
        su, gu = unit()
        o_ks = su[0:136, :].rearrange("n t -> (n t)").rearrange("(h p u) -> h p u", h=2, p=64)
        g_ks = gu[:, 0:136, :].rearrange("r n t -> r (n t)").rearrange("r (h p u) -> r h p u", h=2, p=64)
        b_gks = units_b[-1]
        su, gu = unit()
        o_vs = su[0:139, :].rearrange("n t -> (n t)")[0:2 * 128 * 17 * 65].rearrange("(h p m d) -> h p m d", h=2, p=128, d=65)
        g_vs = gu[:, 0:139, :].rearrange("r n t -> r (n t)")[:, 0:2 * 128 * 17 * 65].rearrange("r (h p m d) -> r h p m d", h=2, p=128, d=65)
        b_gvs = units_b[-1]
        b_src = Buf("src", multi=True)
        b_q = Buf("qscr", multi=True)
        b_g = Buf("gath")
        Bd.store(o_ks.rearrange("h p u -> p h u")[:, :, 0:128], zt[0:64, 0:256].rearrange("p (h u) -> p h u", h=2), b_zt, dstbuf=b_src)
        Bd.store(o_vs.rearrange("h p m d -> p h m d")[:, :, 0, :], zt[:, 0:130].rearrange("p (h d) -> p h d", h=2), b_zt, dstbuf=b_src)
        nw = Bd.norm_work()
        fw = Bd.ffn_work(nw)
        stage = [fw.act[:, 0:8, :].bitcast(F32), fw.act[:, 8:16, :].bitcast(F32)]
        modsb, b_modsb, ng, b_ng = emit_mod(Bd, cT, W["wmod"], W["bmod"], W["ng"], stage, [Buf(), Buf()], ma)
        m = Bd.derive_mod(modsb, b_modsb, ng, b_ng, ma)
        S.barrier()
        Bd.ffn(x, bx, W["wgu1"], W["wdn1"], m, 0, fw)
        S.barrier()
        Bd.release(mk0)
        nw = Bd.norm_work()
        fw2 = Ctx(); fw2.nw = nw
        di = dict(win=W["win"], winv=W["winv"], qan=W["qan"], kvan=W["kvan"], wuq=W["wuq"], wukvk=W["wukvk"], wukvv=W["wukvv"],
                  hn=W["hnorms"], pos=shared_in["pos"], freq=shared_in["freq"], rot=shared_in["rot"])
        do = dict(qm=q_km.ap().rearrange("(h p) t -> h p t", h=6), qs=q_qs.ap().rearrange("(h p) t -> h p t", h=6),
                  qc=q_qc.ap().rearrange("(h p) t -> h p t", h=4),
                  km=o_km, kc=o_kc, ks=o_ks, ks_off=128, vm=o_vm, vc=o_vc, vs=o_vs, vs_off=1,
                  qm_buf=b_q, qs_buf=b_q, qc_buf=b_q, km_buf=b_src, kc_buf=b_src, ks_buf=b_src, vm_buf=b_src, vc_buf=b_src, vs_buf=b_src)
        emit_proj(Bd, x, bx, m, fw2, di, do, final=False)
        S.barrier()
        Bd.release(mk0)
        da = dict(shared_in)
        da.update(onorm=W["onorm"], sinks=W["sinks"], wout=W["wout"],
                  qm=do["qm"], qs=do["qs"], qc=do["qc"], b_q=b_q, b_src=b_src, b_ks=b_gks, b_vs=b_gvs,
                  kmf=g_km, vmf=g_vm, kcf=g_kc, vcf=g_vc, ksf=g_ks, vsf=g_vs, ks_own=o_ks, vs_own=o_vs)
        w = attn_work(Bd)
        ac = attn_consts(Bd, da, w, bias_cache, first=(l == 0))
        for ui in (0, 5, 1, 6, 2, 7, 10, 11, 3, 8, 4, 9):
            S.coll("pool", (lambda a_, b_: lambda h: h.collective_compute("AllGather", ALU.bypass, replica_groups=RG, ins=[a_.ap()], outs=[b_.ap()]))(units_s[ui], units_g[ui]),
                   [b_src], [units_b[ui]], lambda h: h.memset(zt[0:1, 0:8], 0.0))
        emit_attention(Bd, x, bx, m, ac, w, da)
        S.barrier()
        Bd.release(mk0)
        nw = Bd.norm_work()
        fw = Bd.ffn_work(nw)
        Bd.ffn(x, bx, W["wgu2"], W["wdn2"], m, 2, fw)
        S.barrier()
        Bd.release(mk0)
    for kc in range(8):
        Bd.store(xoT[kc * 128:(kc + 1) * 128, :], x[:, kc, :], bx, eng="sp", final=True)
    S.wait_all("sp", Bd.finals)
    print("F sbuf peak", Bd.peak, {e: len(S.q[e]) for e in ENGS})
    S.emit()
    return Bd


_CACHE = {}


def _get(name, fn):
    if name not in _CACHE:
        _CACHE[name] = fn()
    return _CACHE[name]


def _core_tokens(r):
    idx = (np.arange(NBLK)[:, None] * 4 + r) * 128 + np.arange(128)[None, :]
    return idx.reshape(-1)


def _freq_rot():
    half = 16
    fr = (np.float32(10000.0) ** (-np.arange(half, dtype=np.float32) / np.float32(half))).astype(np.float32)
    freq = np.zeros((96, 1), np.float32)
    freq[64:80, 0] = fr
    freq[80:96, 0] = fr
    rotT = np.zeros((96, 96), np.float32)
    for i in range(16):
        rotT[80 + i, 64 + i] = -1.0
        rotT[64 + i, 80 + i] = 1.0
    return freq, rotT


def _ffn_layout(wgu_l, wdn_l):
    f = np.ascontiguousarray
    wgu = wgu_l.reshape(8, 128, 2, NF, 128)
    wdn = wdn_l.reshape(NF, 128, 8, 128)
    return f(wgu.transpose(3, 1, 0, 2, 4)), f(wdn.transpose(2, 1, 0, 3))


def layer_inputs(inp, l):
    f = np.ascontiguousarray
    d = {}
    d["wmod"] = f(inp["w_mod"][l].reshape(8, 128, 9216).transpose(1, 0, 2))
    d["bmod"] = f(inp["b_mod"][l].reshape(72, 128).T)
    d["ng"] = f(inp["norm_g"][l].reshape(3, 8, 128).transpose(2, 0, 1))
    d["wgu1"], d["wdn1"] = _ffn_layout(inp["w_ffn1_gu"][l], inp["w_ffn1_down"][l])
    d["wgu2"], d["wdn2"] = _ffn_layout(inp["w_ffn2_gu"][l], inp["w_ffn2_down"][l])
    win = inp["w_in"][l]
    d["win"] = f(win.reshape(8, 128, 1824).transpose(1, 0, 2))
    winv = np.concatenate([win[:, 928:1056], win[:, 1568:1824]], axis=1)
    d["winv"] = f(winv.reshape(8, 128, 384).transpose(1, 0, 2))
    d["qan"] = f(inp["q_a_norm"][l].reshape(2, 128).T)
    d["kvan"] = f(inp["kv_a_norm"][l].reshape(128, 1))
    d["wuq"] = f(inp["w_uq"][l].reshape(2, 128, 576).transpose(1, 0, 2))
    wukv = inp["w_ukv"][l].reshape(128, 6, 128)
    d["wukvk"] = f(wukv[:, :, 0:64])
    d["wukvv"] = f(wukv[:, :, 64:128].reshape(128, 384))
    hn = np.zeros((96, 4), np.float32)
    hn[:, 0] = inp["mla_q_norm"][l]
    hn[:, 1] = inp["mla_k_norm"][l]
    hn[0:64, 2] = inp["swa_q_norm"][l]
    hn[0:64, 3] = inp["swa_k_norm"][l]
    d["hnorms"] = hn
    d["wout"] = f(inp["w_out"][l].reshape(16, 64, 8, 128).transpose(2, 1, 0, 3))
    d["onorm"] = f(inp["out_norm"][l].reshape(16, 64).T)
    d["sinks"] = f(inp["sinks"][l].reshape(1, 6))
    return d


def _masks(r):
    j = np.arange(128)[:, None]
    q = np.arange(128)[None, :]
    mi = np.zeros((128, 4, 128), np.float32)
    ms = np.zeros((128, 4, 128), np.float32)
    for d in range(4):
        if d < r:
            mi[:, d, :] = 1.0
            ms[:, d, :] = 1.0
        elif d == r:
            mi[:, d, :] = (j <= q)
            ms[:, d, :] = (j < q)
    tc = np.zeros((128, 2, 128), np.float32)
    tc[:, 0, :] = (j >= q)
    tc[:, 1, :] = (j < q)
    return mi, ms, tc


CORES = [(b, r) for b in range(2) for r in range(4)]
_PAD_KEYS = ("wmod", "wgu1", "wdn1", "wgu2", "wdn2", "win", "winv", "wout")


def _pad(a, ci):
    return np.concatenate([a, np.full((1,) + a.shape[1:], float(ci), a.dtype)], axis=0)


def kernel(**inp):
    inp = {k: np.asarray(v) for k, v in inp.items()}
    prog = _get("F", build_F)
    x = inp["x"]
    pos = inp["positions"]
    freq, rotT = _freq_rot()
    lay = [layer_inputs(inp, l) for l in range(2)]
    in_maps = []
    for ci, (b, r) in enumerate(CORES):
        mi, ms, tc = _masks(r)
        wsel = np.zeros((128, 4), np.float32)
        wsel[:, (r + 3) % 4] = 1.0
        dct = dict(xT=np.ascontiguousarray(x[b][_core_tokens(r)].T),
                   cT=np.ascontiguousarray(inp["c"][b].reshape(8, 128).T),
                   pos=np.ascontiguousarray(pos[b][_core_tokens(r)].reshape(1, T).astype(np.int32)),
                   freq=freq, rotT=rotT, tricomp=tc, mincl=mi, mstrict=ms,
                   relb=np.ascontiguousarray(inp["rel_bias"].reshape(1, 192)),
                   posrow=np.ascontiguousarray(pos[b][(4 + r) * 128:(5 + r) * 128].reshape(1, 128).astype(np.int32)),
                   poscol=np.ascontiguousarray(np.stack([pos[b][(3 + r) * 128:(4 + r) * 128], pos[b][(4 + r) * 128:(5 + r) * 128]], axis=1).astype(np.int32)),
                   wsel=wsel)
        for l in range(2):
            for k, a in lay[l].items():
                dct[f"{k}_{l}"] = _pad(a, ci) if k in _PAD_KEYS else a
        in_maps.append(dct)
    res = run_bass_kernel_spmd(prog.nc, in_maps, core_ids=list(range(NCORES)))
    out = np.zeros((2, 8192, 1024), np.float32)
    for ci, (b, r) in enumerate(CORES):
        out[b][_core_tokens(r)] = res.results[ci]["xoT"].T
    return out
```

```python
import math
import numpy as np
import ml_dtypes
import concourse.bass as bass
import concourse.mybir as mybir
from concourse.bass_utils import run_bass_kernel_spmd

F32 = mybir.dt.float32
BF16 = mybir.dt.bfloat16
I32 = mybir.dt.int32
AF = mybir.ActivationFunctionType
ALU = mybir.AluOpType

T = 2048
NBLK = 16
D = 1024
DFF = 2816
NF = 22
EPS = 1e-6
NCORES = 8
TWO_PI = 2.0 * math.pi

ENGS = ("pe", "act", "dve", "pool", "sp")
EPOCH = 30000


class Buf:
    __slots__ = ("name", "writers", "readers", "sem", "cnt", "multi")

    def __init__(self, name="", multi=False):
        self.name = name
        self.writers = []
        self.readers = []
        self.sem = None
        self.cnt = 0
        self.multi = multi


class Op:
    __slots__ = ("eng", "fn", "deps", "signal", "k", "dma_tok", "is_dma")

    def __init__(self, eng, fn):
        self.eng = eng
        self.fn = fn
        self.deps = []
        self.signal = False
        self.k = -1
        self.dma_tok = None
        self.is_dma = False


class Sched:
    def __init__(self, nc, same_engine_raw=True):
        self.nc = nc
        self.q = {e: [] for e in ENGS}
        self.same_engine_raw = same_engine_raw
        self.nsem = 0
        self.dmas = []
        self.csem = None
        self.ccnt = 0
        self.sem_pool = []
        self.sem_bufs = []

    def new_sem(self, name):
        self.nsem += 1
        return self.nc.alloc_semaphore(f"{name}_{self.nsem}")

    def _dep(self, o, p, raw):
        if p is None or p is o:
            return
        if (not p.is_dma) and (not o.is_dma) and p.eng == o.eng:
            if not (raw and self.same_engine_raw and o.eng != "pe"):
                return
        if p not in o.deps:
            o.deps.append(p)
            if not p.is_dma:
                p.signal = True

    def _track(self, o, reads, writes):
        for b in reads:
            for w in b.writers:
                self._dep(o, w, True)
        for b in writes:
            if not (b.multi and o.is_dma):
                for w in b.writers:
                    self._dep(o, w, False)
            for r in b.readers:
                self._dep(o, r, False)
        for b in reads:
            b.readers.append(o)
        for b in writes:
            if b.multi and o.is_dma and not b.readers:
                b.writers.append(o)
            else:
                b.writers = [o]
            b.readers = []

    def op(self, eng, fn, reads=(), writes=()):
        o = Op(eng, fn)
        self._track(o, reads, writes)
        self.q[eng].append(o)
        return o

    def dma(self, eng, fn, reads, writes, semb=None):
        o = Op(eng, fn)
        o.is_dma = True
        self._track(o, reads, writes)
        semb = semb if semb is not None else writes[0]
        if semb.sem is None:
            if self.sem_pool:
                semb.sem, semb.cnt = self.sem_pool.pop()
            else:
                semb.sem = self.new_sem("d")
            self.sem_bufs.append(semb)
        semb.cnt += 16
        o.dma_tok = (semb.sem, semb.cnt)
        self.q[eng].append(o)
        self.dmas.append(o)
        return o

    def coll(self, eng, fn, reads, writes, dummy_fn):
        o = Op(eng, fn)
        o.is_dma = True
        self._track(o, reads, [])
        if self.csem is None:
            self.csem = self.new_sem("c")
        self.ccnt += 1
        o.dma_tok = (self.csem, self.ccnt)
        o.k = -2
        self.q[eng].append(o)
        return self.op(eng, dummy_fn, [], writes)

    def wait_all(self, eng, ops):
        o = Op(eng, None)
        for p in ops:
            self._dep(o, p, True)
        self.q[eng].append(o)

    def barrier(self):
        lasts = []
        for e in ENGS:
            for o in reversed(self.q[e]):
                if not o.is_dma and o.fn is not None:
                    lasts.append(o)
                    break
        pend = list(self.dmas)
        self.dmas = []
        for b in self.sem_bufs:
            self.sem_pool.append((b.sem, b.cnt))
            b.sem = None
        self.sem_bufs = []
        for e in ENGS:
            o = Op(e, None)
            for p in lasts:
                if p.eng != e:
                    o.deps.append(p)
                    p.signal = True
            for p in pend:
                o.deps.append(p)
            self.q[e].append(o)

    def emit(self):
        nc = self.nc
        esems = {}
        for e in ENGS:
            k = 0
            for o in self.q[e]:
                if o.signal and not o.is_dma:
                    o.k = k
                    k += 1
            esems[e] = [self.new_sem(e) for _ in range(k // EPOCH + 1)]

        def tok(p):
            if p.is_dma:
                return p.dma_tok
            return (esems[p.eng][p.k // EPOCH], p.k % EPOCH + 1)

        def run(e, h):
            waited = {}
            for o in self.q[e]:
                need = {}
                for p in o.deps:
                    s, v = tok(p)
                    key = id(s)
                    if key not in need or need[key][1] < v:
                        need[key] = (s, v)
                for key, (s, v) in need.items():
                    if waited.get(key, 0) < v:
                        h.wait_ge(s, v)
                        waited[key] = v
                if o.fn is None:
                    continue
                ins = o.fn(h)
                if o.is_dma and o.k == -2:
                    ins.then_inc(o.dma_tok[0])
                    h.wait_ge(o.dma_tok[0], o.dma_tok[1])
                elif o.is_dma:
                    ins.then_inc(o.dma_tok[0], 16)
                elif o.signal:
                    s, v = tok(o)
                    ins.then_inc(s, 1)

        with nc.Block() as block:
            @block.tensor
            def _(h):
                run("pe", h)

            @block.scalar
            def _(h):
                run("act", h)

            @block.vector
            def _(h):
                run("dve", h)

            @block.gpsimd
            def _(h):
                run("pool", h)

            @block.sync
            def _(h):
                run("sp", h)


def _bucket_thresholds():
    n = np.arange(0, 128)
    nf = np.maximum(n, 1).astype(np.float32)
    large = 16 + (np.log(nf / np.float32(16)) / np.float32(math.log(128 / 16)) * np.float32(16)).astype(np.int32)
    large = np.minimum(large, 31)
    b = np.where(n < 16, n, large)
    lo = []
    for bb in range(1, 32):
        idx = np.nonzero(b >= bb)[0]
        lo.append(int(idx[0]) if len(idx) else 1 << 20)
    return lo


class Ctx:
    pass


class Builder:
    def __init__(self):
        self.nc = bass.Bass("TRN2", target_bir_lowering=False)
        self.S = Sched(self.nc)
        self.ins = {}
        self.outs = {}
        self.finals = []
        self.uid = 0
        self.off = self.SB_BASE
        self.peak = self.off
        self.ps = [self.nc.alloc_psum_tensor(f"psb{i}", [128, 512], F32) for i in range(8)]
        self.psb = [Buf(f"ps{i}") for i in range(8)]
        self.ones_bf = self.sb("ones", [128, 128], BF16)
        self.b_ones = Buf("ones")
        self.memset(self.ones_bf[:, :], 1.0, [self.b_ones])

    def inp(self, name, shape, dt=F32):
        t = self.nc.dram_tensor(name, list(shape), dt, kind="ExternalInput")
        self.ins[name] = t
        return t

    def outp(self, name, shape, dt=F32):
        t = self.nc.dram_tensor(name, list(shape), dt, kind="ExternalOutput")
        self.outs[name] = t
        return t

    SB_BASE = 16512
    SB_TOP = 229344

    def sb(self, name, shape, dt=F32):
        self.uid += 1
        sz = int(np.prod(shape[1:])) * (4 if dt in (F32, I32) else 2)
        sz = (sz + 31) // 32 * 32
        assert self.off + sz <= self.SB_TOP, f"SBUF overflow allocating {name} {shape}: off={self.off} sz={sz}"
        t = self.nc.alloc_sbuf_tensor_at(f"{name}_{self.uid}", list(shape), dt, offset=self.off)
        self.off += sz
        self.peak = max(self.peak, self.off)
        return t

    def mark(self):
        return self.off

    def release(self, mk):
        self.off = mk

    def mm(self, out, lhsT, rhs, start, stop, reads, writes):
        return self.S.op("pe", lambda h: h.matmul(out, lhsT=lhsT, rhs=rhs, start=start, stop=stop), reads, writes)

    def act(self, out, in_, func, reads, writes, scale=1.0, bias=0.0):
        return self.S.op("act", lambda h: h.activation(out=out, in_=in_, func=func, scale=scale, bias=bias), reads, writes)

    def tt(self, out, in0, in1, op, reads, writes, eng="dve"):
        return self.S.op(eng, lambda h: h.tensor_tensor(out=out, in0=in0, in1=in1, op=op), reads, writes)

    def ts(self, out, in0, s1, s2, op0, op1, reads, writes, eng="dve"):
        if s2 is None:
            return self.S.op(eng, lambda h: h.tensor_scalar(out=out, in0=in0, scalar1=s1, scalar2=None, op0=op0), reads, writes)
        return self.S.op(eng, lambda h: h.tensor_scalar(out=out, in0=in0, scalar1=s1, scalar2=s2, op0=op0, op1=op1), reads, writes)

    def stt(self, out, in0, scalar, in1, op0, op1, reads, writes, eng="dve"):
        return self.S.op(eng, lambda h: h.scalar_tensor_tensor(out=out, in0=in0, scalar=scalar, in1=in1, op0=op0, op1=op1), reads, writes)

    def copy(self, out, in_, reads, writes, eng="dve"):
        return self.S.op(eng, lambda h: h.tensor_copy(out=out, in_=in_), reads, writes)

    def recip(self, out, in_, reads, writes):
        return self.S.op("dve", lambda h: h.reciprocal(out=out, in_=in_), reads, writes)

    def memset(self, ap, val, writes, eng="pool"):
        return self.S.op(eng, lambda h: h.memset(ap, val), [], writes)

    def load(self, dst, src, buf, eng="sp", reads=()):
        return self.S.dma(eng, lambda h: h.dma_start(out=dst, in_=src), list(reads), [buf])

    def store(self, dst, src, srcbuf, eng="pool", final=False, dstbuf=None):
        o = self.S.dma(eng, lambda h: h.dma_start(out=dst, in_=src), [srcbuf],
                       [dstbuf] if dstbuf is not None else [Buf()], semb=srcbuf)
        if final:
            self.finals.append(o)
        return o

    def rstd(self, ss, n, out, tmp, reads, tmpbuf, outbuf):
        self.act(tmp, ss, AF.Ln, reads, [tmpbuf], scale=1.0 / n, bias=EPS)
        self.act(out, tmp, AF.Exp, [tmpbuf], [outbuf], scale=-0.5)

    def norm_work(self):
        w = Ctx()
        w.sq = self.sb("nsq", [128, 8, 512], BF16); w.b_sq = Buf()
        w.tmp = self.sb("ntmp", [128, 512], F32); w.b_tmp = Buf()
        w.rstd = self.sb("nrstd", [128, 512], F32); w.b_rstd = Buf()
        w.xn = self.sb("nxn", [128, 8, 512], F32); w.b_xn = Buf()
        return w

    def modnorm_tile(self, x, bx, t0, A, Bv, bmod, hdst, bh, w, pb=7):
        self.act(w.sq[:, :, :], x[:, :, t0:t0 + 512], AF.Square, [bx], [w.b_sq])
        for kc in range(8):
            self.mm(self.ps[pb][:, :], self.ones_bf[:, :], w.sq[:, kc, :], kc == 0, kc == 7, [w.b_sq, self.b_ones], [self.psb[pb]])
        self.rstd(self.ps[pb][:, :], float(D), w.rstd[:, :], w.tmp[:, :], [self.psb[pb]], w.b_tmp, w.b_rstd)
        self.tt(w.xn[:, :, :], x[:, :, t0:t0 + 512], w.rstd[:, None, :].to_broadcast([128, 8, 512]), ALU.mult, [bx, w.b_rstd], [w.b_xn])
        for kc in range(8):
            self.act(hdst[:, kc, :], w.xn[:, kc, :], AF.Identity, [w.b_xn, bmod], [bh], scale=A[:, kc:kc + 1], bias=Bv[:, kc:kc + 1])

    def mod_alloc(self):
        a = Ctx()
        a.cond = self.sb("cond", [128, 8], F32)
        a.condb = self.sb("condb", [128, 8], BF16)
        a.bm = self.sb("bmod", [128, 72], F32)
        a.ng = self.sb("ng", [128, 3, 8], F32)
        a.modsb = self.sb("mod", [128, 72], F32)
        a.A = self.sb("modA", [128, 3, 8], F32)
        a.gate = self.sb("modG", [128, 3, 8], F32)
        return a

    def derive_mod(self, mod, b_mod, ng, b_ng, a):
        m = Ctx()
        m.A = a.A
        m.gate = a.gate
        m.mod = mod
        m.b = Buf("modder")
        for i in range(3):
            self.stt(m.A[:, i, :], mod[:, (3 * i + 1) * 8:(3 * i + 2) * 8], 1.0, ng[:, i, :], ALU.add, ALU.mult, [b_mod, b_ng], [m.b])
            self.ts(m.gate[:, i, :], mod[:, (3 * i + 2) * 8:(3 * i + 3) * 8], 1.0 if i == 1 else 0.5, None, ALU.mult, None, [b_mod], [m.b])
        m.B = lambda i: mod[:, (3 * i) * 8:(3 * i + 1) * 8]
        return m

    def ffn_work(self, nw):
        fw = Ctx()
        fw.h = self.sb("ffh", [128, 2, 8, 512], BF16)
        fw.b_h = [Buf(), Buf()]
        fw.act = self.sb("ffact", [128, NF, 1024], BF16)
        fw.b_act = [[Buf() for t in range(2)] for f in range(NF)]
        fw.wg = [self.sb(f"wg{i}", [128, 8, 2, 128], BF16) for i in range(3)]
        fw.b_wg = [Buf() for i in range(3)]
        fw.wd = [self.sb(f"wd{i}", [128, NF, 128], BF16) for i in range(2)]
        fw.b_wd = [Buf() for i in range(2)]
        fw.sg = [self.sb(f"sg{i}", [128, 512], F32) for i in range(2)]
        fw.b_sg = [Buf() for i in range(2)]
        fw.nw = nw
        return fw

    def ffn(self, x, bx, wgu_d, wdn_d, m, i, fw):
        A = m.A[:, i, :]
        Bv = m.B(i)
        gate = m.gate[:, i, :]
        for half in range(2):
            for tt in range(2):
                self.modnorm_tile(x, bx, half * 1024 + tt * 512, A, Bv, m.b, fw.h[:, tt, :, :], fw.b_h[tt], fw.nw)
            for f in range(NF):
                s = f % 3
                self.load(fw.wg[s][:, :, :, :], wgu_d[f, :, :, :, :], fw.b_wg[s], eng="pool")
                for tt in range(2):
                    par = (f * 2 + tt) % 2
                    pg, pu = par * 2, par * 2 + 1
                    for gu, pb in ((0, pg), (1, pu)):
                        for kc in range(8):
                            self.mm(self.ps[pb][:, :], fw.wg[s][:, kc, gu, :], fw.h[:, tt, kc, :], kc == 0, kc == 7, [fw.b_wg[s], fw.b_h[tt]], [self.psb[pb]])
                    self.act(fw.sg[par][:, :], self.ps[pg][:, :], AF.Silu, [self.psb[pg]], [fw.b_sg[par]])
                    self.tt(fw.act[:, f, tt * 512:(tt + 1) * 512], fw.sg[par][:, :], self.ps[pu][:, :], ALU.mult, [fw.b_sg[par], self.psb[pu]], [fw.b_act[f][tt]])
            for j in range(8):
                s = j % 2
                self.load(fw.wd[s][:, :, :], wdn_d[j, :, :, :], fw.b_wd[s], eng="pool")
                for tt in range(2):
                    pb = 4 + (j * 2 + tt) % 2
                    t0 = half * 1024 + tt * 512
                    for f in range(NF):
                        self.mm(self.ps[pb][:, :], fw.wd[s][:, f, :], fw.act[:, f, tt * 512:(tt + 1) * 512], f == 0, f == NF - 1, [fw.b_wd[s], fw.b_act[f][tt]], [self.psb[pb]])
                    self.stt(x[:, j, t0:t0 + 512], self.ps[pb][:, :], gate[:, j:j + 1], x[:, j, t0:t0 + 512], ALU.mult, ALU.add, [self.psb[pb], m.b, bx], [bx])


def emit_mod(Bd, cT, wmod, bmod, ngd, stage_bf, stage_bufs, a):
    cond = a.cond; b_cond = Buf()
    Bd.load(cond[:, :], cT[:, :], b_cond)
    Bd.act(a.condb[:, :], cond[:, :], AF.Silu, [b_cond], [b_cond])
    cond = a.condb
    bm = a.bm; b_bm = Buf()
    Bd.load(bm[:, :], bmod[:, :], b_bm)
    ng = a.ng; b_ng = Buf()
    Bd.load(ng[:, :, :], ngd[:, :, :], b_ng)
    modsb = a.modsb; b_modsb = Buf()
    pm = 6
    for j9 in range(9):
        for hh in range(2):
            s = (j9 * 2 + hh) % 2
            stage = stage_bf[s]
            col0 = j9 * 1024 + hh * 512
            Bd.load(stage[:, :, :], wmod[0:128, :, col0:col0 + 512], stage_bufs[s], eng="pool")
            for jc in range(4):
                oc = j9 * 8 + hh * 4 + jc
                for kc in range(8):
                    Bd.mm(Bd.ps[pm][:, oc:oc + 1], stage[:, kc, jc * 128:(jc + 1) * 128], cond[:, kc:kc + 1], kc == 0, kc == 7, [stage_bufs[s], b_cond], [Bd.psb[pm]])
    Bd.tt(modsb[:, :], Bd.ps[pm][:, 0:72], bm[:, :], ALU.add, [Bd.psb[pm], b_bm], [b_modsb])
    return modsb, b_modsb, ng, b_ng


def emit_proj(Bd, x, bx, m, fw, di, do, final):
    S = Bd.S
    ps, psb = Bd.ps, Bd.psb
    ones = Bd.ones_bf
    b_ones = Bd.b_ones

    def wload(name, shape, dt, eng="pool"):
        t = Bd.sb(name, shape, dt); b = Buf(name)
        src = di[name]
        Bd.load(t[tuple(slice(None) for _ in shape)], src[(slice(0, shape[0]),) + tuple(slice(None) for _ in shape[1:])], b, eng=eng)
        return t, b
    win, b_win = wload("win", [128, 8, 1824], BF16)
    winv, b_winv = wload("winv", [128, 8, 384], BF16)
    wuq, b_wuq = wload("wuq", [128, 2, 576], BF16)
    wukvk, b_wukvk = wload("wukvk", [128, 6, 64], BF16)
    wukvv, b_wukvv = wload("wukvv", [128, 384], BF16)
    qan, b_qan = wload("qan", [128, 2], F32, "sp")
    kvan, b_kvan = wload("kvan", [128, 1], F32, "sp")
    hn, b_hn = wload("hn", [96, 4], F32, "sp")
    rot, b_rot = wload("rot", [96, 96], F32, "sp")
    freq, b_freq = wload("freq", [96, 1], F32, "sp")

    Ct = Bd.sb("ropeC", [96, T], F32); b_C = Buf()
    St = Bd.sb("ropeS", [96, T], F32); b_S = Buf()
    mk_rope = Bd.mark()
    ang = Bd.sb("ang", [96, T], F32); b_ang = Buf()
    rt = Bd.sb("rt", [96, T], F32); b_rt = Buf()
    ki = Bd.sb("ki", [96, T], I32); b_ki = Buf()
    posi = ki; b_posi = b_ki
    Bd.load(posi[:, :], di["pos"][0:1, :].partition_broadcast(96), b_posi)
    C1 = 6.28125
    C2 = TWO_PI - C1
    Bd.copy(ang[:, :], posi[:, :], [b_posi], [b_ang])
    Bd.ts(ang[:, :], ang[:, :], freq[:, 0:1], None, ALU.mult, None, [b_ang, b_freq], [b_ang])
    for (dst, bd, shift) in ((St, b_S, 0.0), (Ct, b_C, math.pi / 2)):
        Bd.ts(rt[:, :], ang[:, :], shift, 1.0 / TWO_PI, ALU.add, ALU.mult, [b_ang], [b_rt])
        Bd.copy(ki[:, :], rt[:, :], [b_rt], [b_ki])
        Bd.copy(rt[:, :], ki[:, :], [b_ki], [b_rt])
        Bd.stt(dst[:, :], rt[:, :], -C1, ang[:, :], ALU.mult, ALU.add, [b_rt, b_ang], [bd])
        Bd.stt(dst[:, :], rt[:, :], -C2, dst[:, :], ALU.mult, ALU.add, [b_rt, bd], [bd])
        Bd.ts(dst[:, :], dst[:, :], shift, math.pi, ALU.add, ALU.min, [bd], [bd])
        Bd.ts(dst[:, :], dst[:, :], -math.pi, None, ALU.max, None, [bd], [bd])
        Bd.act(dst[:, :], dst[:, :], AF.Sin, [bd], [bd])
    S.barrier()
    Bd.release(mk_rope)

    h2 = Bd.sb("h2", [128, 8, 512], BF16); b_h2 = Buf()
    cqn = Bd.sb("cqn", [128, 2, 512], BF16); b_cqn = Buf()
    ckvn = Bd.sb("ckvn", [128, 512], BF16); b_ckvn = Buf()
    sqw = [Bd.sb(f"sqw{i}", [128, 2, 512], BF16) for i in range(2)]; b_sqw = [Buf() for i in range(2)]
    rsw = [Bd.sb(f"rsw{i}", [128, 512], F32) for i in range(2)]; b_rsw = [Buf() for i in range(2)]
    xnw = [Bd.sb(f"xnw{i}", [96, 512], F32) for i in range(2)]; b_xnw = [Buf() for i in range(2)]
    t1w = [Bd.sb(f"t1w{i}", [96, 512], F32) for i in range(2)]; b_t1w = [Buf() for i in range(2)]

    def stg(name, shape):
        t = Bd.sb(name, shape, BF16); b = Buf()
        return [t, t], [b, b]
    st_qm, b_stqm = stg("stqm", [96, 6, 512])
    st_km, b_stkm = stg("stkm", [96, 6, 512])
    st_qs, b_stqs = stg("stqs", [64, 6, 512])
    st_ks, b_stks = stg("stks", [64, 2, 512])
    st_qc, b_stqc = stg("stqc", [64, 4, 512])
    st_kc, b_stkc = stg("stkc", [64, 4, 512])
    st_vm, b_stvm = stg("stvm", [128, 6, 65])
    st_vs, b_stvs = stg("stvs", [128, 2, 65])
    st_vc, b_stvc = stg("stvc", [128, 4, 65])
    Bd.memset(st_vs[0][:, :, :], 1.0, [b_stvs[0]])
    Bd.memset(st_vm[0][:, :, :], 1.0, [b_stvm[0]])
    Bd.memset(st_vc[0][:, :, :], 1.0, [b_stvc[0]])

    A = m.A[:, 1, :]
    Bv = m.B(1)
    cnt = [0]

    def nxt():
        cnt[0] += 1
        return cnt[0] % 2

    PSS = 6

    def head_chain(proj, pb, rows, gain, n, out_ap, outbuf, rope, t0, w, pss):
        pin = ps[pb][0:rows, :]
        st = [proj,
              lambda: Bd.act(sqw[w][0:rows, 0, :], pin, AF.Square, [psb[pb]], [b_sqw[w]]),
              lambda: Bd.mm(ps[pss][0:rows, :], ones[0:rows, 0:rows], sqw[w][0:rows, 0, :], True, True, [b_sqw[w], b_ones], [psb[pss]]),
              lambda: Bd.act(rsw[w][0:rows, :], ps[pss][0:rows, :], AF.Ln, [psb[pss]], [b_rsw[w]], scale=1.0 / n, bias=EPS),
              lambda: Bd.act(rsw[w][0:rows, :], rsw[w][0:rows, :], AF.Exp, [b_rsw[w]], [b_rsw[w]], scale=-0.5)]
        if not rope:
            st.append(lambda: Bd.stt(out_ap, pin, gain, rsw[w][0:rows, :], ALU.mult, ALU.mult, [psb[pb], b_rsw[w], b_hn], [outbuf]))
            return st
        st += [lambda: Bd.stt(xnw[w][:, :], pin, gain, rsw[w][0:rows, :], ALU.mult, ALU.mult, [psb[pb], b_rsw[w], b_hn], [b_xnw[w]]),
               lambda: Bd.mm(ps[pss][0:96, :], rot[:, :], xnw[w][:, :], True, True, [b_rot, b_xnw[w]], [psb[pss]]),
               lambda: Bd.tt(t1w[w][:, :], xnw[w][:, :], Ct[:, t0:t0 + 512], ALU.mult, [b_xnw[w], b_C], [b_t1w[w]]),
               lambda: Bd.tt(xnw[w][:, :], ps[pss][0:96, :], St[:, t0:t0 + 512], ALU.mult, [psb[pss], b_S], [b_xnw[w]]),
               lambda: Bd.tt(out_ap, t1w[w][:, :], xnw[w][:, :], ALU.add, [b_t1w[w], b_xnw[w]], [outbuf])]
        return st

    def lockstep(chains):
        for s in range(max(len(c) for c in chains)):
            for c in chains:
                if s < len(c):
                    c[s]()

    def heads_lockstep(nheads, pb0, mk_proj, rows, gain, n, out_fn, outbuf, rope, t0):
        for h0 in range(0, nheads, 2):
            chains = []
            for k in range(min(2, nheads - h0)):
                hh = h0 + k
                pb = pb0 + k
                chains.append(head_chain((lambda hh=hh, pb=pb: mk_proj(hh, pb)), pb, rows, gain, n, out_fn(hh), outbuf, rope, t0, k, 6 + k))
            lockstep(chains)

    def units(name):
        u = do[name]
        return u if isinstance(u, list) else [(u, 0, u.shape[0])]

    def store3(name, t0, src, srcbuf):
        o = do.get(name + "_off", 0)
        for (ap, h0, nh) in units(name):
            Bd.store(ap.rearrange("h p t -> p h t")[:, :, o + t0:o + t0 + 512], src[:, h0:h0 + nh, :], srcbuf, final=final, dstbuf=do.get(name + "_buf"))

    def storev(name, mblk, src, srcbuf):
        o = do.get(name + "_off", 0)
        for (ap, h0, nh) in units(name):
            Bd.store(ap.rearrange("h p m d -> p h m d")[:, :, o + mblk, :], src[:, h0:h0 + nh, :], srcbuf, final=final, dstbuf=do.get(name + "_buf"))

    for tt in range(4):
        t0 = tt * 512
        par = tt % 2
        Bd.modnorm_tile(x, bx, t0, A, Bv, m.b, h2, b_h2, fw.nw)
        for cc in range(2):
            for kc in range(8):
                Bd.mm(ps[cc][:, :], win[:, kc, cc * 128:(cc + 1) * 128], h2[:, kc, :], kc == 0, kc == 7, [b_win, b_h2], [psb[cc]])
        w = nxt()
        for cc in range(2):
            Bd.act(sqw[w][:, cc, :], ps[cc][:, :], AF.Square, [psb[cc]], [b_sqw[w]])
        for cc in range(2):
            Bd.mm(ps[PSS][:, :], ones[:, :], sqw[w][:, cc, :], cc == 0, cc == 1, [b_sqw[w], b_ones], [psb[PSS]])
        Bd.rstd(ps[PSS][:, :], 256.0, rsw[w][:, :], rsw[w][:, :], [psb[PSS]], b_rsw[w], b_rsw[w])
        for cc in range(2):
            Bd.stt(cqn[:, cc, :], ps[cc][:, :], qan[:, cc:cc + 1], rsw[w][:, :], ALU.mult, ALU.mult, [psb[cc], b_rsw[w], b_qan], [b_cqn])
        for kc in range(8):
            Bd.mm(ps[2][:, :], win[:, kc, 256:384], h2[:, kc, :], kc == 0, kc == 7, [b_win, b_h2], [psb[2]])
        w = nxt()
        Bd.act(sqw[w][:, 0, :], ps[2][:, :], AF.Square, [psb[2]], [b_sqw[w]])
        Bd.mm(ps[PSS][:, :], ones[:, :], sqw[w][:, 0, :], True, True, [b_sqw[w], b_ones], [psb[PSS]])
        Bd.rstd(ps[PSS][:, :], 128.0, rsw[w][:, :], rsw[w][:, :], [psb[PSS]], b_rsw[w], b_rsw[w])
        Bd.stt(ckvn[:, :], ps[2][:, :], kvan[:, 0:1], rsw[w][:, :], ALU.mult, ALU.mult, [psb[2], b_rsw[w], b_kvan], [b_ckvn])
        def proj_q(hh, pb):
            for kc in range(2):
                Bd.mm(ps[pb][0:96, :], wuq[:, kc, hh * 96:(hh + 1) * 96], cqn[:, kc, :], kc == 0, kc == 1, [b_wuq, b_cqn], [psb[pb]])
        heads_lockstep(6, 0, proj_q, 96, hn[0:96, 0:1], 96, lambda hh: st_qm[par][:, hh, :], b_stqm[par], True, t0)
        store3("qm", t0, st_qm[par], b_stqm[par])

        def proj_k(hh, pb):
            Bd.mm(ps[pb][0:64, :], wukvk[:, hh, :], ckvn[:, :], True, True, [b_wukvk, b_ckvn], [psb[pb]])
            for kc in range(8):
                Bd.mm(ps[pb][64:96, :], win[:, kc, 384:416], h2[:, kc, :], kc == 0, kc == 7, [b_win, b_h2], [psb[pb]])
        heads_lockstep(6, 2, proj_k, 96, hn[0:96, 1:2], 96, lambda hh: st_km[par][:, hh, :], b_stkm[par], True, t0)
        store3("km", t0, st_km[par], b_stkm[par])

        def proj_sq(hh, pb):
            for kc in range(8):
                Bd.mm(ps[pb][0:64, :], win[:, kc, 416 + hh * 64:416 + (hh + 1) * 64], h2[:, kc, :], kc == 0, kc == 7, [b_win, b_h2], [psb[pb]])
        heads_lockstep(6, 0, proj_sq, 64, hn[0:64, 2:3], 64, lambda hh: st_qs[par][:, hh, :], b_stqs[par], False, t0)
        store3("qs", t0, st_qs[par], b_stqs[par])

        def proj_sk(hh, pb):
            for kc in range(8):
                Bd.mm(ps[pb][0:64, :], win[:, kc, 800 + hh * 64:800 + (hh + 1) * 64], h2[:, kc, :], kc == 0, kc == 7, [b_win, b_h2], [psb[pb]])
        heads_lockstep(2, 2, proj_sk, 64, hn[0:64, 3:4], 64, lambda hh: st_ks[par][:, hh, :], b_stks[par], False, t0)
        store3("ks", t0, st_ks[par], b_stks[par])
        for hh in range(4):
            pb = hh % 2
            for kc in range(8):
                Bd.mm(ps[pb][0:64, :], win[:, kc, 1056 + hh * 64:1056 + (hh + 1) * 64], h2[:, kc, :], kc == 0, kc == 7, [b_win, b_h2], [psb[pb]])
            Bd.act(st_qc[par][:, hh, :], ps[pb][0:64, :], AF.Copy, [psb[pb]], [b_stqc[par]], scale=0.125)
        store3("qc", t0, st_qc[par], b_stqc[par])
        for hh in range(4):
            pb = 2 + hh % 2
            for kc in range(8):
                Bd.mm(ps[pb][0:64, :], win[:, kc, 1312 + hh * 64:1312 + (hh + 1) * 64], h2[:, kc, :], kc == 0, kc == 7, [b_win, b_h2], [psb[pb]])
            Bd.copy(st_kc[par][:, hh, :], ps[pb][0:64, :], [psb[pb]], [b_stkc[par]])
        store3("kc", t0, st_kc[par], b_stkc[par])
        for blk in range(4):
            mblk = tt * 4 + blk
            bp = blk % 2
            pb = 4 + bp
            Bd.mm(ps[pb][:, 0:384], ckvn[:, blk * 128:(blk + 1) * 128], wukvv[:, :], True, True, [b_ckvn, b_wukvv], [psb[pb]])
            Bd.act(st_vm[bp][:, :, 0:64], ps[pb][:, 0:384].rearrange("p (h d) -> p h d", h=6), AF.Copy, [psb[pb]], [b_stvm[bp]])
            storev("vm", mblk, st_vm[bp], b_stvm[bp])
            pb2 = bp
            for kc in range(8):
                Bd.mm(ps[pb2][:, 0:384], h2[:, kc, blk * 128:(blk + 1) * 128], winv[:, kc, :], kc == 0, kc == 7, [b_h2, b_winv], [psb[pb2]])
            Bd.copy(st_vs[bp][:, :, 0:64], ps[pb2][:, 0:128].rearrange("p (h d) -> p h d", h=2), [psb[pb2]], [b_stvs[bp]])
            Bd.copy(st_vc[bp][:, :, 0:64], ps[pb2][:, 128:384].rearrange("p (h d) -> p h d", h=4), [psb[pb2]], [b_stvc[bp]])
            storev("vs", mblk, st_vs[bp], b_stvs[bp])
            storev("vc", mblk, st_vc[bp], b_stvc[bp])


LO_B = _bucket_thresholds()
MLA_SCALE = 96.0 ** -0.5


def attn_consts(Bd, di, w, bias_cache=None, first=True):
    S = Bd.S
    a = Ctx()
    a.zer = Bd.sb("zer", [128, 128], BF16); a.b_zer = Buf()
    Bd.memset(a.zer[:, :], 0.0, [a.b_zer])
    a.zrhs = Bd.sb("zrhs", [128, 512], BF16); a.b_zrhs = Buf()
    Bd.memset(a.zrhs[:, :], 0.0, [a.b_zrhs])
    a.tri = Bd.sb("tri", [128, 2, 128], BF16); a.b_tri = Buf()
    Bd.load(a.tri[:, :, :], di["tricomp"][:, :, :], a.b_tri, eng="pool")
    a.mincl = Bd.sb("mincl", [128, 4, 128], F32); a.b_mincl = Buf()
    Bd.load(a.mincl[:, :, :], di["mincl"][:, :, :], a.b_mincl)
    a.mstr = Bd.sb("mstr", [128, 4, 128], F32); a.b_mstr = Buf()
    Bd.load(a.mstr[:, :, :], di["mstrict"][:, :, :], a.b_mstr)
    a.sel = Bd.sb("sel", [65, 64], F32); a.b_sel = Buf()
    Bd.memset(a.sel[:, :], 0.0, [a.b_sel])
    Bd.memset(a.sel[64:65, :], 1.0, [a.b_sel])
    a.vsink = Bd.sb("vsink", [1, 65], F32); a.b_vsink = Buf()
    Bd.memset(a.vsink[:, :], 0.0, [a.b_vsink])
    Bd.memset(a.vsink[0:1, 64:65], 1.0, [a.b_vsink])
    a.wsel = Bd.sb("wsel", [128, 4], F32); a.b_wsel = Buf()
    Bd.load(a.wsel[:, :], di["wsel"][:, :], a.b_wsel)
    a.onorm = Bd.sb("onorm", [64, 16], F32); a.b_onorm = Buf()
    Bd.load(a.onorm[:, :], di["onorm"][:, :], a.b_onorm)
    es = Bd.sb("es", [1, 6], F32); b_es = Buf()
    Bd.load(es[:, :], di["sinks"][:, :], b_es)
    Bd.act(es[:, :], es[:, :], AF.Exp, [b_es], [b_es])
    a.esr = Bd.sb("esr", [1, 6, 128], F32); a.b_esr = Buf()
    Bd.copy(a.esr[:, :, :], es[0:1, :, None].to_broadcast([1, 6, 128]), [b_es], [a.b_esr])
    a.bias = Bd.sb("swabias", [128, 2, 6, 128], F32); a.b_bias = Buf()
    if bias_cache is not None and not first:
        Bd.load(a.bias[:, :, :, :].rearrange("p w h q -> p (w h q)"), bias_cache[0].ap(), a.b_bias, reads=[bias_cache[1]])
    else:
        tb = w.e[0][:, 0:192].rearrange("p (b h) -> p b h", h=6); b_tb = w.b_e[0]
        Bd.load(w.e[0][:, 0:192], di["relb"][0:1, :].partition_broadcast(128), b_tb)
        dtb = w.e[1][:, 0:186].rearrange("p (b h) -> p b h", h=6); b_dtb = w.b_e[1]
        Bd.tt(dtb[:, :, :], tb[:, 1:32, :], tb[:, 0:31, :], ALU.subtract, [b_tb], [b_dtb])
        pqi = w.e[3][:, 256:384].bitcast(I32); b_pqi = w.b_e[3]
        Bd.load(pqi, di["posrow"][0:1, :].partition_broadcast(128), b_pqi)
        pki = w.e[3][:, 384:386].bitcast(I32); b_pki = w.b_e[3]
        Bd.load(pki, di["poscol"][:, :], b_pki)
        pq = w.e[2][:, 0:128]; b_pq = w.b_e[2]
        pk = w.e[2][:, 128:130]; b_pk = w.b_e[2]
        Bd.copy(pq, pqi, [b_pqi], [b_pq])
        Bd.copy(pk, pki, [b_pki], [b_pk])
        rel = w.ec[0][:, 0:256].rearrange("p (w q) -> p w q", w=2); b_rel = w.b_ec[0]
        for wi in range(2):
            Bd.ts(rel[:, wi, :], pq, pk[:, wi:wi + 1], None, ALU.subtract, None, [b_pq, b_pk], [b_rel])
        ind = w.ec[1][:, 0:256].rearrange("p (w q) -> p w q", w=2); b_ind = w.b_ec[1]
        for h in range(6):
            Bd.ts(a.bias[:, :, h, :], rel[:, :, :], 0.0, tb[:, 0, h:h + 1], ALU.mult, ALU.add, [b_rel, b_tb], [a.b_bias])
        for b in range(1, 32):
            Bd.ts(ind[:, :, :], rel[:, :, :], float(LO_B[b - 1]), None, ALU.is_ge, None, [b_rel], [b_ind])
            for h in range(6):
                Bd.stt(a.bias[:, :, h, :], ind[:, :, :], dtb[:, b - 1, h:h + 1], a.bias[:, :, h, :], ALU.mult, ALU.add, [b_ind, b_dtb, a.b_bias], [a.b_bias])
        val = w.e[1][:, 256:512].rearrange("p (w q) -> p w q", w=2); b_val = w.b_e[1]
        Bd.ts(val[:, :, :], rel[:, :, :], 0.0, None, ALU.is_ge, None, [b_rel], [b_val])
        Bd.ts(ind[:, :, :], rel[:, :, :], 128.0, None, ALU.is_ge, None, [b_rel], [b_ind])
        Bd.ts(ind[:, :, :], ind[:, :, :], -1.0, 1.0, ALU.mult, ALU.add, [b_ind], [b_ind])
        Bd.tt(val[:, :, :], val[:, :, :], ind[:, :, :], ALU.mult, [b_val, b_ind], [b_val])
        Bd.ts(ind[:, :, :], val[:, :, :], 1.0e4, -1.0e4, ALU.mult, ALU.add, [b_val], [b_ind])
        for h in range(6):
            Bd.tt(a.bias[:, :, h, :], a.bias[:, :, h, :], val[:, :, :], ALU.mult, [a.b_bias, b_val], [a.b_bias])
            Bd.tt(a.bias[:, :, h, :], a.bias[:, :, h, :], ind[:, :, :], ALU.add, [a.b_bias, b_ind], [a.b_bias])

        if bias_cache is not None:
            Bd.store(bias_cache[0].ap(), a.bias[:, :, :, :].rearrange("p w h q -> p (w h q)"), a.b_bias, eng="sp", dstbuf=bias_cache[1])
    return a


def attn_work(Bd):
    w = Ctx()
    w.kbuf = [Bd.sb(f"kbuf{i}", [96, 4, T], BF16) for i in range(2)]; w.b_k = [Buf() for _ in range(2)]
    w.vbuf = [Bd.sb(f"vbuf{i}", [128, 4, NBLK, 65], BF16) for i in range(2)]; w.b_v = [Buf() for _ in range(2)]
    w.og = Bd.sb("ogrp", [64, 6, 512], F32); w.b_og = [Buf() for _ in range(6)]
    w.mix = Bd.sb("mix", [64, 16, 512], BF16); w.b_mix = [Buf() for _ in range(16)]
    w.qm = Bd.sb("qmg", [96, 6, 512], BF16); w.b_qm = Buf()
    w.qc = Bd.sb("qcg", [64, 4, 512], BF16); w.b_qc = Buf()
    w.qs = Bd.sb("qsg", [64, 6, 512], BF16); w.b_qs = Buf()
    w.p = [Bd.sb(f"pw{i}", [128, 512], BF16) for i in range(2)]; w.b_p = [Buf() for _ in range(2)]
    w.e = [Bd.sb(f"ew{i}", [128, 512], F32) for i in range(4)]; w.b_e = [Buf() for _ in range(4)]
    w.sp = [Bd.sb(f"spw{i}", [128, 512], BF16) for i in range(4)]; w.b_sp = [Buf() for _ in range(4)]
    w.ec = [Bd.sb(f"ecw{i}", [128, 512], F32) for i in range(2)]; w.b_ec = [Buf() for _ in range(2)]
    w.a = w.p; w.b_a = w.b_p
    w.oa = Bd.sb("oa", [65, 512], F32); w.b_oa = Buf()
    w.rden = Bd.sb("rden", [64, 512], F32); w.b_rden = Buf()
    w.ksw = Bd.sb("ksw", [64, 2, 2, 512], BF16); w.b_ksw = Buf()
    w.vsw = Bd.sb("vsw", [128, 2, 2, 4, 65], BF16); w.b_vsw = Buf()
    w.sarg = [w.e[i][:, 0:384].rearrange("p (j q) -> p j q", j=3) for i in range(2)]; w.b_sarg = w.b_e[0:2]
    w.psw = [w.sp[i][:, 0:384].rearrange("p (j q) -> p j q", j=3) for i in range(2)]; w.b_psw = w.b_sp[0:2]
    w.gsq = w.sp[2][0:64, :]; w.b_gsq = w.b_sp[2]
    w.grs = w.e[2][0:64, :]; w.b_grs = w.b_e[2]
    wo = Bd.sb("wo", [64, 16, 128], BF16); bwo = Buf()
    w.wo = [wo, wo]; w.b_wo = [bwo, bwo]
    return w


def emit_attention(Bd, x, bx, m, ac, w, di):
    S = Bd.S
    ps, psb = Bd.ps, Bd.psb
    ones = Bd.ones_bf
    for G in (3, 2, 1, 0):
        t0 = G * 512
        nk = 4 * G + 4
        ntok = nk * 128
        nsteps = 16 * G + 16
        Bd.load(w.qm[:, :, :], di["qm"].rearrange("h p t -> p h t")[:, :, t0:t0 + 512], w.b_qm, reads=[di["b_q"]])
        Bd.load(w.qc[:, :, :], di["qc"].rearrange("h p t -> p h t")[:, :, t0:t0 + 512], w.b_qc, reads=[di["b_q"]])
        Bd.load(w.qs[:, :, :], di["qs"].rearrange("h p t -> p h t")[:, :, t0:t0 + 512], w.b_qs, reads=[di["b_q"]])

        def kvload(slot, kf, vf, h, rows):
            Bd.load(w.kbuf[slot][0:rows, :, 0:ntok], di[kf][h][0].rearrange("r p t -> p r t")[:, :, 0:ntok], w.b_k[slot], reads=[di[kf][h][1]])
            Bd.load(w.vbuf[slot][:, :, 0:nk, :], di[vf][h][0].rearrange("r p m d -> p r m d")[:, :, 0:nk, :], w.b_v[slot], reads=[di[vf][h][1]])

        def step_geom(i):
            kb = nsteps - 1 - i
            kw = kb - 16 * G
            c0 = (kw // 4) * 128 if kw >= 0 else 0
            return kb % 4, kb // 4, c0, (kw % 4 if kw >= 0 else None)

        for pair in range(3):
            hs = (2 * pair, 2 * pair + 1)
            for hi, h in enumerate(hs):
                kvload(hi, "kmf", "vmf", h, 96)
            PS = {0: (0, 1), 1: (2, 7)}
            POm = {0: 3, 1: 4}
            PT = {0: ((w.p[0], w.b_p[0]), (w.p[1], w.b_p[1])), 1: ((w.sp[2], w.b_sp[2]), (w.sp[3], w.b_sp[3]))}
            for hi in range(2):
                Bd.mm(ps[POm[hi]][0:65, :], ac.zer[:, 0:65], ac.zrhs[:, :], True, False, [ac.b_zer, ac.b_zrhs], [psb[POm[hi]]])

            def m_s(hi, i):
                rk, mb, c0, dm = step_geom(i)
                pb = PS[hi][i % 2]
                Bd.mm(ps[pb][:, c0:512], w.kbuf[hi][0:96, rk, mb * 128:(mb + 1) * 128], w.qm[0:96, hs[hi], c0:512], True, True, [w.b_k[hi], w.b_qm], [psb[pb]])

            def m_e(hi, i):
                rk, mb, c0, dm = step_geom(i)
                pb = PS[hi][i % 2]
                pt, bpt = PT[hi][i % 2]
                Bd.act(pt[:, c0:512], ps[pb][:, c0:512], AF.Exp, [psb[pb]], [bpt], scale=MLA_SCALE)
                if dm is not None:
                    Bd.tt(pt[:, c0:c0 + 128], pt[:, c0:c0 + 128], ac.mincl[:, dm, :], ALU.mult, [bpt, ac.b_mincl], [bpt])

            def m_pv(hi, i):
                rk, mb, c0, dm = step_geom(i)
                pt, bpt = PT[hi][i % 2]
                Bd.mm(ps[POm[hi]][0:65, c0:512], w.vbuf[hi][:, rk, mb, 0:65], pt[:, c0:512], False, i == nsteps - 1, [w.b_v[hi], bpt], [psb[POm[hi]]])
            for hi in range(2):
                m_s(hi, 0)
            for hi in range(2):
                m_e(hi, 0)
            for i in range(nsteps):
                if i + 1 < nsteps:
                    for hi in range(2):
                        m_s(hi, i + 1)
                for hi in range(2):
                    m_pv(hi, i)
                if i + 1 < nsteps:
                    for hi in range(2):
                        m_e(hi, i + 1)
            for hi, h in enumerate(hs):
                po = POm[hi]
                Bd.act(w.oa[:, :], ps[po][0:65, :], AF.Copy, [psb[po]], [w.b_oa])
                Bd.mm(ps[5][0:64, :], ac.sel[:, :], w.oa[:, :], True, True, [ac.b_sel, w.b_oa], [psb[5]])
                Bd.recip(w.rden[:, :], ps[5][0:64, :], [psb[5]], [w.b_rden])
                Bd.tt(w.og[:, h, :], w.oa[0:64, :], w.rden[:, :], ALU.mult, [w.b_oa, w.b_rden], [w.b_og[h]])
        group_norm(Bd, w, ac, 0, 6, 384.0, G)

        Bd.load(w.ksw[:, :, 1, :], di["ks_own"].rearrange("h p t -> p h t")[:, :, 128 + t0:128 + t0 + 512], w.b_ksw, reads=[di["b_src"]])
        Bd.load(w.vsw[:, :, 1, :, :], di["vs_own"].rearrange("h p m d -> p h m d")[:, :, 1 + 4 * G:5 + 4 * G, :], w.b_vsw, reads=[di["b_src"]])
        for c in range(4):
            ko = 128 + t0 if c < 3 else t0
            vo = 1 + 4 * G if c < 3 else 4 * G
            kcand = w.kbuf[0][0:64, c, 0:1024].rearrange("p (k t) -> p k t", k=2)
            vcand = w.vbuf[0][:, c, 0:8, :].rearrange("p (k m) d -> p k m d", k=2)
            Bd.load(kcand, di["ksf"].rearrange("r h p t -> p r h t")[:, c, :, ko:ko + 512], w.b_k[0], reads=[di["b_ks"]])
            Bd.load(vcand, di["vsf"].rearrange("r h p m d -> p r h m d")[:, c, :, vo:vo + 4, :], w.b_v[0], reads=[di["b_vs"]])
        for c in range(4):
            kcand = w.kbuf[0][0:64, c, 0:1024].rearrange("p (k t) -> p k t", k=2)
            vcand = w.vbuf[0][:, c, 0:8, :].rearrange("p (k m) d -> p k m d", k=2)
            if c == 0:
                Bd.ts(w.ksw[:, :, 0, :], kcand, ac.wsel[0:64, 0:1], None, ALU.mult, None, [w.b_k[0], ac.b_wsel], [w.b_ksw])
                Bd.ts(w.vsw[:, :, 0, :, :], vcand, ac.wsel[:, 0:1], None, ALU.mult, None, [w.b_v[0], ac.b_wsel], [w.b_vsw])
            else:
                Bd.stt(w.ksw[:, :, 0, :], kcand, ac.wsel[0:64, c:c + 1], w.ksw[:, :, 0, :], ALU.mult, ALU.add, [w.b_k[0], ac.b_wsel, w.b_ksw], [w.b_ksw])
                Bd.stt(w.vsw[:, :, 0, :, :], vcand, ac.wsel[:, c:c + 1], w.vsw[:, :, 0, :, :], ALU.mult, ALU.add, [w.b_v[0], ac.b_wsel, w.b_vsw], [w.b_vsw])
        for blk in range(4):
            q0 = blk * 128
            for kvh in range(2):
                pso = 3 + kvh
                Bd.mm(ps[pso][0:65, 0:384], ac.zer[:, 0:65], ac.zrhs[:, 0:384], True, False, [ac.b_zer, ac.b_zrhs], [psb[pso]])
                for wh in range(2):
                    pss = (blk * 4 + kvh * 2 + wh) % 2
                    for j in range(3):
                        hq = kvh * 3 + j
                        Bd.mm(ps[pss][:, j * 128:(j + 1) * 128], w.ksw[:, kvh, wh, q0:q0 + 128], w.qs[:, hq, q0:q0 + 128], True, True, [w.b_ksw, w.b_qs], [psb[pss]])
                    Bd.stt(w.sarg[pss][:, :, :], ps[pss][:, 0:384].rearrange("p (j q) -> p j q", j=3), 0.125, ac.bias[:, wh, kvh * 3:(kvh + 1) * 3, :], ALU.mult, ALU.add, [psb[pss], ac.b_bias], [w.b_sarg[pss]])
                    Bd.act(w.psw[pss][:, :, :], w.sarg[pss][:, :, :], AF.Exp, [w.b_sarg[pss]], [w.b_psw[pss]])
                    for j in range(3):
                        Bd.mm(ps[pso][0:65, j * 128:(j + 1) * 128], w.vsw[:, kvh, wh, blk, 0:65], w.psw[pss][:, j, :], False, False, [w.b_vsw, w.b_psw[pss]], [psb[pso]])
                for j in range(3):
                    hq = kvh * 3 + j
                    Bd.mm(ps[pso][0:65, j * 128:(j + 1) * 128], ac.vsink[0:1, :], ac.esr[0:1, hq, :], False, True, [ac.b_vsink, ac.b_esr], [psb[pso]])
                Bd.act(w.oa[:, 0:384], ps[pso][0:65, 0:384], AF.Copy, [psb[pso]], [w.b_oa])
                Bd.mm(ps[5][0:64, 0:384], ac.sel[:, :], w.oa[:, 0:384], True, True, [ac.b_sel, w.b_oa], [psb[5]])
                Bd.recip(w.rden[:, 0:384], ps[5][0:64, 0:384], [psb[5]], [w.b_rden])
                for j in range(3):
                    hq = kvh * 3 + j
                    Bd.tt(w.og[:, hq, q0:q0 + 128], w.oa[0:64, j * 128:(j + 1) * 128], w.rden[:, j * 128:(j + 1) * 128], ALU.mult, [w.b_oa, w.b_rden], [w.b_og[hq]])
        group_norm(Bd, w, ac, 6, 6, 384.0, G)

        for pair in range(2):
            hs = (2 * pair, 2 * pair + 1)
            for hi, h in enumerate(hs):
                kvload(hi, "kcf", "vcf", h, 64)
            PZ = {0: (0, 1), 1: (2, 7)}
            PC = {0: 3, 1: 4}
            PO = {0: 5, 1: 6}
            for hi in range(2):
                Bd.mm(ps[PC[hi]][:, :], ac.zer[:, :], ac.zrhs[:, :], True, False, [ac.b_zer, ac.b_zrhs], [psb[PC[hi]]])
                Bd.mm(ps[PO[hi]][0:64, :], ac.zer[:, 0:64], ac.zrhs[:, :], True, False, [ac.b_zer, ac.b_zrhs], [psb[PO[hi]]])

            def s_z(hi, i):
                h = hs[hi]
                rk, mb, c0, dm = step_geom(i)
                pz = PZ[hi][i % 2]
                Bd.mm(ps[pz][:, c0:512], w.kbuf[hi][0:64, rk, mb * 128:(mb + 1) * 128], w.qc[0:64, h, c0:512], True, True, [w.b_k[hi], w.b_qc], [psb[pz]])

            def s_e(hi, i):
                rk, mb, c0, dm = step_geom(i)
                pz = PZ[hi][i % 2]
                ew = hi * 2 + i % 2
                Bd.act(w.e[ew][:, c0:512], ps[pz][:, c0:512], AF.Exp, [psb[pz]], [w.b_e[ew]])
                if dm is not None:
                    Bd.tt(w.e[ew][:, c0:c0 + 128], w.e[ew][:, c0:c0 + 128], ac.mstr[:, dm, :], ALU.mult, [w.b_e[ew], ac.b_mstr], [w.b_e[ew]])

            def s_sp(hi, i):
                rk, mb, c0, dm = step_geom(i)
                ew = hi * 2 + i % 2
                Bd.act(w.sp[ew][:, c0:512], w.e[ew][:, c0:512], AF.Ln, [w.b_e[ew]], [w.b_sp[ew]], bias=1.0)

            def s_tri(hi, i):
                rk, mb, c0, dm = step_geom(i)
                ew = hi * 2 + i % 2
                Bd.mm(ps[PC[hi]][:, c0:512], ac.tri[:, 0, :], w.sp[ew][:, c0:512], False, False, [ac.b_tri, w.b_sp[ew]], [psb[PC[hi]]])

            def s_ec(hi, i):
                rk, mb, c0, dm = step_geom(i)
                Bd.act(w.ec[hi][:, c0:512], ps[PC[hi]][:, c0:512], AF.Exp, [psb[PC[hi]]], [w.b_ec[hi]], scale=-1.0)

            def s_a(hi, i):
                rk, mb, c0, dm = step_geom(i)
                ew = hi * 2 + i % 2
                Bd.tt(w.a[hi][:, c0:512], w.e[ew][:, c0:512], w.ec[hi][:, c0:512], ALU.mult, [w.b_e[ew], w.b_ec[hi]], [w.b_a[hi]])

            def s_co(hi, i):
                rk, mb, c0, dm = step_geom(i)
                ew = hi * 2 + i % 2
                pc, po = PC[hi], PO[hi]
                Bd.mm(ps[pc][:, c0:512], ac.tri[:, 1, :], w.sp[ew][:, c0:512], False, i == nsteps - 1, [ac.b_tri, w.b_sp[ew]], [psb[pc]])
                Bd.mm(ps[po][0:64, c0:512], w.vbuf[hi][:, rk, mb, 0:64], w.a[hi][:, c0:512], False, i == nsteps - 1, [w.b_v[hi], w.b_a[hi]], [psb[po]])
            for hi in range(2):
                s_z(hi, 0)
            for hi in range(2):
                s_e(hi, 0)
            for hi in range(2):
                s_sp(hi, 0)
            for i in range(nsteps):
                nxt_ = i + 1 < nsteps
                for hi in range(2):
                    s_tri(hi, i)
                if nxt_:
                    for hi in range(2):
                        s_z(hi, i + 1)
                for hi in range(2):
                    s_ec(hi, i)
                if nxt_:
                    for hi in range(2):
                        s_e(hi, i + 1)
                for hi in range(2):
                    s_a(hi, i)
                if nxt_:
                    for hi in range(2):
                        s_sp(hi, i + 1)
                for hi in range(2):
                    s_co(hi, i)
            for hi, h in enumerate(hs):
                Bd.act(w.og[:, h, :], ps[PO[hi]][0:64, :], AF.Copy, [psb[PO[hi]]], [w.b_og[h]])
        group_norm(Bd, w, ac, 12, 4, 256.0, G)

        for j in range(8):
            s = j % 2
            Bd.load(w.wo[s][:, :, :], di["wout"][j, :, :, :], w.b_wo[s], eng="pool")
            pb = j % 2
            for hh in range(16):
                Bd.mm(ps[pb][:, :], w.wo[s][:, hh, :], w.mix[:, hh, :], hh == 0, hh == 15, [w.b_wo[s], w.b_mix[hh]], [psb[pb]])
            Bd.stt(x[:, j, t0:t0 + 512], ps[pb][:, :], m.gate[:, 1, j:j + 1], x[:, j, t0:t0 + 512], ALU.mult, ALU.add, [psb[pb], m.b, bx], [bx])


DEBUG = False


def group_norm(Bd, w, ac, h0, nh, width, G=0):
    ps, psb = Bd.ps, Bd.psb
    if DEBUG:
        for i in range(nh):
            Bd.store(Bd.outs["dbg_og"][G, :, h0 + i, :], w.og[:, i, :], w.b_og[i], eng="sp", final=True)
    for i in range(nh):
        Bd.act(w.gsq[:, :], w.og[:, i, :], AF.Square, [w.b_og[i]], [w.b_gsq])
        Bd.mm(ps[7][0:64, :], Bd.ones_bf[0:64, 0:64], w.gsq[:, :], i == 0, i == nh - 1, [w.b_gsq, Bd.b_ones], [psb[7]])
    Bd.rstd(ps[7][0:64, :], width, w.grs[:, :], w.grs[:, :], [psb[7]], w.b_grs, w.b_grs)
    for i in range(nh):
        Bd.stt(w.mix[:, h0 + i, :], w.og[:, i, :], ac.onorm[:, h0 + i:h0 + i + 1], w.grs[:, :], ALU.mult, ALU.mult, [w.b_og[i], ac.b_onorm, w.b_grs], [w.b_mix[h0 + i]])


LAYER_IN = dict(wmod=[129, 8, 9216], bmod=[128, 72], ng=[128, 3, 8], wgu1=[NF + 1, 128, 8, 2, 128], wdn1=[9, 128, NF, 128],
                win=[129, 8, 1824], winv=[129, 8, 384], qan=[128, 2], kvan=[128, 1], wuq=[128, 2, 576], wukvk=[128, 6, 64],
                wukvv=[128, 384], hnorms=[96, 4], wout=[9, 64, 16, 128], onorm=[64, 16], sinks=[1, 6],
                wgu2=[NF + 1, 128, 8, 2, 128], wdn2=[9, 128, NF, 128])
RG = [[0, 1, 2, 3], [4, 5, 6, 7]]


def build_F():
    Bd = Builder()
    S = Bd.S
    nc = Bd.nc
    xT = Bd.inp("xT", [D, T])
    xoT = Bd.outp("xoT", [D, T])
    cT = Bd.inp("cT", [128, 8])
    shared_in = dict(pos=Bd.inp("pos", [1, T], I32), freq=Bd.inp("freq", [96, 1]), rot=Bd.inp("rotT", [96, 96]),
                     tricomp=Bd.inp("tricomp", [128, 2, 128]), mincl=Bd.inp("mincl", [128, 4, 128]),
                     mstrict=Bd.inp("mstrict", [128, 4, 128]), relb=Bd.inp("relb", [1, 192]),
                     posrow=Bd.inp("posrow", [1, 128], I32), poscol=Bd.inp("poscol", [128, 2], I32),
                     wsel=Bd.inp("wsel", [128, 4]))
    L = [{k: Bd.inp(f"{k}_{l}", shp) for k, shp in LAYER_IN.items()} for l in range(2)]

    x = Bd.sb("x", [128, 8, T], F32); bx = Buf("x")
    for kc in range(8):
        Bd.load(x[:, kc, :], xT[kc * 128:(kc + 1) * 128, :], bx)
    ma = Bd.mod_alloc()
    zt = Bd.sb("zt", [128, 1105], BF16); b_zt = Buf()
    Bd.memset(zt[:, :], 0.0, [b_zt])
    mk0 = Bd.mark()
    bias_cache = (nc.dram_tensor("bias_scr", [128, 2 * 6 * 128], F32), Buf("bias_scr"))
    for l in range(2):
        W = L[l]
        def dt2(name, rows, cols):
            return nc.dram_tensor(f"{name}_{l}", [rows, cols], BF16)
        q_km = dt2("sq_m", 6 * 96, T); q_qs = dt2("sq_s", 6 * 64, T); q_qc = dt2("sq_c", 4 * 64, T)
        UR = 192
        units_s, units_g = [], []

        units_b = []

        def unit():
            i = len(units_s)
            units_s.append(dt2(f"su{i}", UR, T))
            units_g.append(dt2(f"gu{i}", 4 * UR, T))
            units_b.append(Buf(f"gu{i}"))
            return units_s[-1].ap(), units_g[-1].ap().rearrange("(r n) t -> r n t", r=4)
        o_km, g_km, o_kc, g_kc, o_vm, g_vm, o_vc, g_vc = [], [], [], [], [], [], [], []
        for i in range(3):
            su, gu = unit()
            o_km.append((su[0:192, :].rearrange("(h p) t -> h p t", h=2), 2 * i, 2))
            g_km += [(gu[:, hh * 96:(hh + 1) * 96, :], units_b[-1]) for hh in range(2)]
        for i in range(2):
            su, gu = unit()
            o_kc.append((su[0:128, :].rearrange("(h p) t -> h p t", h=2), 2 * i, 2))
            g_kc += [(gu[:, hh * 64:(hh + 1) * 64, :], units_b[-1]) for hh in range(2)]
        for (ol, gl, n) in ((o_vm, g_vm, 3), (o_vc, g_vc, 2)):
            for i in range(n):
                su, gu = unit()
                ol.append((su[0:130, :].rearrange("n t -> (n t)").rearrange("(h p m d) -> h p m d", h=2, p=128, d=65), 2 * i, 2))
                gv = gu[:, 0:130, :].rearrange("r n t -> r (n t)").rearrange("r (h p m d) -> r h p m d", h=2, p=128, d=65)
                gl += [(gv[:, hh, :, :, :], units_b[-1]) for hh in range(2)]
        su, gu = unit()
        o_ks = su[0:136, :].rearrange("n t -> (n t)").rearrange("(h p u) -> h p u", h=2, p=64)
        g_ks = gu[:, 0:136, :].rearrange("r n t -> r (n t)").rearrange("r (h p u) -> r h p u", h=2, p=64)
        b_gks = units_b[-1]
        su, gu = unit()
        o_vs = su[0:139, :].rearrange("n t -> (n t)")[0:2 * 128 * 17 * 65].rearrange("(h p m d) -> h p m d", h=2, p=128, d=65)
        g_vs = gu[:, 0:139, :].rearrange("r n t -> r (n t)")[:, 0:2 * 128 * 17 * 65].rearrange("r (h p m d) -> r h p m d", h=2, p=128, d=65)
        b_gvs = units_b[-1]
        b_src = Buf("src", multi=True)
        b_q = Buf("qscr", multi=True)
        b_g = Buf("gath")
        Bd.store(o_ks.rearrange("h p u -> p h u")[:, :, 0:128], zt[0:64, 0:256].rearrange("p (h u) -> p h u", h=2), b_zt, dstbuf=b_src)
        Bd.store(o_vs.rearrange("h p m d -> p h m d")[:, :, 0, :], zt[:, 0:130].rearrange("p (h d) -> p h d", h=2), b_zt, dstbuf=b_src)
        nw = Bd.norm_work()
        fw = Bd.ffn_work(nw)
        stage = [fw.act[:, 4 * s_:4 * s_ + 4, :].rearrange("p a (b c) -> p (a b) c", c=512) for s_ in range(2)]
        modsb, b_modsb, ng, b_ng = emit_mod(Bd, cT, W["wmod"], W["bmod"], W["ng"], stage, [Buf(), Buf()], ma)
        m = Bd.derive_mod(modsb, b_modsb, ng, b_ng, ma)
        S.barrier()
        Bd.ffn(x, bx, W["wgu1"], W["wdn1"], m, 0, fw)
        S.barrier()
        Bd.release(mk0)
        nw = Bd.norm_work()
        fw2 = Ctx(); fw2.nw = nw
        di = dict(win=W["win"], winv=W["winv"], qan=W["qan"], kvan=W["kvan"], wuq=W["wuq"], wukvk=W["wukvk"], wukvv=W["wukvv"],
                  hn=W["hnorms"], pos=shared_in["pos"], freq=shared_in["freq"], rot=shared_in["rot"])
        do = dict(qm=q_km.ap().rearrange("(h p) t -> h p t", h=6), qs=q_qs.ap().rearrange("(h p) t -> h p t", h=6),
                  qc=q_qc.ap().rearrange("(h p) t -> h p t", h=4),
                  km=o_km, kc=o_kc, ks=o_ks, ks_off=128, vm=o_vm, vc=o_vc, vs=o_vs, vs_off=1,
                  qm_buf=b_q, qs_buf=b_q, qc_buf=b_q, km_buf=b_src, kc_buf=b_src, ks_buf=b_src, vm_buf=b_src, vc_buf=b_src, vs_buf=b_src)
        emit_proj(Bd, x, bx, m, fw2, di, do, final=False)
        S.barrier()
        Bd.release(mk0)
        da = dict(shared_in)
        da.update(onorm=W["onorm"], sinks=W["sinks"], wout=W["wout"],
                  qm=do["qm"], qs=do["qs"], qc=do["qc"], b_q=b_q, b_src=b_src, b_ks=b_gks, b_vs=b_gvs,
                  kmf=g_km, vmf=g_vm, kcf=g_kc, vcf=g_vc, ksf=g_ks, vsf=g_vs, ks_own=o_ks, vs_own=o_vs)
        w = attn_work(Bd)
        ac = attn_consts(Bd, da, w, bias_cache, first=(l == 0))
        for ui in (0, 5, 1, 6, 2, 7, 10, 11, 3, 8, 4, 9):
            S.coll("pool", (lambda a_, b_: lambda h: h.collective_compute("AllGather", ALU.bypass, replica_groups=RG, ins=[a_.ap()], outs=[b_.ap()]))(units_s[ui], units_g[ui]),
                   [b_src], [units_b[ui]], lambda h: h.memset(zt[0:1, 0:8], 0.0))
        emit_attention(Bd, x, bx, m, ac, w, da)
        S.barrier()
        Bd.release(mk0)
        nw = Bd.norm_work()
        fw = Bd.ffn_work(nw)
        Bd.ffn(x, bx, W["wgu2"], W["wdn2"], m, 2, fw)
        S.barrier()
        Bd.release(mk0)
    for kc in range(8):
        Bd.store(xoT[kc * 128:(kc + 1) * 128, :], x[:, kc, :], bx, eng="sp", final=True)
    S.wait_all("sp", Bd.finals)
    print("F sbuf peak", Bd.peak, {e: len(S.q[e]) for e in ENGS})
    S.emit()
    return Bd


_CACHE = {}


def _get(name, fn):
    if name not in _CACHE:
        _CACHE[name] = fn()
    return _CACHE[name]


def _core_tokens(r):
    idx = (np.arange(NBLK)[:, None] * 4 + r) * 128 + np.arange(128)[None, :]
    return idx.reshape(-1)


def _freq_rot():
    half = 16
    fr = (np.float32(10000.0) ** (-np.arange(half, dtype=np.float32) / np.float32(half))).astype(np.float32)
    freq = np.zeros((96, 1), np.float32)
    freq[64:80, 0] = fr
    freq[80:96, 0] = fr
    rotT = np.zeros((96, 96), np.float32)
    for i in range(16):
        rotT[80 + i, 64 + i] = -1.0
        rotT[64 + i, 80 + i] = 1.0
    return freq, rotT


def _ffn_layout(wgu_l, wdn_l):
    f = np.ascontiguousarray
    wgu = wgu_l.reshape(8, 128, 2, NF, 128)
    wdn = wdn_l.reshape(NF, 128, 8, 128)
    return f(wgu.transpose(3, 1, 0, 2, 4)), f(wdn.transpose(2, 1, 0, 3))


def layer_inputs(inp, l):
    f = np.ascontiguousarray
    d = {}
    d["wmod"] = f(inp["w_mod"][l].reshape(8, 128, 9216).transpose(1, 0, 2))
    d["bmod"] = f(inp["b_mod"][l].reshape(72, 128).T)
    d["ng"] = f(inp["norm_g"][l].reshape(3, 8, 128).transpose(2, 0, 1))
    d["wgu1"], d["wdn1"] = _ffn_layout(inp["w_ffn1_gu"][l], inp["w_ffn1_down"][l])
    d["wgu2"], d["wdn2"] = _ffn_layout(inp["w_ffn2_gu"][l], inp["w_ffn2_down"][l])
    win = inp["w_in"][l]
    d["win"] = f(win.reshape(8, 128, 1824).transpose(1, 0, 2))
    winv = np.concatenate([win[:, 928:1056], win[:, 1568:1824]], axis=1)
    d["winv"] = f(winv.reshape(8, 128, 384).transpose(1, 0, 2))
    d["qan"] = f(inp["q_a_norm"][l].reshape(2, 128).T)
    d["kvan"] = f(inp["kv_a_norm"][l].reshape(128, 1))
    d["wuq"] = f(inp["w_uq"][l].reshape(2, 128, 576).transpose(1, 0, 2))
    wukv = inp["w_ukv"][l].reshape(128, 6, 128)
    d["wukvk"] = f(wukv[:, :, 0:64])
    d["wukvv"] = f(wukv[:, :, 64:128].reshape(128, 384))
    hn = np.zeros((96, 4), np.float32)
    hn[:, 0] = inp["mla_q_norm"][l]
    hn[:, 1] = inp["mla_k_norm"][l]
    hn[0:64, 2] = inp["swa_q_norm"][l]
    hn[0:64, 3] = inp["swa_k_norm"][l]
    d["hnorms"] = hn
    d["wout"] = f(inp["w_out"][l].reshape(16, 64, 8, 128).transpose(2, 1, 0, 3))
    d["onorm"] = f(inp["out_norm"][l].reshape(16, 64).T)
    d["sinks"] = f(inp["sinks"][l].reshape(1, 6))
    return d


def _masks(r):
    j = np.arange(128)[:, None]
    q = np.arange(128)[None, :]
    mi = np.zeros((128, 4, 128), np.float32)
    ms = np.zeros((128, 4, 128), np.float32)
    for d in range(4):
        if d < r:
            mi[:, d, :] = 1.0
            ms[:, d, :] = 1.0
        elif d == r:
            mi[:, d, :] = (j <= q)
            ms[:, d, :] = (j < q)
    tc = np.zeros((128, 2, 128), np.float32)
    tc[:, 0, :] = (j >= q)
    tc[:, 1, :] = (j < q)
    return mi, ms, tc


CORES = [(b, r) for b in range(2) for r in range(4)]
_PAD_KEYS = ("wmod", "wgu1", "wdn1", "wgu2", "wdn2", "win", "winv", "wout")


def _pad(a, ci):
    return np.concatenate([a, np.full((1,) + a.shape[1:], float(ci), a.dtype)], axis=0)


def kernel(**inp):
    inp = {k: np.asarray(v) for k, v in inp.items()}
    prog = _get("F", build_F)
    x = inp["x"]
    pos = inp["positions"]
    freq, rotT = _freq_rot()
    lay = [layer_inputs(inp, l) for l in range(2)]
    in_maps = []
    for ci, (b, r) in enumerate(CORES):
        mi, ms, tc = _masks(r)
        wsel = np.zeros((128, 4), np.float32)
        wsel[:, (r + 3) % 4] = 1.0
        dct = dict(xT=np.ascontiguousarray(x[b][_core_tokens(r)].T),
                   cT=np.ascontiguousarray(inp["c"][b].reshape(8, 128).T),
                   pos=np.ascontiguousarray(pos[b][_core_tokens(r)].reshape(1, T).astype(np.int32)),
                   freq=freq, rotT=rotT, tricomp=tc, mincl=mi, mstrict=ms,
                   relb=np.ascontiguousarray(inp["rel_bias"].reshape(1, 192)),
                   posrow=np.ascontiguousarray(pos[b][(4 + r) * 128:(5 + r) * 128].reshape(1, 128).astype(np.int32)),
                   poscol=np.ascontiguousarray(np.stack([pos[b][(3 + r) * 128:(4 + r) * 128], pos[b][(4 + r) * 128:(5 + r) * 128]], axis=1).astype(np.int32)),
                   wsel=wsel)
        for l in range(2):
            for k, a in lay[l].items():
                dct[f"{k}_{l}"] = _pad(a, ci) if k in _PAD_KEYS else a
        in_maps.append(dct)
    res = run_bass_kernel_spmd(prog.nc, in_maps, core_ids=list(range(NCORES)))
    out = np.zeros((2, 8192, 1024), np.float32)
    for ci, (b, r) in enumerate(CORES):
        out[b][_core_tokens(r)] = res.results[ci]["xoT"].T
    return out
```
